# Optimizing a Trainium2 kernel written in Bass

```python
import jax, jax.numpy as jnp
from jax import lax
import numpy as np

D_MODEL = 1024
BATCH = 16
SEQ = 256
DEPTH = 4
DEC_BATCH = 2
DEC_SEQ = 1024
PAST_LEN = 512

GRID_W = 64
N_DIR = 2
N_BRANCH = 3
N_ADA = 9
D_FF = 2816
CONV_K = 3
EPS = 1e-6

S5_WIDTH = 512
S5_GROUP = 16
S5_GROUPS = S5_WIDTH // S5_GROUP
S5_STATE = 64

SSD_WIDTH = 512
SSD_HEADDIM = 64
SSD_HEADS = SSD_WIDTH // SSD_HEADDIM
SSD_GROUPS = 2
SSD_STATE = 64
SSD_CHUNK = 128
SSD_CONV_DIM = SSD_WIDTH + 2 * SSD_GROUPS * SSD_STATE

DN_HEADS = 4
DN_DK = 128
DN_DV = 128
DN_QK = DN_HEADS * DN_DK
DN_V = DN_HEADS * DN_DV
DN_CHUNK = 64
DN_CONV_DIM = 2 * DN_QK + DN_V

IN_SEGMENTS = (S5_WIDTH, SSD_WIDTH, SSD_CONV_DIM, N_DIR * SSD_HEADS, DN_CONV_DIM, N_DIR * DN_HEADS, N_DIR * DN_HEADS, DN_V, N_BRANCH * D_MODEL)
IN_WIDTH = sum(IN_SEGMENTS)
IN_SPLITS = tuple(int(s) for s in np.cumsum(IN_SEGMENTS)[:-1])

kernel_name = 'hybrid_s5_ssd_deltanet_diffusion_step'


def _rmsnorm(x, g):
    xf = x.astype(jnp.float32)
    y = xf * lax.rsqrt(jnp.mean(xf * xf, axis=-1, keepdims=True) + EPS)
    return (y * g.astype(jnp.float32)).astype(x.dtype)


def _l2norm(x):
    return x * lax.rsqrt(jnp.sum(x * x, axis=-1, keepdims=True) + EPS)


def _rev(t):
    return jnp.flip(t, axis=1)


def _short_conv(x, w, grid):
    bsz, seq, ch = x.shape
    width = GRID_W if grid else seq
    rows = seq // width
    pad = CONV_K // 2
    xp = jnp.pad(x.reshape(bsz, rows, width, ch), ((0, 0), (0, 0), (pad, pad), (0, 0)))
    y = xp[:, :, 0:width] * w[0]
    for k in range(1, CONV_K):
        y = y + xp[:, :, k:k + width] * w[k]
    return y.reshape(bsz, seq, ch)


def _complex_affine_combine(e1, e2):
    a1r, a1i, b1r, b1i = e1
    a2r, a2i, b2r, b2i = e2
    return (a2r * a1r - a2i * a1i,
            a2r * a1i + a2i * a1r,
            a2r * b1r - a2i * b1i + b2r,
            a2r * b1i + a2i * b1r + b2i)


def _s5_scan(u, lam_re, lam_im, log_dt, b_re, b_im, c_re, c_im, h0_re, h0_im):
    dt = jnp.exp(log_dt)[:, None]
    mag = jnp.exp(lam_re * dt)
    lb_re = mag * jnp.cos(lam_im * dt)
    lb_im = mag * jnp.sin(lam_im * dt)
    den = lam_re * lam_re + lam_im * lam_im
    cr = ((lb_re - 1.0) * lam_re + lb_im * lam_im) / den
    ci = (lb_im * lam_re - (lb_re - 1.0) * lam_im) / den
    bb_re = cr[..., None] * b_re - ci[..., None] * b_im
    bb_im = cr[..., None] * b_im + ci[..., None] * b_re
    bu_re = jnp.einsum('blgi,gpi->blgp', u, bb_re)
    bu_im = jnp.einsum('blgi,gpi->blgp', u, bb_im)
    bu_re = bu_re.at[:, 0].add(lb_re * h0_re - lb_im * h0_im)
    bu_im = bu_im.at[:, 0].add(lb_re * h0_im + lb_im * h0_re)
    a_re = jnp.broadcast_to(lb_re, bu_re.shape)
    a_im = jnp.broadcast_to(lb_im, bu_im.shape)
    _, _, h_re, h_im = lax.associative_scan(_complex_affine_combine, (a_re, a_im, bu_re, bu_im), axis=1)
    y = jnp.einsum('blgp,gip->blgi', h_re, c_re) - jnp.einsum('blgp,gip->blgi', h_im, c_im)
    return y, h_re[:, -1], h_im[:, -1]


def _ssd_scan(x, dt, a, bm, cm, h0):
    bsz, seq, nh, hp = x.shape
    ns = bm.shape[-1]
    nc = seq // SSD_CHUNK
    xc = (x * dt[..., None]).reshape(bsz, nc, SSD_CHUNK, nh, hp)
    bc = bm.reshape(bsz, nc, SSD_CHUNK, nh, ns)
    cc = cm.reshape(bsz, nc, SSD_CHUNK, nh, ns)
    acum = jnp.cumsum((dt * a).reshape(bsz, nc, SSD_CHUNK, nh), axis=2)
    idx = jnp.arange(SSD_CHUNK)
    lower = (idx[:, None] >= idx[None, :])[None, None, :, :, None]
    seg = jnp.exp(jnp.where(lower, acum[:, :, :, None, :] - acum[:, :, None, :, :], -jnp.inf))
    scores = jnp.einsum('bcihn,bcjhn->bcijh', cc, bc) * seg
    y_diag = jnp.einsum('bcijh,bcjhp->bcihp', scores, xc)
    to_end = jnp.exp(acum[:, :, -1:, :] - acum)
    chunk_states = jnp.einsum('bcjhn,bcjh,bcjhp->bchpn', bc, to_end, xc)
    chunk_decay = jnp.exp(acum[:, :, -1, :])

    def step(h, inp):
        s, d = inp
        return h * d[:, :, None, None] + s, h

    h_final, h_enter = lax.scan(step, h0, (jnp.moveaxis(chunk_states, 1, 0), jnp.moveaxis(chunk_decay, 1, 0)))
    h_enter = jnp.moveaxis(h_enter, 0, 1)
    y_off = jnp.einsum('bcihn,bchpn,bcih->bcihp', cc, h_enter, jnp.exp(acum))
    return (y_diag + y_off).reshape(bsz, seq, nh, hp), h_final


def _gated_delta(q, k, v, beta, g, h0):
    bsz, seq, nh, dk = q.shape
    dv = v.shape[-1]
    nc = seq // DN_CHUNK

    def chunks(t):
        t = jnp.moveaxis(t, 2, 1)
        return t.reshape(t.shape[0], t.shape[1], nc, DN_CHUNK, *t.shape[3:])

    q, k, v, beta, g = chunks(q), chunks(k), chunks(v), chunks(beta), chunks(g)
    gc = jnp.cumsum(g, axis=-1)
    idx = jnp.arange(DN_CHUNK)
    incl = idx[:, None] >= idx[None, :]
    strict = idx[:, None] > idx[None, :]
    decay = jnp.exp(jnp.where(incl, gc[..., :, None] - gc[..., None, :], -jnp.inf))
    kb = k * beta[..., None]
    m = jnp.where(strict, jnp.einsum('bhnid,bhnjd->bhnij', kb, k) * decay, 0.0)
    a_mat = jnp.eye(DN_CHUNK, dtype=m.dtype) + m
    u = lax.linalg.triangular_solve(a_mat, v * beta[..., None], left_side=True, lower=True)
    w = lax.linalg.triangular_solve(a_mat, kb * jnp.exp(gc)[..., None], left_side=True, lower=True)
    attn = jnp.einsum('bhnid,bhnjd->bhnij', q, k) * decay

    def step(s, inp):
        q_c, k_c, u_c, w_c, attn_c, gc_c = inp
        v_new = u_c - jnp.einsum('bhcd,bhde->bhce', w_c, s)
        o = (jnp.einsum('bhcd,bhde->bhce', q_c * jnp.exp(gc_c)[..., None], s)
             + jnp.einsum('bhij,bhje->bhie', attn_c, v_new))
        g_last = gc_c[..., -1]
        s = (s * jnp.exp(g_last)[..., None, None]
             + jnp.einsum('bhcd,bhce->bhde', k_c * jnp.exp(g_last[..., None] - gc_c)[..., None], v_new))
        return s, o

    xs = tuple(jnp.moveaxis(t, 2, 0) for t in (q, k, u, w, attn, gc))
    s_final, o = lax.scan(step, h0, xs)
    o = jnp.moveaxis(o, 0, 2).reshape(bsz, nh, seq, dv)
    return jnp.moveaxis(o, 1, 2), s_final


def _mixer_block(h, p, init, grid):
    f32 = jnp.float32
    bsz, seq, _ = h.shape
    proj = (h @ p['w_in']).astype(f32)
    u_s5, z, xbc, dt_raw, qkv, beta_raw, a_raw, dn_gate, gate_raw = jnp.split(proj, IN_SPLITS, axis=-1)

    u = u_s5.reshape(bsz, seq, S5_GROUPS, S5_GROUP)
    s5p = [p[n].astype(f32) for n in ('s5_lam_re', 's5_lam_im', 's5_log_dt', 's5_b_re', 's5_b_im', 's5_c_re', 's5_c_im')]
    h0r = init['s5_re'].astype(f32)
    h0i = init['s5_im'].astype(f32)
    y_f, sfr, sfi = _s5_scan(u, *[t[0] for t in s5p], h0r[:, 0], h0i[:, 0])
    y_b, sbr, sbi = _s5_scan(_rev(u), *[t[1] for t in s5p], h0r[:, 1], h0i[:, 1])
    y_s5 = (y_f + _rev(y_b) + p['s5_d'].astype(f32).reshape(S5_GROUPS, S5_GROUP) * u).reshape(bsz, seq, S5_WIDTH)
    g_s5 = jax.nn.gelu(y_s5)
    br_a = (g_s5 @ p['s5_glu'][0]) * jax.nn.sigmoid(g_s5 @ p['s5_glu'][1])

    xbc = jax.nn.silu(_short_conv(xbc, p['ssd_conv_w'].astype(f32), grid) + p['ssd_conv_b'].astype(f32))
    xs, bm, cm = jnp.split(xbc, (SSD_WIDTH, SSD_WIDTH + SSD_GROUPS * SSD_STATE), axis=-1)
    xs = xs.reshape(bsz, seq, SSD_HEADS, SSD_HEADDIM)
    rep = SSD_HEADS // SSD_GROUPS
    bm = jnp.repeat(bm.reshape(bsz, seq, SSD_GROUPS, SSD_STATE), rep, axis=2)
    cm = jnp.repeat(cm.reshape(bsz, seq, SSD_GROUPS, SSD_STATE), rep, axis=2)
    dt = jax.nn.softplus(dt_raw.reshape(bsz, seq, N_DIR, SSD_HEADS) + p['ssd_dt_bias'].astype(f32))
    a_ssd = -jnp.exp(p['ssd_a_log'].astype(f32))
    h0s = init['ssd'].astype(f32)
    ys_f, hs_f = _ssd_scan(xs, dt[:, :, 0], a_ssd[0], bm, cm, h0s[:, 0])
    ys_b, hs_b = _ssd_scan(_rev(xs), _rev(dt[:, :, 1]), a_ssd[1], _rev(bm), _rev(cm), h0s[:, 1])
    y_ssd = ys_f + _rev(ys_b) + p['ssd_d'].astype(f32)[:, None] * xs
    y_ssd = y_ssd.reshape(bsz, seq, SSD_WIDTH) * jax.nn.silu(z)
    br_b = _rmsnorm(y_ssd, p['ssd_norm_g']) @ p['ssd_w_out']

    qkv = jax.nn.silu(_short_conv(qkv, p['dn_conv_w'].astype(f32), grid))
    q, k, v = jnp.split(qkv, (DN_QK, 2 * DN_QK), axis=-1)
    q = _l2norm(q.reshape(bsz, seq, DN_HEADS, DN_DK)) * (DN_DK ** -0.5)
    k = _l2norm(k.reshape(bsz, seq, DN_HEADS, DN_DK))
    v = v.reshape(bsz, seq, DN_HEADS, DN_DV)
    beta = jax.nn.sigmoid(beta_raw.reshape(bsz, seq, N_DIR, DN_HEADS))
    g_dn = -jnp.exp(p['dn_a_log'].astype(f32)) * jax.nn.softplus(a_raw.reshape(bsz, seq, N_DIR, DN_HEADS) + p['dn_dt_bias'].astype(f32))
    h0d = init['dn'].astype(f32)
    o_f, sd_f = _gated_delta(q, k, v, beta[:, :, 0], g_dn[:, :, 0], h0d[:, 0])
    o_b, sd_b = _gated_delta(_rev(q), _rev(k), _rev(v), _rev(beta[:, :, 1]), _rev(g_dn[:, :, 1]), h0d[:, 1])
    o = _rmsnorm(o_f + _rev(o_b), p['dn_norm_g']) * jax.nn.silu(dn_gate.reshape(bsz, seq, DN_HEADS, DN_DV))
    br_c = o.reshape(bsz, seq, DN_V) @ p['dn_w_out']

    gates = jax.nn.sigmoid(gate_raw.reshape(bsz, seq, N_BRANCH, D_MODEL))
    merged = gates[:, :, 0] * br_a + gates[:, :, 1] * br_b + gates[:, :, 2] * br_c
    out = (merged @ p['w_out']).astype(h.dtype)
    states = {'s5_re': jnp.stack([sfr, sbr], axis=1), 's5_im': jnp.stack([sfi, sbi], axis=1),
              'ssd': jnp.stack([hs_f, hs_b], axis=1), 'dn': jnp.stack([sd_f, sd_b], axis=1)}
    return out, states


def _swiglu(h, wi, wo):
    a, b = jnp.split(h @ wi, 2, axis=-1)
    return (jax.nn.silu(a) * b) @ wo


def _layer(x, cond, p, init, grid):
    mod = (jax.nn.silu(cond) @ p['ada_w'] + p['ada_b']).reshape(cond.shape[0], 1, N_ADA, D_MODEL)
    sh1, sc1, g1, sh2, sc2, g2, sh3, sc3, g3 = [mod[:, :, i] for i in range(N_ADA)]
    h = _rmsnorm(x, p['norm_g'][0]) * (1.0 + sc1) + sh1
    x = x + 0.5 * g1 * _swiglu(h, p['ffn_wi'][0], p['ffn_wo'][0])
    h = _rmsnorm(x, p['norm_g'][1]) * (1.0 + sc2) + sh2
    mix, states = _mixer_block(h, p, init, grid)
    x = x + g2 * mix
    h = _rmsnorm(x, p['norm_g'][2]) * (1.0 + sc3) + sh3
    x = x + 0.5 * g3 * _swiglu(h, p['ffn_wi'][1], p['ffn_wo'][1])
    return x, states


def setup_inputs(seed: int = 0) -> dict:
    key = jax.random.key(seed)
    ks = iter(jax.random.split(key, 48))

    def nrm(shape, scale):
        return jax.random.normal(next(ks), shape, jnp.float32) * scale

    def unif(shape, lo, hi):
        return jax.random.uniform(next(ks), shape, jnp.float32, lo, hi)

    def dt_bias(shape):
        dt = jnp.exp(unif(shape, float(np.log(1e-3)), float(np.log(1e-1))))
        return dt + jnp.log(-jnp.expm1(-dt))

    n_idx = jnp.arange(S5_STATE, dtype=jnp.float32)
    s5_shape = (DEPTH, N_DIR, S5_GROUPS, S5_STATE)
    return {
        'x_prompt': nrm((BATCH, SEQ, D_MODEL), 1.0),
        'x_sample': nrm((DEC_BATCH, DEC_SEQ, D_MODEL), 1.0),
        'state_s5_re': nrm((DEC_BATCH, DEPTH, N_DIR, S5_GROUPS, S5_STATE), 0.5),
        'state_s5_im': nrm((DEC_BATCH, DEPTH, N_DIR, S5_GROUPS, S5_STATE), 0.5),
        'state_ssd': nrm((DEC_BATCH, DEPTH, N_DIR, SSD_HEADS, SSD_HEADDIM, SSD_STATE), 0.1),
        'state_dn': nrm((DEC_BATCH, DEPTH, N_DIR, DN_HEADS, DN_DK, DN_DV), 0.1),
        'c': nrm((DEC_BATCH, D_MODEL), 1.0),
        'c_ctx': nrm((D_MODEL,), 1.0),
        'ada_w': nrm((DEPTH, D_MODEL, N_ADA * D_MODEL), 0.5 * D_MODEL ** -0.5),
        'ada_b': nrm((DEPTH, N_ADA * D_MODEL), 0.02),
        'norm_g': 1.0 + nrm((DEPTH, 3, D_MODEL), 0.05),
        'ffn_wi': nrm((DEPTH, 2, D_MODEL, 2 * D_FF), D_MODEL ** -0.5),
        'ffn_wo': nrm((DEPTH, 2, D_FF, D_MODEL), D_FF ** -0.5),
        'w_in': nrm((DEPTH, D_MODEL, IN_WIDTH), D_MODEL ** -0.5),
        's5_lam_re': -0.5 + nrm(s5_shape, 0.01),
        's5_lam_im': jnp.pi * n_idx + nrm(s5_shape, 0.01),
        's5_log_dt': unif((DEPTH, N_DIR, S5_GROUPS), float(np.log(1e-3)), float(np.log(1e-1))),
        's5_b_re': nrm((DEPTH, N_DIR, S5_GROUPS, S5_STATE, S5_GROUP), (2 * S5_GROUP) ** -0.5),
        's5_b_im': nrm((DEPTH, N_DIR, S5_GROUPS, S5_STATE, S5_GROUP), (2 * S5_GROUP) ** -0.5),
        's5_c_re': nrm((DEPTH, N_DIR, S5_GROUPS, S5_GROUP, S5_STATE), (2 * S5_STATE) ** -0.5),
        's5_c_im': nrm((DEPTH, N_DIR, S5_GROUPS, S5_GROUP, S5_STATE), (2 * S5_STATE) ** -0.5),
        's5_d': nrm((DEPTH, S5_WIDTH), 1.0),
        's5_glu': nrm((DEPTH, 2, S5_WIDTH, D_MODEL), S5_WIDTH ** -0.5),
        'ssd_conv_w': nrm((DEPTH, CONV_K, SSD_CONV_DIM), CONV_K ** -0.5),
        'ssd_conv_b': nrm((DEPTH, SSD_CONV_DIM), 0.02),
        'ssd_dt_bias': dt_bias((DEPTH, N_DIR, SSD_HEADS)),
        'ssd_a_log': jnp.log(unif((DEPTH, N_DIR, SSD_HEADS), 1.0, 16.0)),
        'ssd_d': 1.0 + nrm((DEPTH, SSD_HEADS), 0.05),
        'ssd_norm_g': 1.0 + nrm((DEPTH, SSD_WIDTH), 0.05),
        'ssd_w_out': nrm((DEPTH, SSD_WIDTH, D_MODEL), SSD_WIDTH ** -0.5),
        'dn_conv_w': nrm((DEPTH, CONV_K, DN_CONV_DIM), CONV_K ** -0.5),
        'dn_dt_bias': dt_bias((DEPTH, N_DIR, DN_HEADS)),
        'dn_a_log': jnp.log(unif((DEPTH, N_DIR, DN_HEADS), 1.0, 16.0)),
        'dn_norm_g': 1.0 + nrm((DEPTH, DN_DV), 0.05),
        'dn_w_out': nrm((DEPTH, DN_V, D_MODEL), DN_V ** -0.5),
        'w_out': nrm((DEPTH, D_MODEL, D_MODEL), D_MODEL ** -0.5),
        'final_norm_g': 1.0 + nrm((D_MODEL,), 0.05),
    }


def reference(x_prompt, x_sample, state_s5_re, state_s5_im, state_ssd, state_dn, c, c_ctx,
              ada_w, ada_b, norm_g, ffn_wi, ffn_wo, w_in,
              s5_lam_re, s5_lam_im, s5_log_dt, s5_b_re, s5_b_im, s5_c_re, s5_c_im, s5_d, s5_glu,
              ssd_conv_w, ssd_conv_b, ssd_dt_bias, ssd_a_log, ssd_d, ssd_norm_g, ssd_w_out,
              dn_conv_w, dn_dt_bias, dn_a_log, dn_norm_g, dn_w_out, w_out, final_norm_g):
    bsz = x_prompt.shape[0]
    f32 = jnp.float32
    zero_init = {'s5_re': jnp.zeros((bsz, N_DIR, S5_GROUPS, S5_STATE), f32),
                 's5_im': jnp.zeros((bsz, N_DIR, S5_GROUPS, S5_STATE), f32),
                 'ssd': jnp.zeros((bsz, N_DIR, SSD_HEADS, SSD_HEADDIM, SSD_STATE), f32),
                 'dn': jnp.zeros((bsz, N_DIR, DN_HEADS, DN_DK, DN_DV), f32)}
    h_ctx = x_prompt
    h_lat = x_sample
    ctx_states = []
    for l in range(DEPTH):
        p = {'ada_w': ada_w[l], 'ada_b': ada_b[l], 'norm_g': norm_g[l], 'ffn_wi': ffn_wi[l], 'ffn_wo': ffn_wo[l],
             'w_in': w_in[l], 's5_lam_re': s5_lam_re[l], 's5_lam_im': s5_lam_im[l], 's5_log_dt': s5_log_dt[l],
             's5_b_re': s5_b_re[l], 's5_b_im': s5_b_im[l], 's5_c_re': s5_c_re[l], 's5_c_im': s5_c_im[l],
             's5_d': s5_d[l], 's5_glu': s5_glu[l], 'ssd_conv_w': ssd_conv_w[l], 'ssd_conv_b': ssd_conv_b[l],
             'ssd_dt_bias': ssd_dt_bias[l], 'ssd_a_log': ssd_a_log[l], 'ssd_d': ssd_d[l],
             'ssd_norm_g': ssd_norm_g[l], 'ssd_w_out': ssd_w_out[l], 'dn_conv_w': dn_conv_w[l],
             'dn_dt_bias': dn_dt_bias[l], 'dn_a_log': dn_a_log[l], 'dn_norm_g': dn_norm_g[l],
             'dn_w_out': dn_w_out[l], 'w_out': w_out[l]}
        h_ctx, st = _layer(h_ctx, c_ctx[None, :], p, zero_init, grid=False)
        ctx_states.append(st)
        lat_init = {'s5_re': state_s5_re[:, l], 's5_im': state_s5_im[:, l],
                    'ssd': state_ssd[:, l], 'dn': state_dn[:, l]}
        h_lat, _ = _layer(h_lat, c, p, lat_init, grid=True)
    y_prompt = _rmsnorm(h_ctx, final_norm_g)
    y_sample = _rmsnorm(h_lat, final_norm_g)
    new_state_s5_re = jnp.stack([s['s5_re'] for s in ctx_states], axis=1)
    new_state_s5_im = jnp.stack([s['s5_im'] for s in ctx_states], axis=1)
    new_state_ssd = jnp.stack([s['ssd'] for s in ctx_states], axis=1)
    new_state_dn = jnp.stack([s['dn'] for s in ctx_states], axis=1)
    return (y_prompt, y_sample, new_state_s5_re, new_state_s5_im, new_state_ssd, new_state_dn)
```

```python
import numpy as np
from contextlib import ExitStack
import concourse.bass as bass
import concourse.mybir as mybir
from concourse.bass_utils import run_bass_kernel_spmd

F32 = mybir.dt.float32
F32R = mybir.dt.float32r
ALU = mybir.AluOpType
AF = mybir.ActivationFunctionType

D = 1024
T = 1024
L = 4
NFC = 8
DFF = 2816
NJ = 22
HALF = 512
EPS = 1e-6
N_CORES = 8
MIXER = True
SAME_ENGINE_RAW_ONLY = True
EMBED_WAIT = True
DEBUG = False
import os
STAGE = int(os.environ.get('KSTAGE', '99'))
NLAYERS_RUN = L if False else 4
NF_SB = 15600
NR_SB = 37600


class Tile:
    def __init__(self, arena, space, off, shape):
        self.arena, self.space, self.off, self.shape = arena, space, off, tuple(shape)
        n = 1
        for s in shape:
            n *= s
        self.n = n

    def ap(self, dt=None):
        a = self.arena[:, self.off:self.off + self.n]
        if len(self.shape) == 2:
            a = a.rearrange("p (a b) -> p a b", b=self.shape[1])
        elif len(self.shape) == 3:
            a = a.rearrange("p (a b c) -> p a b c", b=self.shape[1], c=self.shape[2])
        if dt is not None:
            a = a.bitcast(dt)
        return a

    def r(self):
        return self.ap(F32R)

    def __getitem__(self, i):
        sub = self.shape[1:]
        n = self.n // self.shape[0]
        return Tile(self.arena, self.space, self.off + i * n, sub if sub else (n,))

    def cells(self):
        G = 32
        return [(self.space, c) for c in range(self.off // G, (self.off + self.n - 1) // G + 1)]


class Alloc:
    def __init__(self, arena, space, size):
        self.arena, self.space, self.size, self.top = arena, space, size, 0
        self.peak = 0

    def get(self, *shape):
        n = 1
        for s in shape:
            n *= s
        n2 = (n + 31) // 32 * 32
        t = Tile(self.arena, self.space, self.top, shape)
        self.top += n2
        self.peak = max(self.peak, self.top)
        assert self.top <= self.size, (self.space, self.top, self.size)
        return t

    def mark(self):
        return self.top

    def release(self, m):
        self.top = m


class Prog:
    ENG = ('pe', 'act', 'dve', 'pool', 'sp')

    def __init__(self, nc, same_engine_sync=True):
        self.nc = nc
        self.ops = {e: [] for e in self.ENG}
        self.cnt = {e: 0 for e in self.ENG}
        self.dcnt = {}
        self.last_w = {}
        self.last_r = {}
        self.seen = {e: {} for e in self.ENG}
        self.ses = same_engine_sync
        self.raw_only = SAME_ENGINE_RAW_ONLY

    @staticmethod
    def _cells(items):
        out = []
        for it in items:
            if isinstance(it, Tile):
                out.extend(it.cells())
            else:
                out.append(it)
        return out

    def _deps(self, eng, reads, writes, is_dma):
        deps = {}
        same_raw = 0

        def add(s, n):
            if deps.get(s, 0) < n:
                deps[s] = n
        for r in reads:
            ev = self.last_w.get(r)
            if ev:
                add(*ev)
                if ev[0] == eng and ev[1] > same_raw:
                    same_raw = ev[1]
        for w in writes:
            ev = self.last_w.get(w)
            if ev:
                add(*ev)
            for s, n in self.last_r.get(w, {}).items():
                add(s, n)
        waits = []
        for s, n in deps.items():
            if s == eng and not is_dma:
                if eng == 'pe' or not self.ses:
                    continue
                if self.raw_only:
                    n = same_raw
                    if n == 0:
                        continue
            if self.seen[eng].get(s, 0) >= n:
                continue
            self.seen[eng][s] = n
            waits.append((s, n))
        return waits

    def _commit(self, ev, reads, writes):
        s, n = ev
        for r in reads:
            d = self.last_r.setdefault(r, {})
            if d.get(s, 0) < n:
                d[s] = n
        for w in writes:
            self.last_w[w] = ev
            self.last_r[w] = {}

    def op(self, eng, fn, *args, R=(), W=(), **kw):
        if isinstance(fn, str):
            name = fn
            fn = (lambda e, name=name, args=args, kw=kw: getattr(e, name)(*args, **kw))
        reads, writes = self._cells(R), self._cells(W)
        waits = self._deps(eng, reads, writes, False)
        self.cnt[eng] += 1
        ev = (eng, self.cnt[eng])
        self.ops[eng].append((waits, fn, ev))
        self._commit(ev, reads, writes)

    def dma(self, eng, out, in_, R=(), W=(), dkey=None, **kw):
        reads, writes = self._cells(R), self._cells(W)
        dk = 'd:' + dkey
        n = self.dcnt.get(dk, 0)
        waits = self._deps(eng, reads, writes, True)
        if n > 0 and self.seen[eng].get(dk, 0) < n:
            self.seen[eng][dk] = n
            waits.append((dk, n))
        self.dcnt[dk] = n + 16
        ev = (dk, n + 16)
        self.ops[eng].append((waits, (lambda e: e.dma_start(out=out, in_=in_, **kw)), ev))
        self._commit(ev, reads, writes)

    def final_wait(self, eng='sp'):
        waits = [(dk, n) for dk, n in self.dcnt.items()]
        for e in self.ENG:
            if e != eng and self.cnt[e] > 0:
                waits.append((e, self.cnt[e]))
        self.ops[eng].append((waits, None, None))

    def emit(self):
        nc = self.nc
        names = list(self.ENG) + sorted(self.dcnt.keys())
        with ExitStack() as st:
            sems = {nm: st.enter_context(nc.semaphore('s_' + nm.replace(':', '_'))) for nm in names}
            block = st.enter_context(nc.Block())

            def run(engname, e):
                for waits, fn, ev in self.ops[engname]:
                    emb = None
                    if EMBED_WAIT and fn is not None and waits and not ev[0].startswith('d:'):
                        emb = waits[-1]
                        waits = waits[:-1]
                    for s, n in waits:
                        e.wait_ge(sems[s], n)
                    if fn is None:
                        continue
                    ins = fn(e)
                    if emb is not None:
                        ins._wait_ge(sems[emb[0]], emb[1])
                    ins.then_inc(sems[ev[0]], 16 if ev[0].startswith('d:') else 1)

            @block.tensor
            def _(e):
                run('pe', e)

            @block.scalar
            def _(e):
                run('act', e)

            @block.vector
            def _(e):
                run('dve', e)

            @block.gpsimd
            def _(e):
                run('pool', e)

            @block.sync
            def _(e):
                run('sp', e)


def build_program():
    nc = bass.Bass("TRN2", target_bir_lowering=False)
    nc.dge_precook = False

    def din(name, shape, dt=F32):
        return nc.dram_tensor(name, list(shape), dt, kind="ExternalInput").ap()

    def dout(name, shape):
        return nc.dram_tensor(name, list(shape), F32, kind="ExternalOutput").ap()

    x_in = din("x_t", [NFC, 128, T])
    cond_in = din("cond", [128, NFC, 2])
    adaw = din("adaw", [L, 36, 128, 8, 256], F32R)
    adab = din("adab", [L, 128, 72])
    normg = din("normg", [128, L, 3, NFC])
    fng = din("fng", [128, NFC])
    wi = din("wi", [L, 2, NJ, 128, 8, 256], F32R)
    wo = din("wo", [L, 2, NFC, 128, NJ * 128], F32R)
    wsm = din("wsm", [L, 128, 8, 32], F32R)
    smb = din("smb", [L, 128, 32])
    sma = din("sma", [L, 128, 24])
    ssdcw = din("ssdcw", [L, 128, 6, 4])
    ssdv = din("ssdv", [L, 128, 4, 2])
    consts_in = din("consts", [128, 7, 128], F32R)
    cmask_in = din("cmask", [128, 2, T], F32R)
    ssdh0 = din("ssdh0", [L, 2, 128, 512], F32R)
    dncw = din("dncw", [L, 128, 12, 3])
    dnv = din("dnv", [L, 128, 1])
    dnh0 = din("dnh0", [L, 2, 4, 128, 128], F32R)
    wfm = din("wfm", [L, 54, 128, 8, 128], F32R)
    wbr = din("wbr", [L, 4, NFC, 128, 4, 128], F32R)
    wout = din("wout", [L, NFC, 128, 8, 128], F32R)
    s5p = din("s5p", [L, 2, 128, 3, 16])
    s5b = din("s5b", [L, 2, 2, 128, 512], F32R)
    s5c = din("s5c", [L, 2, 2, 128, 16, 128])
    s5d = din("s5d", [L, 128, 4])
    s5h0 = din("s5h0", [L, 2, 128, 2, 16])
    cf_in = din("cf", [128, 1])
    y_out = dout("y_t", [NFC, 128, T])
    st_s5 = dout("st_s5", [2, L, 2, 128, 16, 4])
    st_ssd = dout("st_ssd", [L, 2, 4, 128, 512])
    st_dn = dout("st_dn", [L, 2, 4, 4, 128, 128])

    with ExitStack() as st:
        arena = st.enter_context(nc.sbuf_tensor("arena", [128, NF_SB], F32))
        arena_r = st.enter_context(nc.sbuf_tensor("arena_r", [128, NR_SB], F32))
        psum = st.enter_context(nc.psum_tensor("psum", [128, 8 * 512], F32))
        SB = Alloc(arena, 'sb', NF_SB)
        SR = Alloc(arena_r, 'sr', NR_SB)
        PSA = Alloc(psum, 'ps', 4096)
        banks = [PSA.get(512) for _ in range(8)]
        P = Prog(nc)
        bank_i = [0]

        nrot = [8]
        pin_i = [0]

        def pbank():
            b = banks[bank_i[0] % nrot[0]]
            bank_i[0] += 1
            return b

        def pinbank():
            b = banks[6 + pin_i[0] % 2]
            pin_i[0] += 1
            return b

        def tap(name, tile, n=None):
            if not DEBUG:
                return
            n = n or tile.n
            dd = nc.dram_tensor("dbg_" + name, [128, n], F32, kind="ExternalOutput").ap()
            P.dma('sp', dd, tile.arena[:, tile.off:tile.off + n], R=[tile], dkey='dbg')

        ones = SB.get(128)
        P.op('pool', 'memset', ones.ap(), 1.0, W=[ones])
        onesm = SR.get(128)
        P.op('dve', 'tensor_copy', onesm.r(), ones.ap(), R=[ones], W=[onesm])
        epst_b = SB.get(1)
        P.op('pool', 'memset', epst_b.ap(), EPS, W=[epst_b])
        zer = SB.get(512 if not MIXER else 1)
        P.op('pool', 'memset', zer.ap(), 0.0, W=[zer])

        x = SB.get(NFC, T)
        for fc in range(NFC):
            P.dma('sp', x[fc].ap(), x_in[fc], W=[x[fc]], dkey='xin%d' % (fc % 2))
        modT = SB.get(L, 72)
        ng = SB.get(L * 3 * NFC)
        P.dma('sp', ng.ap(), normg.rearrange("p l i f -> p (l i f)"), W=[ng], dkey='c0')
        fg = SB.get(NFC)
        P.dma('sp', fg.ap(), fng, W=[fg], dkey='c1')
        ab = SB.get(L, 72)
        for l in range(L):
            P.dma('sp', ab[l].ap(), adab[l], W=[ab[l]], dkey='c0')

        m0 = SB.mark()
        mr0 = SR.mark()
        cnd = SB.get(NFC, 2)
        P.dma('sp', cnd.ap(), cond_in, W=[cnd], dkey='c1')
        sc = SR.get(NFC, 2)
        P.op('act', 'activation', sc.r(), cnd.ap(), AF.Silu, R=[cnd], W=[sc])
        NB = 3
        abuf = [SR.get(8, 256) for _ in range(NB)]
        it = 0
        for l in range(L):
            for mp in range(36):
                wb = abuf[it % NB]
                it += 1
                P.dma('sp', wb.r(), adaw[l, mp], W=[wb], dkey='aw%d' % (it % NB))
                pb = pbank()
                for s in range(2):
                    for kc in range(8):
                        P.op('pe', 'matmul', pb.ap()[:, 2 * s:2 * s + 2], wb.r()[:, kc, s * 128:(s + 1) * 128],
                             sc.r()[:, kc, :], start=(kc == 0), stop=(kc == 7), R=[wb, sc], W=[pb])
                col = mp * 2
                P.op('dve', 'tensor_tensor', modT.ap()[:, l, col:col + 2], pb.ap()[:, 0:4:2],
                     ab.ap()[:, l, col:col + 2], ALU.add, R=[pb, ab[l]], W=[modT[l]])
        SB.release(m0)
        SR.release(mr0)

        Avec = SB.get(L, 3 * NFC)
        Gvec = SB.get(L, 3 * NFC)
        for l in range(L):
            for i in range(3):
                P.op('dve', 'scalar_tensor_tensor', Avec.ap()[:, l, i * 8:(i + 1) * 8],
                     modT.ap()[:, l, (3 * i + 1) * 8:(3 * i + 2) * 8], 1.0,
                     ng.ap()[:, (l * 3 + i) * 8:(l * 3 + i + 1) * 8], ALU.add, ALU.mult,
                     R=[modT[l], ng], W=[Avec[l]])
                P.op('dve', 'tensor_scalar', Gvec.ap()[:, l, i * 8:(i + 1) * 8],
                     modT.ap()[:, l, (3 * i + 2) * 8:(3 * i + 3) * 8], (1.0 if i == 1 else 0.5), None, ALU.mult,
                     R=[modT[l]], W=[Gvec[l]])
        tap('modT', modT)
        tap('Avec', Avec)
        tap('Gvec', Gvec)

        sqb = [SR.get(HALF) for _ in range(2)]
        tmpb = [SB.get(HALF) for _ in range(2)]
        rstd = [SB.get(HALF) for _ in range(2)]
        h = SR.get(NFC, T)
        sqi = [0]

        def hs(half):
            return slice(half * HALF, (half + 1) * HALF)

        def rms_rstd(src, half, out_rstd, nch=NFC):
            pb = pbank()
            sl = hs(half)
            for fc in range(nch):
                sq = sqb[sqi[0] % 2]
                sqi[0] += 1
                P.op('act', 'activation', sq.r(), src.ap()[:, fc, sl], AF.Square, R=[src[fc]], W=[sq])
                P.op('pe', 'matmul', pb.ap(), onesm.r(), sq.r(), start=(fc == 0), stop=(fc == nch - 1),
                     R=[sq, onesm], W=[pb])
            P.op('act', 'activation', out_rstd.ap(), pb.ap(), AF.Sqrt, bias=epst_b.ap(), scale=1.0 / (128 * nch),
                 R=[pb, epst_b], W=[out_rstd])
            P.op('dve', 'reciprocal', out_rstd.ap(), out_rstd.ap(), R=[out_rstd], W=[out_rstd])

        def norm_mod(l, i, hdst):
            for half in range(2):
                sl = hs(half)
                rs = rstd[half]
                rms_rstd(x, half, rs)
                for fc in range(NFC):
                    tm = tmpb[fc % 2]
                    P.op('dve', 'tensor_tensor', tm.ap(), x.ap()[:, fc, sl], rs.ap(), ALU.mult,
                         R=[x[fc], rs], W=[tm])
                    P.op('act', 'activation', hdst.r()[:, fc, sl], tm.ap(), AF.Identity,
                         scale=Avec.ap()[:, l, i * 8 + fc:i * 8 + fc + 1],
                         bias=modT.ap()[:, l, 3 * i * 8 + fc:3 * i * 8 + fc + 1],
                         R=[tm, Avec[l], modT[l]], W=[hdst[fc]])

        def ffn(l, f):
            i = 0 if f == 0 else 2
            norm_mod(l, i, h)
            if l == 0 and f == 0:
                tap('h0', h)
            m1 = SR.mark()
            act = SR.get(NJ, T)
            wbufs = [SR.get(8, 256) for _ in range(2)]
            for j in range(NJ):
                wb = wbufs[j % 2]
                P.dma('sp', wb.r(), wi[l, f, j], W=[wb], dkey='wi%d' % (j % 2))
                for half in range(2):
                    sl = hs(half)
                    pa, pbk = pbank(), pbank()
                    for s, pp in ((0, pa), (1, pbk)):
                        for kc in range(8):
                            P.op('pe', 'matmul', pp.ap(), wb.r()[:, kc, s * 128:(s + 1) * 128], h.r()[:, kc, sl],
                                 start=(kc == 0), stop=(kc == 7), R=[wb, h[kc]], W=[pp])
                    tm = tmpb[half]
                    P.op('act', 'activation', tm.ap(), pa.ap(), AF.Silu, R=[pa], W=[tm])
                    P.op('dve', 'tensor_tensor', act.r()[:, j, sl], tm.ap(), pbk.ap(), ALU.mult,
                         R=[tm, pbk], W=[act[j]])
            SR.release(m1)
            act = SR.get(NJ, T)
            wobufs = [SR.get(NJ, 128) for _ in range(2)]
            for mc in range(NFC):
                wb = wobufs[mc % 2]
                P.dma('sp', wb.r(), wo[l, f, mc].rearrange("p (j m) -> p j m", m=128), W=[wb], dkey='wo%d' % (mc % 2))
                for half in range(2):
                    sl = hs(half)
                    po = pbank()
                    for j in range(NJ):
                        P.op('pe', 'matmul', po.ap(), wb.r()[:, j, :], act.r()[:, j, sl],
                             start=(j == 0), stop=(j == NJ - 1), R=[wb, act[j]], W=[po])
                    P.op('dve', 'scalar_tensor_tensor', x.ap()[:, mc, sl], po.ap(),
                         Gvec.ap()[:, l, i * 8 + mc:i * 8 + mc + 1], x.ap()[:, mc, sl], ALU.mult, ALU.add,
                         R=[po, Gvec[l], x[mc]], W=[x[mc]])
            SR.release(m1)


        def lockstep(gens, lag=1):
            live = [True] * len(gens)
            step = 0
            while any(live):
                for gi_ in range(len(gens)):
                    if not live[gi_]:
                        continue
                    if gi_ * lag > step and live[0]:
                        continue
                    try:
                        next(gens[gi_])
                    except StopIteration:
                        live[gi_] = False
                step += 1

        cft = SB.get(1)
        P.dma('sp', cft.ap(), cf_in, W=[cft], dkey='c1')
        MAGIC = 12582912.0
        TWO_PI_HI = 6.28125
        TWO_PI_LO = 6.283185307179586 - 6.28125

        class VS:
            def __init__(self, n):
                self.n = n

            def new(self):
                return SB.get(self.n)

            def tt(self, a, b, op):
                o = self.new()
                P.op('dve', 'tensor_tensor', o.ap(), a.ap(), b.ap(), op, R=[a, b], W=[o])
                return o

            def mul(self, a, b): return self.tt(a, b, ALU.mult)
            def add(self, a, b): return self.tt(a, b, ALU.add)
            def sub(self, a, b): return self.tt(a, b, ALU.subtract)

            def ts(self, a, s1, s2, op0, op1=None):
                o = self.new()
                if op1 is None:
                    P.op('dve', 'tensor_scalar', o.ap(), a.ap(), s1, None, op0, R=[a], W=[o])
                else:
                    P.op('dve', 'tensor_scalar', o.ap(), a.ap(), s1, s2, op0, op1, R=[a], W=[o])
                return o

            def stt(self, a, sc_, b, op0, op1):
                o = self.new()
                P.op('dve', 'scalar_tensor_tensor', o.ap(), a.ap(), sc_, b.ap(), op0, op1, R=[a, b], W=[o])
                return o

            def act(self, a, func, scale=1.0):
                o = self.new()
                P.op('act', 'activation', o.ap(), a.ap(), func, scale=scale, R=[a], W=[o])
                return o

            def recip(self, a):
                o = self.new()
                P.op('dve', 'reciprocal', o.ap(), a.ap(), R=[a], W=[o])
                return o

            def cmul(self, ar, ai, br, bi):
                return (self.sub(self.mul(ar, br), self.mul(ai, bi)), self.add(self.mul(ar, bi), self.mul(ai, br)))

        def bc3(t2, n):
            return t2.ap().unsqueeze(2).to_broadcast([128, t2.n, n])

        def finalize(l, b, actb, first, mg):
            mk = SR.mark()
            nb = 2 if b == 0 else 1
            wbs = [[SR.get(4, 128) for _ in range(nb)] for _ in range(2)]
            wgs = [SR.get(8, 128) for _ in range(2)]
            for fc in range(NFC):
                wg = wgs[fc % 2]
                P.dma('sp', wg.r(), wfm[l, 26 + b * 8 + fc], W=[wg], dkey='fg%d' % (fc % 2))
                wb_ = wbs[fc % 2]
                for k in range(nb):
                    bi = (k if b == 0 else b + 1)
                    P.dma('sp', wb_[k].r(), wbr[l, bi, fc], W=[wb_[k]], dkey='fb%d' % (fc % 2))
                for half in range(2):
                    sl = hs(half)
                    pg = pbank()
                    for kc in range(8):
                        P.op('pe', 'matmul', pg.ap(), wg.r()[:, kc, :], h.r()[:, kc, sl], start=(kc == 0), stop=(kc == 7),
                             R=[wg, h[kc]], W=[pg])
                    gt = tmpb[0]
                    P.op('act', 'activation', gt.ap(), pg.ap(), AF.Sigmoid, R=[pg], W=[gt])
                    p0 = pbank()
                    for kc in range(4):
                        P.op('pe', 'matmul', p0.ap(), wb_[0].r()[:, kc, :], actb.r()[:, kc, sl], start=(kc == 0), stop=(kc == 3),
                             R=[wb_[0], actb[kc]], W=[p0])
                    if b == 0:
                        p1 = pbank()
                        for kc in range(4):
                            P.op('pe', 'matmul', p1.ap(), wb_[1].r()[:, kc, :], actb.r()[:, kc, sl], start=(kc == 0), stop=(kc == 3),
                                 R=[wb_[1], actb[kc]], W=[p1])
                        s1 = tmpb[1]
                        P.op('act', 'activation', s1.ap(), p1.ap(), AF.Sigmoid, R=[p1], W=[s1])
                        P.op('dve', 'tensor_tensor', s1.ap(), s1.ap(), p0.ap(), ALU.mult, R=[s1, p0], W=[s1])
                        brs, brt = s1.ap(), s1
                    else:
                        brs, brt = p0.ap(), p0
                    if first:
                        P.op('dve', 'tensor_tensor', mg.r()[:, fc, sl], gt.ap(), brs, ALU.mult, R=[gt, brt], W=[mg[fc]])
                    else:
                        P.op('dve', 'tensor_tensor', gt.ap(), gt.ap(), brs, ALU.mult, R=[gt, brt], W=[gt])
                        P.op('dve', 'tensor_tensor', mg.r()[:, fc, sl], mg.ap()[:, fc, sl], gt.ap(), ALU.add,
                             R=[gt, mg[fc]], W=[mg[fc]])
            SR.release(mk)

        def proj_fm(l, chunk, dst_fn, wfb):
            wb = wfb[chunk % 2]
            P.dma('sp', wb.r(), wfm[l, chunk], W=[wb], dkey='wf%d' % (chunk % 2))
            for half in range(2):
                sl = hs(half)
                pp = pbank()
                for kc in range(8):
                    P.op('pe', 'matmul', pp.ap(), wb.r()[:, kc, :], h.r()[:, kc, sl], start=(kc == 0), stop=(kc == 7),
                         R=[wb, h[kc]], W=[pp])
                dst_fn(half, pp)

        def s5_branch(l, u):
            mkF, mkR = SB.mark(), SR.mark()
            nrot[0] = 6
            wfb = [SR.get(8, 128) for _ in range(2)]
            for q in range(4):
                def ev(half, pp, q=q):
                    P.op('act', 'activation', u.r()[:, q, hs(half)], pp.ap(), AF.Identity, R=[pp], W=[u[q]])
                proj_fm(l, q, ev, wfb)
            SR.release(mkR)
            ys = SR.get(4, T)
            dsk = SB.get(4)
            P.dma('sp', dsk.ap(), s5d[l], W=[dsk], dkey='c0')
            Ec, Es = SR.get(16, 128), SR.get(16, 128)
            Bb = [SR.get(4, 128) for _ in range(2)]
            Cb = [SR.get(16, 128) for _ in range(2)]
            nCb0 = SR.get(16, 128)
            wk = [SR.get(512) for _ in range(6)]
            wkall = Tile(arena_r, 'sr', wk[0].off, (3072,))
            wk2 = [wk + [SR.get(512), SR.get(512)], [SR.get(512) for _ in range(8)]]
            SBt2 = [[SB.get(2), SB.get(2)], [SB.get(2), SB.get(2)]]
            hst = [SB.get(16, 4) for _ in range(2)]
            V = VS(16)
            for d in range(2):
                mkd = SB.mark()
                prm = SB.get(3, 16)
                P.dma('sp', prm.ap(), s5p[l, d], W=[prm], dkey='c1')
                h0 = SB.get(2, 16)
                P.dma('sp', h0.ap(), s5h0[l, d], W=[h0], dkey='c0')
                for ri in range(2):
                    P.dma('sp', Bb[ri].r().rearrange("p a b -> p (a b)"), s5b[l, d, ri], W=[Bb[ri]], dkey='sb%d' % ri)
                lamr, lami, ldt = prm[0], prm[1], prm[2]
                dt = V.act(ldt, AF.Exp)
                ar = V.mul(lamr, dt)
                th = V.mul(lami, dt)
                rr = V.act(ar, AF.Exp)

                def red(tx):
                    k = V.ts(tx, 1.0 / (2 * np.pi), MAGIC, ALU.mult, ALU.add)
                    k = V.ts(k, -MAGIC, None, ALU.add)
                    t_ = V.stt(k, -TWO_PI_HI, tx, ALU.mult, ALU.add)
                    return V.stt(k, -TWO_PI_LO, t_, ALU.mult, ALU.add)
                sn = V.act(red(th), AF.Sin)
                cs = V.act(red(V.ts(th, np.pi / 2, None, ALU.add)), AF.Sin)
                lbr, lbi = V.mul(rr, cs), V.mul(rr, sn)
                den = V.add(V.mul(lamr, lamr), V.mul(lami, lami))
                inv = V.recip(den)
                lm1 = V.ts(lbr, -1.0, None, ALU.add)
                cr = V.mul(V.add(V.mul(lm1, lamr), V.mul(lbi, lami)), inv)
                ci = V.mul(V.sub(V.mul(lbi, lamr), V.mul(lm1, lami)), inv)
                ncr, nci = V.ts(cr, -1.0, None, ALU.mult), V.ts(ci, -1.0, None, ALU.mult)
                cinv = V.recip(V.add(V.mul(cr, cr), V.mul(ci, ci)))
                qr, qi = V.cmul(h0[0], h0[1], cr, nci)
                qr, qi = V.mul(qr, cinv), V.mul(qi, cinv)
                i0r, i0i = V.cmul(qr, qi, cs, sn)
                P.op('dve', 'tensor_scalar', Ec.r()[:, :, 0:1], ones.ap()[:, 0:16].unsqueeze(2), 1.0, None, ALU.mult, R=[ones], W=[Ec])
                P.op('dve', 'tensor_scalar', Es.r()[:, :, 0:1], ones.ap()[:, 0:16].unsqueeze(2), 0.0, None, ALU.mult, R=[ones], W=[Es])
                wc, ws = cs, sn
                for k in range(7):
                    n = 1 << k
                    def tv(i):
                        return wkall.r()[:, 1024 * i:1024 * i + 16 * n].rearrange("p (a b) -> p a b", a=16), wkall
                    (a1, T1), (a2, T2), (a3, T3) = tv(0), tv(1), tv(2)
                    a4, T4 = tmp2k.ap()[:, 0:16 * n].rearrange("p (a b) -> p a b", a=16), tmp2k
                    P.op('dve', 'tensor_tensor', a1, Ec.ap()[:, :, 0:n], bc3(wc, n), ALU.mult, R=[Ec, wc], W=[T1])
                    P.op('dve', 'tensor_tensor', a2, Es.ap()[:, :, 0:n], bc3(ws, n), ALU.mult, R=[Es, ws], W=[T2])
                    P.op('dve', 'tensor_tensor', a3, Ec.ap()[:, :, 0:n], bc3(ws, n), ALU.mult, R=[Ec, ws], W=[T3])
                    P.op('dve', 'tensor_tensor', a4, Es.ap()[:, :, 0:n], bc3(wc, n), ALU.mult, R=[Es, wc], W=[T4])
                    P.op('dve', 'tensor_tensor', Ec.r()[:, :, n:2 * n], a1, a2, ALU.subtract, R=[T1, T2], W=[Ec])
                    P.op('dve', 'tensor_tensor', Es.r()[:, :, n:2 * n], a3, a4, ALU.add, R=[T3, T4], W=[Es])
                    wc, ws = V.sub(V.mul(wc, wc), V.mul(ws, ws)), V.ts(V.mul(wc, ws), 2.0, None, ALU.mult)
                wLr, wLi = wc, ws
                wLrc, wLic = V.new(), V.new()
                P.op('dve', 'tensor_scalar', wLrc.ap(), wLr.ap(), cft.ap()[:, 0:1], None, ALU.mult, R=[wLr, cft], W=[wLrc])
                P.op('dve', 'tensor_scalar', wLic.ap(), wLi.ap(), cft.ap()[:, 0:1], None, ALU.mult, R=[wLi, cft], W=[wLic])
                for qq in range(4):
                    cre = tmp2k.ap()[:, 0:512].rearrange("p (a b) -> p a b", a=4)
                    cim = tmp2k.ap()[:, 512:1024].rearrange("p (a b) -> p a b", a=4)
                    P.dma('sp', cre, s5c[l, d, 0, :, 4 * qq:4 * qq + 4, :], W=[tmp2k], dkey='sc0')
                    P.dma('sp', cim, s5c[l, d, 1, :, 4 * qq:4 * qq + 4, :], W=[tmp2k], dkey='sc1')
                    tq = [wk[i].r().rearrange("p (a b) -> p a b", a=4) for i in range(4)]
                    def b4(v):
                        return v.ap()[:, 4 * qq:4 * qq + 4].unsqueeze(2).to_broadcast([128, 4, 128])
                    P.op('dve', 'tensor_tensor', tq[0], cre, b4(cr), ALU.mult, R=[tmp2k, cr], W=[wk[0]])
                    P.op('dve', 'tensor_tensor', tq[1], cim, b4(ci), ALU.mult, R=[tmp2k, ci], W=[wk[1]])
                    P.op('dve', 'tensor_tensor', tq[2], cim, b4(ncr), ALU.mult, R=[tmp2k, ncr], W=[wk[2]])
                    P.op('dve', 'tensor_tensor', tq[3], cre, b4(ci), ALU.mult, R=[tmp2k, ci], W=[wk[3]])
                    P.op('dve', 'tensor_tensor', Cb[0].r()[:, 4 * qq:4 * qq + 4, :], tq[0], tq[1], ALU.subtract,
                         R=[wk[0], wk[1]], W=[Cb[0]])
                    P.op('dve', 'tensor_tensor', Cb[1].r()[:, 4 * qq:4 * qq + 4, :], tq[2], tq[3], ALU.subtract,
                         R=[wk[2], wk[3]], W=[Cb[1]])
                    P.op('dve', 'tensor_tensor', nCb0.r()[:, 4 * qq:4 * qq + 4, :], tq[1], tq[0], ALU.subtract,
                         R=[wk[0], wk[1]], W=[nCb0])
                car = SB.get(16, 2)
                P.op('dve', 'tensor_copy', car.ap()[:, :, 0], i0r.ap(), R=[i0r], W=[car])
                P.op('dve', 'tensor_copy', car.ap()[:, :, 1], i0i.ap(), R=[i0i], W=[car])
                wpN, wpC = SB.get(16, 2), SB.get(16, 2)
                for wp_, wi_src in ((wpN, wLi), (wpC, wLic)):
                    P.op('dve', 'tensor_scalar', wp_.ap()[:, :, 0], wi_src.ap(), -1.0, None, ALU.mult, R=[wi_src], W=[wp_])
                    P.op('dve', 'tensor_copy', wp_.ap()[:, :, 1], wi_src.ap(), R=[wi_src], W=[wp_])
                rev = (d == 1)
                halves = [1, 0] if rev else [0, 1]
                for half in halves:
                    sl = hs(half)
                    for q in range(4):
                        py = pinbank()
                        def unit(ch, mi):
                            mc = 4 * q + mi
                            W_ = wk2[ch]
                            vrT, viT, grT, giT, t1T, t2T, t3T, t4T = W_
                            vr, vi, gr, gi, t1, t2, t3, t4 = [w_.r() for w_ in W_]
                            pa, pb_ = pbank(), pbank()
                            r0 = 32 * mi
                            for ri, pp in ((0, pa), (1, pb_)):
                                P.op('pe', 'matmul', pp.ap(), Bb[ri].r()[r0:r0 + 32, q, :], u.r()[r0:r0 + 32, q, sl],
                                     start=True, stop=True, tile_position=(r0, 0), R=[Bb[ri], u[q]], W=[pp])
                            yield
                            tabc = Ec.ap()[:, mc, ::-1] if rev else Ec.ap()[:, mc, :]
                            tabs = Es.ap()[:, mc, ::-1] if rev else Es.ap()[:, mc, :]
                            tc4 = tabc.unsqueeze(1).to_broadcast([128, 4, 128])
                            tsn4 = tabs.unsqueeze(1).to_broadcast([128, 4, 128])
                            def v3(a):
                                return a.rearrange("p (a b) -> p a b", a=4)
                            P.op('act', 'activation', gr, pa.ap(), AF.Identity, R=[pa], W=[grT])
                            P.op('act', 'activation', gi, pb_.ap(), AF.Identity, R=[pb_], W=[giT])
                            yield
                            P.op('dve', 'tensor_tensor', v3(t1), v3(gr), tc4, ALU.mult, R=[grT, Ec], W=[t1T])
                            P.op('dve', 'tensor_tensor', v3(t2), v3(gi), tsn4, ALU.mult, R=[giT, Es], W=[t2T])
                            P.op('pool', 'tensor_tensor', v3(t3), v3(gi), tc4, ALU.mult, R=[giT, Ec], W=[t3T])
                            P.op('pool', 'tensor_tensor', v3(t4), v3(gr), tsn4, ALU.mult, R=[grT, Es], W=[t4T])
                            yield
                            P.op('dve', 'tensor_tensor', vr, t1, t2, ALU.add, R=[t1T, t2T], W=[vrT])
                            yield
                            P.op('dve', 'tensor_tensor', vi, t3, t4, ALU.subtract, R=[t3T, t4T], W=[viT])
                            yield
                            corder = [3, 2, 1, 0] if rev else [0, 1, 2, 3]
                            for ci_, c in enumerate(corder):
                                cs_ = slice(c * 128, (c + 1) * 128)
                                def dv(a):
                                    a = a[:, cs_]
                                    return a[:, ::-1] if rev else a
                                rb = rr.ap()[:, mc:mc + 1].to_broadcast([128, 128])
                                P.op('dve', 'tensor_tensor_scan', dv(gr), rb, dv(vr), car.ap()[:, mc, 0:1], ALU.mult, ALU.add,
                                     R=[rr, vrT, car], W=[grT])
                                P.op('dve', 'tensor_tensor_scan', dv(gi), rb, dv(vi), car.ap()[:, mc, 1:2], ALU.mult, ALU.add,
                                     R=[rr, viT, car], W=[giT])
                                yield
                                gc = c * 128 if rev else c * 128 + 127
                                gch = 4 * half + c
                                nxt = gch - 1 if rev else gch + 1
                                segb = (gch % 2 == 0) if rev else (nxt % 2 == 0)
                                wr_, wp_ = (wLrc, wpC) if segb else (wLr, wpN)
                                g0, g1 = grT.off + gc, giT.off + gc
                                assert g1 == g0 + 512
                                glast = arena_r[:, g0:g0 + 513:512]
                                glrev = arena_r[:, g1:g0 - 1:-512]
                                tP = SBt2[ch][0].ap()[:, 0:2]
                                P.op('dve', 'tensor_tensor', tP, glrev, wp_.ap()[:, mc, :], ALU.mult,
                                     R=[grT, giT, wp_], W=[SBt2[ch][0]])
                                yield
                                P.op('dve', 'scalar_tensor_tensor', car.ap()[:, mc, :], glast, wr_.ap()[:, mc:mc + 1], tP,
                                     ALU.mult, ALU.add, R=[grT, giT, wr_, SBt2[ch][0]], W=[car])
                                yield
                            P.op('pool', 'tensor_tensor', v3(t1), v3(gr), tc4, ALU.mult, R=[grT, Ec], W=[t1T])
                            P.op('pool', 'tensor_tensor', v3(t2), v3(gi), tsn4, ALU.mult, R=[giT, Es], W=[t2T])
                            yield
                            P.op('pool', 'tensor_tensor', v3(t3), v3(gi), tc4, ALU.mult, R=[giT, Ec], W=[t3T])
                            P.op('pool', 'tensor_tensor', v3(t4), v3(gr), tsn4, ALU.mult, R=[grT, Es], W=[t4T])
                            yield
                            c0 = 0 if rev else 255
                            P.op('pool', 'tensor_tensor', hst[0].ap()[:, mc, 2 * half:2 * half + 2], t1[:, c0::256], t2[:, c0::256], ALU.subtract,
                                 R=[t1T, t2T], W=[hst[0]])
                            P.op('pool', 'tensor_tensor', hst[1].ap()[:, mc, 2 * half:2 * half + 2], t3[:, c0::256], t4[:, c0::256], ALU.add,
                                 R=[t3T, t4T], W=[hst[1]])
                            yield
                            P.op('pe', 'matmul', py.ap(), Cb[0].r()[:, mc, :], t1, start=(mi == 0), stop=False, R=[Cb[0], t1T], W=[py])
                            P.op('pe', 'matmul', py.ap(), nCb0.r()[:, mc, :], t2, start=False, stop=False, R=[nCb0, t2T], W=[py])
                            P.op('pe', 'matmul', py.ap(), Cb[1].r()[:, mc, :], t3, start=False, stop=False, R=[Cb[1], t3T], W=[py])
                            P.op('pe', 'matmul', py.ap(), Cb[1].r()[:, mc, :], t4, start=False, stop=(mi == 3), R=[Cb[1], t4T], W=[py])
                            yield

                        for pair in ((0, 1), (2, 3)):
                            gens = [unit(0, pair[0]), unit(1, pair[1])]
                            live = [True, True]
                            step = 0
                            while any(live):
                                for gi_ in range(2):
                                    if not live[gi_]:
                                        continue
                                    if gi_ == 1 and step < 2 and live[0]:
                                        continue
                                    try:
                                        next(gens[gi_])
                                    except StopIteration:
                                        live[gi_] = False
                                step += 1
                        if d == 0:
                            P.op('dve', 'scalar_tensor_tensor', ys.r()[:, q, sl], u.ap()[:, q, sl], dsk.ap()[:, q:q + 1], py.ap(),
                                 ALU.mult, ALU.add, R=[u[q], dsk, py], W=[ys[q]])
                        else:
                            P.op('dve', 'tensor_tensor', ys.r()[:, q, sl], ys.ap()[:, q, sl], py.ap(), ALU.add,
                                 R=[ys[q], py], W=[ys[q]])
                V64 = VS(64)
                so = [V64.new(), V64.new()]
                crb = cr.ap().unsqueeze(2).to_broadcast([128, 16, 4])
                cib = ci.ap().unsqueeze(2).to_broadcast([128, 16, 4])
                def s3(t_):
                    return t_.ap().rearrange("p (a b) -> p a b", a=16)
                ta, tb_ = V64.new(), V64.new()
                P.op('dve', 'tensor_tensor', s3(ta), hst[0].ap(), crb, ALU.mult, R=[hst[0], cr], W=[ta])
                P.op('dve', 'tensor_tensor', s3(tb_), hst[1].ap(), cib, ALU.mult, R=[hst[1], ci], W=[tb_])
                P.op('dve', 'tensor_tensor', so[0].ap(), ta.ap(), tb_.ap(), ALU.subtract, R=[ta, tb_], W=[so[0]])
                P.op('dve', 'tensor_tensor', s3(ta), hst[0].ap(), cib, ALU.mult, R=[hst[0], ci], W=[ta])
                P.op('dve', 'tensor_tensor', s3(tb_), hst[1].ap(), crb, ALU.mult, R=[hst[1], cr], W=[tb_])
                P.op('dve', 'tensor_tensor', so[1].ap(), ta.ap(), tb_.ap(), ALU.add, R=[ta, tb_], W=[so[1]])
                for ri in range(2):
                    P.dma('sp', st_s5[ri, l, d].rearrange("p a b -> p (a b)"), so[ri].ap(), R=[so[ri]], dkey='so%d' % ri)
                SB.release(mkd)
            if l == 0:
                tap('ys', ys)
            for q in range(4):
                for half in range(2):
                    sl = hs(half)
                    t1, t2 = tmpb[0], tmpb[1]
                    P.op('act', 'activation', t1.ap(), ys.ap()[:, q, sl], AF.Square, R=[ys[q]], W=[t1])
                    P.op('dve', 'tensor_scalar', t1.ap(), t1.ap(), 0.044715, 1.0, ALU.mult, ALU.add, R=[t1], W=[t1])
                    P.op('dve', 'tensor_tensor', t1.ap(), t1.ap(), ys.ap()[:, q, sl], ALU.mult, R=[t1, ys[q]], W=[t1])
                    P.op('act', 'activation', t2.ap(), t1.ap(), AF.Sigmoid, scale=1.5957691216057308, R=[t1], W=[t2])
                    P.op('dve', 'tensor_tensor', u.r()[:, q, sl], t2.ap(), ys.ap()[:, q, sl], ALU.mult, R=[t2, ys[q]], W=[u[q]])
            SB.release(mkF)
            SR.release(mkR)
            nrot[0] = 8


        def load_consts():
            cst = SR.get(7, 128)
            P.dma('sp', cst.r(), consts_in, W=[cst], dkey='c0')
            return cst

        def small_params(l):
            wsb = SR.get(8, 32)
            P.dma('sp', wsb.r(), wsm[l], W=[wsb], dkey='c1')
            bia = SB.get(32)
            P.dma('sp', bia.ap(), smb[l], W=[bia], dkey='c0')
            alg = SB.get(24)
            P.dma('sp', alg.ap(), sma[l], W=[alg], dkey='c1')
            pb = pbank()
            for tc in range(8):
                for kc in range(8):
                    P.op('pe', 'matmul', pb.ap()[:, tc * 32:(tc + 1) * 32], h.r()[:, kc, tc * 128:(tc + 1) * 128], wsb.r()[:, kc, :],
                         start=(kc == 0), stop=(kc == 7), R=[h[kc], wsb], W=[pb])
            sp = SB.get(8, 32)
            beta = SB.get(8, 8)
            na = SB.get(24)
            da = SB.get(8, 16)
            gdn = SB.get(8, 8)
            mkt = SB.mark()
            raw = SB.get(8, 32)
            P.op('dve', 'tensor_tensor', raw.ap(), pb.ap()[:, 0:256].rearrange("p (a b) -> p a b", a=8),
                 bia.ap().unsqueeze(1).to_broadcast([128, 8, 32]), ALU.add, R=[pb, bia], W=[raw])
            ex = SB.get(8, 32)
            P.op('act', 'activation', ex.ap(), raw.ap(), AF.Exp, R=[raw], W=[ex])
            P.op('act', 'activation', sp.ap(), ex.ap(), AF.Ln, bias=1.0, R=[ex], W=[sp])
            P.op('act', 'activation', beta.ap(), raw.ap()[:, :, 16:24], AF.Sigmoid, R=[raw], W=[beta])
            P.op('act', 'activation', na.ap(), alg.ap(), AF.Exp, R=[alg], W=[na])
            P.op('dve', 'tensor_scalar', na.ap(), na.ap(), -1.0, None, ALU.mult, R=[na], W=[na])
            P.op('dve', 'tensor_tensor', da.ap(), sp.ap()[:, :, 0:16], na.ap()[:, 0:16].unsqueeze(1).to_broadcast([128, 8, 16]),
                 ALU.mult, R=[sp, na], W=[da])
            P.op('dve', 'tensor_tensor', gdn.ap(), sp.ap()[:, :, 24:32], na.ap()[:, 16:24].unsqueeze(1).to_broadcast([128, 8, 8]),
                 ALU.mult, R=[sp, na], W=[gdn])
            SB.release(mkt)
            return dict(dt=sp, da=da, beta=beta, gdn=gdn)

        def conv_chunk(l, wchunk, cw, ci, dst, msk, tmps, wfb, bias=True):
            raw, xl, xr, acc = tmps
            def ev(half, pp):
                P.op('act', 'activation', raw.r()[:, hs(half)], pp.ap(), AF.Identity, R=[pp], W=[raw])
            proj_fm(l, wchunk, ev, wfb)
            P.op('pool', 'tensor_tensor', xl.r()[:, 0:T - 1], raw.ap()[:, 0:T - 1], msk.ap()[:, 0, 1:T], ALU.mult, R=[raw, msk], W=[xl])
            P.op('pool', 'tensor_tensor', xr.r()[:, 1:T], raw.ap()[:, 1:T], msk.ap()[:, 1, 1:T], ALU.mult, R=[raw, msk], W=[xr])
            if bias:
                P.op('act', 'activation', acc.r(), raw.ap(), AF.Identity, scale=cw.ap()[:, ci, 1:2], bias=cw.ap()[:, ci, 3:4],
                     R=[raw, cw], W=[acc])
            else:
                P.op('act', 'activation', acc.r(), raw.ap(), AF.Identity, scale=cw.ap()[:, ci, 1:2], R=[raw, cw], W=[acc])
            P.op('dve', 'scalar_tensor_tensor', acc.r()[:, 1:T], xl.ap()[:, 0:T - 1], cw.ap()[:, ci, 0:1], acc.ap()[:, 1:T],
                 ALU.mult, ALU.add, R=[xl, cw, acc], W=[acc])
            P.op('dve', 'scalar_tensor_tensor', acc.r()[:, 0:T - 1], xr.ap()[:, 1:T], cw.ap()[:, ci, 2:3], acc.ap()[:, 0:T - 1],
                 ALU.mult, ALU.add, R=[xr, cw, acc], W=[acc])
            P.op('act', 'activation', dst.r(), acc.ap(), AF.Silu, R=[acc], W=[dst])

        def decay_T(cst, d, strict, da_col, acs_col, out_tile, xt, tt_):
            TRI = cst[d]
            MN = cst[(5 if strict else 3) + d]
            P.op('dve', 'tensor_scalar', xt.ap(), TRI.ap(), da_col, None, ALU.mult, R=[TRI], W=[xt])
            bc = pbank()
            P.op('pe', 'matmul', bc.ap()[:, 0:128], ones.ap(), xt.ap(), start=True, stop=True, R=[ones, xt], W=[bc])
            P.op('dve', 'scalar_tensor_tensor', tt_.ap(), bc.ap()[:, 0:128], acs_col, MN.ap(), ALU.subtract, ALU.add,
                 R=[bc, MN], W=[tt_])
            P.op('act', 'activation', out_tile.ap(), tt_.ap(), AF.Exp, R=[tt_], W=[out_tile])
            return bc

        def ssd_branch(l, actb, sm, cst):
            mkF, mkR = SB.mark(), SR.mark()
            nrot[0] = 3
            B3, B4, B5, B6, B7 = banks[3], banks[4], banks[5], banks[6], banks[7]
            SCB = [B6, B3]
            xbc = SR.get(6, T)
            xs, Bm, Cm = [xbc[i] for i in range(4)], xbc[4], xbc[5]
            mk2 = SR.mark()
            wfb = [SR.get(8, 128) for _ in range(2)]
            msk = SR.get(2, T)
            P.dma('sp', msk.r(), cmask_in, W=[msk], dkey='c0')
            cw = SB.get(6, 4)
            P.dma('sp', cw.ap(), ssdcw[l], W=[cw], dkey='c1')
            tmps = [SR.get(T) for _ in range(4)]
            for ci in range(6):
                conv_chunk(l, 4 + ci, cw, ci, xbc[ci], msk, tmps, wfb)
            SR.release(mk2)
            if l == 0:
                tap('xbc', xbc)
            ST = [SR.get(512) for _ in range(2)]
            xdt, xdtw, BmT = SR.get(512), SR.get(512), SR.get(128)
            PT = [SR.get(128) for _ in range(2)]
            Xt = [SB.get(128) for _ in range(2)]
            Tt = [SB.get(128) for _ in range(2)]
            Dc = [SB.get(128) for _ in range(2)]
            yt = SB.get(512)
            acs, tot, eA, toe, cdd = SB.get(8), SB.get(8), SB.get(8), SB.get(8), SB.get(8)
            ident = cst[2]
            for d in range(2):
                P.dma('sp', ST[d].r(), ssdh0[l, d], W=[ST[d]], dkey='sh%d' % d)
                order = range(8) if d == 0 else range(7, -1, -1)
                for c in order:
                    tok = slice(c * 128, (c + 1) * 128)
                    if STAGE < 1:
                        continue
                    for kc in range(4):
                        P.op('pe', 'transpose', B4.ap()[:, kc * 128:(kc + 1) * 128], xs[kc].ap()[:, tok], ident.ap(),
                             R=[xs[kc], ident], W=[B4])
                    dtb = sm['dt'].ap()[:, c, 8 * d:8 * d + 8].unsqueeze(2).to_broadcast([128, 8, 64])
                    P.op('dve', 'tensor_tensor', xdt.r().rearrange("p (a b) -> p a b", a=8),
                         B4.ap().rearrange("p (a b) -> p a b", a=8), dtb, ALU.mult, R=[B4, sm['dt']], W=[xdt])
                    if STAGE < 2:
                        continue
                    P.op('pe', 'transpose', B5.ap()[:, 0:128], Bm.ap()[:, tok], ident.ap(), R=[Bm, ident], W=[B5])
                    P.op('act', 'activation', BmT.r(), B5.ap()[:, 0:128], AF.Identity, R=[B5], W=[BmT])
                    if STAGE < 3:
                        continue
                    dac = sm['da'].ap()[:, c, 8 * d:8 * d + 8]
                    P.op('pe', 'matmul', B5.ap()[:, 128:136], cst[d].ap(), dac, start=True, stop=True, R=[cst[d], sm['da']], W=[B5])
                    P.op('pe', 'matmul', B5.ap()[:, 136:144], ones.ap(), dac, start=True, stop=True, R=[ones, sm['da']], W=[B5])
                    P.op('dve', 'tensor_copy', acs.ap(), B5.ap()[:, 128:136], R=[B5], W=[acs])
                    P.op('dve', 'tensor_copy', tot.ap(), B5.ap()[:, 136:144], R=[B5], W=[tot])
                    P.op('act', 'activation', eA.ap(), acs.ap(), AF.Exp, R=[acs], W=[eA])
                    P.op('act', 'activation', cdd.ap(), tot.ap(), AF.Exp, R=[tot], W=[cdd])
                    P.op('dve', 'tensor_tensor', toe.ap(), tot.ap(), acs.ap(), ALU.subtract, R=[tot, acs], W=[toe])
                    P.op('act', 'activation', toe.ap(), toe.ap(), AF.Exp, R=[toe], W=[toe])
                    if STAGE < 4:
                        continue
                    for gr in range(2):
                        P.op('pe', 'matmul', SCB[gr].ap()[:, 0:128], Bm.r()[64 * gr:64 * gr + 64, tok],
                             Cm.r()[64 * gr:64 * gr + 64, tok], start=True, stop=True, tile_position=(64 * gr, 0),
                             R=[Bm, Cm], W=[SCB[gr]])
                    if STAGE < 5:
                        continue
                    def head_unit(hh):
                        gr = hh // 4
                        dc_, xt_, tt_, pt = Dc[hh % 2], Xt[hh % 2], Tt[hh % 2], PT[hh % 2]
                        da_col = sm['da'].ap()[:, c, 8 * d + hh:8 * d + hh + 1]
                        P.op('dve', 'tensor_scalar', xt_.ap(), cst[d].ap(), da_col, None, ALU.mult, R=[cst[d], sm['da']], W=[xt_])
                        bc = pbank()
                        P.op('pe', 'matmul', bc.ap()[:, 0:128], ones.ap(), xt_.ap(), start=True, stop=True, R=[ones, xt_], W=[bc])
                        yield
                        P.op('dve', 'scalar_tensor_tensor', tt_.ap(), bc.ap()[:, 0:128], acs.ap()[:, hh:hh + 1], cst[3 + d].ap(),
                             ALU.subtract, ALU.add, R=[bc, acs, cst[3 + d]], W=[tt_])
                        yield
                        P.op('act', 'activation', dc_.ap(), tt_.ap(), AF.Exp, R=[tt_], W=[dc_])
                        yield
                        P.op('dve', 'tensor_tensor', pt.r(), SCB[gr].ap()[:, 0:128], dc_.ap(), ALU.mult,
                             R=[SCB[gr], dc_], W=[pt])
                        yield
                        P.op('pe', 'matmul', B7.ap()[:, 64 * hh:64 * hh + 64], pt.r(), xdt.r()[:, 64 * hh:64 * hh + 64],
                             start=True, stop=True, R=[pt, xdt], W=[B7])
                        yield
                    for hp in range(4):
                        lockstep([head_unit(2 * hp), head_unit(2 * hp + 1)], lag=1)
                    if STAGE < 6:
                        continue
                    yo = pbank()
                    P.op('pe', 'matmul', yo.ap(), Cm.r()[:, tok], ST[d].r(), start=True, stop=True, R=[Cm, ST[d]], W=[yo])
                    eab = eA.ap().unsqueeze(2).to_broadcast([128, 8, 64])
                    P.op('dve', 'tensor_tensor', yt.ap().rearrange("p (a b) -> p a b", a=8),
                         yo.ap().rearrange("p (a b) -> p a b", a=8), eab, ALU.mult, R=[yo, eA], W=[yt])
                    P.op('dve', 'tensor_tensor', yt.ap(), yt.ap(), B7.ap(), ALU.add, R=[yt, B7], W=[yt])
                    if STAGE < 7:
                        continue
                    for kc in range(4):
                        P.op('pe', 'transpose', B4.ap()[:, kc * 128:(kc + 1) * 128], yt.ap()[:, kc * 128:(kc + 1) * 128], ident.ap(),
                             R=[yt, ident], W=[B4])
                    b4v = B4.ap().rearrange("p (a b) -> p a b", a=4)
                    if d == 0:
                        P.op('act', 'activation', actb.r()[:, :, tok], b4v, AF.Identity, R=[B4], W=[actb[k_] for k_ in range(4)])
                    else:
                        P.op('dve', 'tensor_tensor', actb.r()[:, :, tok], actb.ap()[:, :, tok], b4v, ALU.add,
                             R=[B4] + [actb[k_] for k_ in range(4)], W=[actb[k_] for k_ in range(4)])
                    if STAGE < 8:
                        continue
                    teb = toe.ap().unsqueeze(2).to_broadcast([128, 8, 64])
                    P.op('dve', 'tensor_tensor', xdtw.r().rearrange("p (a b) -> p a b", a=8),
                         xdt.ap().rearrange("p (a b) -> p a b", a=8), teb, ALU.mult, R=[xdt, toe], W=[xdtw])
                    sn = pbank()
                    P.op('pe', 'matmul', sn.ap(), BmT.r(), xdtw.r(), start=True, stop=True, R=[BmT, xdtw], W=[sn])
                    for gr in range(2):
                        ps_ = slice(64 * gr, 64 * gr + 64)
                        fs_ = slice(256 * gr, 256 * gr + 256)
                        cdb = cdd.ap()[ps_, 4 * gr:4 * gr + 4].unsqueeze(2).to_broadcast([64, 4, 64])
                        P.op('dve', 'tensor_tensor', ST[d].r()[ps_, fs_].rearrange("p (a b) -> p a b", a=4),
                             ST[d].ap()[ps_, fs_].rearrange("p (a b) -> p a b", a=4), cdb, ALU.mult,
                             R=[ST[d], cdd], W=[ST[d]])
                        P.op('dve', 'tensor_tensor', ST[d].r()[ps_, fs_], ST[d].ap()[ps_, fs_], sn.ap()[ps_, fs_], ALU.add,
                             R=[ST[d], sn], W=[ST[d]])
                    seg_end = (c % 2 == 1) if d == 0 else (c % 2 == 0)
                    if seg_end:
                        P.dma('sp', st_ssd[l, d, c // 2], ST[d].ap(), R=[ST[d]], dkey='so%d' % d)
                        P.op('dve', 'tensor_scalar', ST[d].r(), ST[d].ap(), cft.ap()[:, 0:1], None, ALU.mult,
                             R=[ST[d], cft], W=[ST[d]])
            if l == 0:
                tap('yssd', actb)
            sv = SB.get(4, 2)
            P.dma('sp', sv.ap(), ssdv[l], W=[sv], dkey='c1')
            wfb = [SR.get(8, 128) for _ in range(2)]
            for kc in range(4):
                P.op('dve', 'scalar_tensor_tensor', actb.r()[:, kc, :], xs[kc].ap(), sv.ap()[:, kc, 1:2], actb.ap()[:, kc, :],
                     ALU.mult, ALU.add, R=[xs[kc], sv, actb[kc]], W=[actb[kc]])
                def evz(half, pp, kc=kc):
                    tm = tmpb[half]
                    P.op('act', 'activation', tm.ap(), pp.ap(), AF.Silu, R=[pp], W=[tm])
                    P.op('dve', 'tensor_tensor', actb.r()[:, kc, hs(half)], actb.ap()[:, kc, hs(half)], tm.ap(), ALU.mult,
                         R=[tm, actb[kc]], W=[actb[kc]])
                proj_fm(l, 50 + kc, evz, wfb)
            nrot[0] = 8
            for half in range(2):
                sl = hs(half)
                rs = rstd[half]
                rms_rstd(actb, half, rs, nch=4)
                for kc in range(4):
                    P.op('dve', 'scalar_tensor_tensor', actb.r()[:, kc, sl], actb.ap()[:, kc, sl], sv.ap()[:, kc, 0:1], rs.ap(),
                         ALU.mult, ALU.mult, R=[actb[kc], sv, rs], W=[actb[kc]])
            SB.release(mkF)
            SR.release(mkR)


        def dn_branch(l, actb, sm, cst):
            mkF, mkR = SB.mark(), SR.mark()
            ident = cst[2]
            gd = sm['gdn']
            pb = pbank()
            for c in range(8):
                for d in range(2):
                    P.op('pe', 'matmul', pb.ap()[:, 64 * d + c * 8:64 * d + c * 8 + 8], cst[d].ap(), gd.ap()[:, c, :],
                         start=True, stop=True, R=[cst[d], gd], W=[pb])
                P.op('pe', 'matmul', pb.ap()[:, 128 + c * 8:128 + c * 8 + 8], ones.ap(), gd.ap()[:, c, :],
                     start=True, stop=True, R=[ones, gd], W=[pb])
            gcs, gto = SB.get(8, 8), SB.get(8, 8)
            for d in range(2):
                P.op('dve', 'tensor_copy', gcs.ap()[:, :, 4 * d:4 * d + 4],
                     pb.ap()[:, 64 * d:64 * d + 64].rearrange("p (a b) -> p a b", a=8)[:, :, 4 * d:4 * d + 4], R=[pb], W=[gcs])
            P.op('dve', 'tensor_copy', gto.ap(), pb.ap()[:, 128:192].rearrange("p (a b) -> p a b", a=8), R=[pb], W=[gto])
            egc, ed, egt, bg, nbe = SB.get(8, 8), SB.get(8, 8), SB.get(8, 8), SB.get(8, 8), SB.get(8, 8)
            P.op('act', 'activation', egc.ap(), gcs.ap(), AF.Exp, R=[gcs], W=[egc])
            P.op('act', 'activation', egt.ap(), gto.ap(), AF.Exp, R=[gto], W=[egt])
            P.op('dve', 'tensor_tensor', ed.ap(), gto.ap(), gcs.ap(), ALU.subtract, R=[gto, gcs], W=[ed])
            P.op('act', 'activation', ed.ap(), ed.ap(), AF.Exp, R=[ed], W=[ed])
            P.op('dve', 'tensor_tensor', bg.ap(), sm['beta'].ap(), egc.ap(), ALU.mult, R=[sm['beta'], egc], W=[bg])
            P.op('dve', 'tensor_scalar', nbe.ap(), sm['beta'].ap(), -1.0, None, ALU.mult, R=[sm['beta']], W=[nbe])
            cw = SB.get(12, 3)
            P.dma('sp', cw.ap(), dncw[l], W=[cw], dkey='c1')
            gv = SB.get(1)
            P.dma('sp', gv.ap(), dnv[l], W=[gv], dkey='c0')
            Xt = [SB.get(128) for _ in range(2)]
            Tt = [SB.get(128) for _ in range(2)]
            DT = [SB.get(128) for _ in range(2)]
            DM = [SB.get(128) for _ in range(2)]
            Eg = [SB.get(128) for _ in range(2)]
            X, XT, TT, Xn = SB.get(2, 128), SB.get(2, 128), SB.get(2, 128), SB.get(2, 128)
            for hd in range(4):
                mkh = SR.mark()
                qkv = SR.get(3, T)
                q, k, v = qkv[0], qkv[1], qkv[2]
                mk2 = SR.mark()
                nrot[0] = 8
                wfb = [SR.get(8, 128) for _ in range(2)]
                msk = SR.get(2, T)
                P.dma('sp', msk.r(), cmask_in, W=[msk], dkey='c0')
                tmps = [SR.get(T) for _ in range(4)]
                for j3 in range(3):
                    conv_chunk(l, 10 + 4 * j3 + hd, cw, 4 * j3 + hd, qkv[j3], msk, tmps, wfb, bias=False)
                SR.release(mk2)
                for j3 in range(2):
                    for half in range(2):
                        sl = hs(half)
                        sq = sqb[half]
                        pbk = pbank()
                        P.op('act', 'activation', sq.r(), qkv[j3].ap()[:, sl], AF.Square, R=[qkv[j3]], W=[sq])
                        P.op('pe', 'matmul', pbk.ap(), onesm.r(), sq.r(), start=True, stop=True, R=[sq, onesm], W=[pbk])
                        rs = rstd[half]
                        P.op('act', 'activation', rs.ap(), pbk.ap(), AF.Sqrt, bias=epst_b.ap(), scale=1.0, R=[pbk, epst_b], W=[rs])
                        P.op('dve', 'reciprocal', rs.ap(), rs.ap(), R=[rs], W=[rs])
                        P.op('dve', 'scalar_tensor_tensor', qkv[j3].r()[:, sl], qkv[j3].ap()[:, sl],
                             (128.0 ** -0.5 if j3 == 0 else 1.0), rs.ap(), ALU.mult, ALU.mult, R=[qkv[j3], rs], W=[qkv[j3]])
                if l == 0 and hd == 0:
                    tap('dnqkv', qkv)
                nrot[0] = 4
                AT, vb, kbg, kd, nwT, vn, qg = [SR.get(2, 128) for _ in range(7)]
                S = SR.get(2, 128)
                for d in range(2):
                    P.dma('sp', S[d].r(), dnh0[l, d, hd], W=[S[d]], dkey='sh%d' % d)
                KB = [banks[4], banks[5]]
                TB = [banks[6], banks[7]]
                for t in range(8 if STAGE > 10 else 0):
                    cs = [t, 7 - t]
                    for d in range(2):
                        c = cs[d]
                        tok = slice(c * 128, (c + 1) * 128)
                        col = slice(4 * d + hd, 4 * d + hd + 1)
                        kb_ = KB[d]
                        tb_ = TB[d]
                        P.op('pe', 'matmul', kb_.ap()[:, 0:128], k.r()[:, tok], k.r()[:, tok], start=True, stop=True, R=[k], W=[kb_])
                        P.op('pe', 'matmul', kb_.ap()[:, 128:256], k.r()[:, tok], q.r()[:, tok], start=True, stop=True, R=[k, q], W=[kb_])
                        P.op('pe', 'transpose', tb_.ap()[:, 0:128], k.ap()[:, tok], ident.ap(), R=[k, ident], W=[tb_])
                        P.op('pe', 'transpose', tb_.ap()[:, 128:256], v.ap()[:, tok], ident.ap(), R=[v, ident], W=[tb_])
                        TRI = cst[d]
                        P.op('dve', 'tensor_scalar', Xt[d].ap(), TRI.ap(), gd.ap()[:, c, col], None, ALU.mult, R=[TRI, gd], W=[Xt[d]])
                        bc = pbank()
                        P.op('pe', 'matmul', bc.ap()[:, 0:128], ones.ap(), Xt[d].ap(), start=True, stop=True, R=[ones, Xt[d]], W=[bc])
                        gcc = gcs.ap()[:, c, col]
                        P.op('dve', 'scalar_tensor_tensor', Tt[d].ap(), bc.ap()[:, 0:128], gcc, cst[3 + d].ap(), ALU.subtract, ALU.add,
                             R=[bc, gcs, cst[3 + d]], W=[Tt[d]])
                        P.op('act', 'activation', DT[d].ap(), Tt[d].ap(), AF.Exp, R=[Tt[d]], W=[DT[d]])
                        P.op('dve', 'scalar_tensor_tensor', Tt[d].ap(), bc.ap()[:, 0:128], -1.0, cst[6 - d].ap(), ALU.mult, ALU.add,
                             R=[bc, cst[6 - d]], W=[Tt[d]])
                        P.op('act', 'activation', DM[d].ap(), Tt[d].ap(), AF.Exp, bias=gcc, R=[Tt[d], gcs], W=[DM[d]])
                        P.op('act', 'activation', Eg[d].ap(), bc.ap()[:, 0:128], AF.Exp, R=[bc], W=[Eg[d]])
                        P.op('dve', 'scalar_tensor_tensor', X.ap()[:, d, :], kb_.ap()[:, 0:128], nbe.ap()[:, c, col], DM[d].ap(),
                             ALU.mult, ALU.mult, R=[kb_, nbe, DM[d]], W=[X[d]])
                        P.op('dve', 'tensor_tensor', AT.r()[:, d, :], kb_.ap()[:, 128:256], DT[d].ap(), ALU.mult, R=[kb_, DT[d]], W=[AT[d]])
                        P.op('dve', 'tensor_tensor', qg.r()[:, d, :], q.ap()[:, tok], Eg[d].ap(), ALU.mult, R=[q, Eg[d]], W=[qg[d]])
                        P.op('act', 'activation', kbg.r()[:, d, :], tb_.ap()[:, 0:128], AF.Identity, scale=bg.ap()[:, c, col],
                             R=[tb_, bg], W=[kbg[d]])
                        P.op('act', 'activation', kd.r()[:, d, :], tb_.ap()[:, 0:128], AF.Identity, scale=ed.ap()[:, c, col],
                             R=[tb_, ed], W=[kd[d]])
                        P.op('act', 'activation', vb.r()[:, d, :], tb_.ap()[:, 128:256], AF.Identity, scale=sm['beta'].ap()[:, c, col],
                             R=[tb_, sm['beta']], W=[vb[d]])
                    if STAGE < 12:
                        continue
                    pt_ = TB[0]
                    for d in range(2):
                        P.op('pe', 'transpose', pt_.ap()[:, 256 + 128 * d:256 + 128 * d + 128], X.ap()[:, d, :], ident.ap(), R=[X[d], ident], W=[pt_])
                    if STAGE == 12:
                        continue
                    for d in range(2):
                        pc = pt_.ap()[:, 256 + 128 * d:256 + 128 * d + 128]
                        P.op('act', 'activation', XT.ap()[:, d, :], pc, AF.Identity, R=[pt_], W=[XT[d]])
                        P.op('dve', 'tensor_tensor', TT.ap()[:, d, :], XT.ap()[:, d, :], ident.ap(), ALU.add, R=[XT[d], ident], W=[TT[d]])
                    Xc, Xo = X, Xn
                    if STAGE < 13 or STAGE > 100:
                        continue
                    for kk in range(1, 7):
                        p1 = pbank()
                        for d in range(2):
                            P.op('pe', 'matmul', p1.ap()[:, 128 * d:128 * d + 128], XT.ap()[:, d, :], Xc.ap()[:, d, :], start=True, stop=True,
                                 R=[XT[d], Xc[d]], W=[p1])
                        P.op('act', 'activation', Xo.ap(), p1.ap()[:, 0:256].rearrange("p (a b) -> p a b", a=2), AF.Identity, R=[p1], W=[Xo])
                        if kk < 6:
                            p2 = TB[1]
                            for d in range(2):
                                P.op('pe', 'transpose', p2.ap()[:, 256 + 128 * d:256 + 128 * d + 128], Xo.ap()[:, d, :], ident.ap(),
                                     R=[Xo[d], ident], W=[p2])
                            P.op('dve', 'tensor_copy', XT.ap(), p2.ap()[:, 256:512].rearrange("p (a b) -> p a b", a=2), R=[p2], W=[XT])
                        Xc, Xo = Xo, Xc
                        p3 = pbank()
                        for d in range(2):
                            P.op('pe', 'matmul', p3.ap()[:, 128 * d:128 * d + 128], Xc.ap()[:, d, :], TT.ap()[:, d, :], start=True, stop=True,
                                 R=[Xc[d], TT[d]], W=[p3])
                        P.op('dve', 'tensor_tensor', TT.ap(), TT.ap(), p3.ap()[:, 0:256].rearrange("p (a b) -> p a b", a=2), ALU.add,
                             R=[TT, p3], W=[TT])
                    if STAGE < 14:
                        continue
                    for d in range(2):
                        c = cs[d]
                        tok = slice(c * 128, (c + 1) * 128)
                        col = slice(4 * d + hd, 4 * d + hd + 1)
                        pw = pbank()
                        P.op('pe', 'matmul', pw.ap()[:, 0:128], kbg.ap()[:, d, :], TT.ap()[:, d, :], start=True, stop=True, R=[kbg[d], TT[d]], W=[pw])
                        P.op('act', 'activation', nwT.r()[:, d, :], pw.ap()[:, 0:128], AF.Identity, scale=-1.0, R=[pw], W=[nwT[d]])
                        if STAGE < 15:
                            continue
                        pv = pbank()
                        P.op('pe', 'matmul', pv.ap()[:, 0:128], TT.ap()[:, d, :], vb.ap()[:, d, :], start=True, stop=False, R=[TT[d], vb[d]], W=[pv])
                        P.op('pe', 'matmul', pv.ap()[:, 0:128], nwT.ap()[:, d, :], S.ap()[:, d, :], start=False, stop=True, R=[nwT[d], S[d]], W=[pv])
                        P.op('act', 'activation', vn.r()[:, d, :], pv.ap()[:, 0:128], AF.Identity, R=[pv], W=[vn[d]])
                        if STAGE < 16:
                            continue
                        po = pbank()
                        P.op('pe', 'matmul', po.ap()[:, 0:128], S.r()[:, d, :], qg.r()[:, d, :], start=True, stop=False, R=[S[d], qg[d]], W=[po])
                        P.op('pe', 'matmul', po.ap()[:, 0:128], vn.r()[:, d, :], AT.r()[:, d, :], start=False, stop=True, R=[vn[d], AT[d]], W=[po])
                        first = (d == 0 and t < 4) or (d == 1 and t <= 3)
                        first = (t < 7 - t) if d == 0 else (t < 7 - t)
                        if first:
                            P.op('act', 'activation', actb.r()[:, hd, tok], po.ap()[:, 0:128], AF.Identity, R=[po], W=[actb[hd]])
                        else:
                            P.op('dve', 'tensor_tensor', actb.r()[:, hd, tok], actb.ap()[:, hd, tok], po.ap()[:, 0:128], ALU.add,
                                 R=[po, actb[hd]], W=[actb[hd]])
                        if STAGE < 17:
                            continue
                        ps_ = pbank()
                        P.op('pe', 'matmul', ps_.ap()[:, 0:128], kd.r()[:, d, :], vn.r()[:, d, :], start=True, stop=True, R=[kd[d], vn[d]], W=[ps_])
                        P.op('dve', 'scalar_tensor_tensor', S.r()[:, d, :], S.ap()[:, d, :], egt.ap()[:, c, col], ps_.ap()[:, 0:128],
                             ALU.mult, ALU.add, R=[S[d], egt, ps_], W=[S[d]])
                        seg_end = (c % 2 == 1) if d == 0 else (c % 2 == 0)
                        if seg_end and STAGE > 17:
                            P.dma('sp', st_dn[l, d, c // 2, hd], S.ap()[:, d, :], R=[S[d]], dkey='sd%d' % d)
                            P.op('dve', 'tensor_scalar', S.r()[:, d, :], S.ap()[:, d, :], cft.ap()[:, 0:1], None, ALU.mult,
                                 R=[S[d], cft], W=[S[d]])
                SR.release(mkh)
            nrot[0] = 8
            if l == 0:
                tap('dno', actb)
            wfb = [SR.get(8, 128) for _ in range(2)]
            for hd in range(4):
                for half in range(2):
                    sl = hs(half)
                    sq = sqb[half]
                    pbk = pbank()
                    P.op('act', 'activation', sq.r(), actb.ap()[:, hd, sl], AF.Square, R=[actb[hd]], W=[sq])
                    P.op('pe', 'matmul', pbk.ap(), onesm.r(), sq.r(), start=True, stop=True, R=[sq, onesm], W=[pbk])
                    rs = rstd[half]
                    P.op('act', 'activation', rs.ap(), pbk.ap(), AF.Sqrt, bias=epst_b.ap(), scale=1.0 / 128, R=[pbk, epst_b], W=[rs])
                    P.op('dve', 'reciprocal', rs.ap(), rs.ap(), R=[rs], W=[rs])
                    P.op('dve', 'scalar_tensor_tensor', actb.r()[:, hd, sl], actb.ap()[:, hd, sl], gv.ap()[:, 0:1], rs.ap(),
                         ALU.mult, ALU.mult, R=[actb[hd], gv, rs], W=[actb[hd]])
                def evg(half, pp, hd=hd):
                    tm = tmpb[half]
                    P.op('act', 'activation', tm.ap(), pp.ap(), AF.Silu, R=[pp], W=[tm])
                    P.op('dve', 'tensor_tensor', actb.r()[:, hd, hs(half)], actb.ap()[:, hd, hs(half)], tm.ap(), ALU.mult,
                         R=[tm, actb[hd]], W=[actb[hd]])
                proj_fm(l, 22 + hd, evg, wfb)
            SB.release(mkF)
            SR.release(mkR)

        def mixer(l):
            norm_mod(l, 1, h)
            mk0 = SR.mark()
            actb = SR.get(4, T)
            s5_branch(l, actb)
            mg = SR.get(NFC, T)
            finalize(l, 0, actb, True, mg)
            if l == 0:
                tap('mg', mg)
            mkS = SB.mark()
            cst = load_consts()
            sm = small_params(l)
            ssd_branch(l, actb, sm, cst)
            if l == 0:
                tap('actb_ssd', actb)
            finalize(l, 1, actb, False, mg)
            if l == 0:
                tap('mg2', mg)
            dn_branch(l, actb, sm, cst)
            finalize(l, 2, actb, False, mg)
            if l == 0:
                tap('mg3', mg)
            SB.release(mkS)
            wfb = [SR.get(8, 128) for _ in range(2)]
            for fc in range(NFC):
                wb = wfb[fc % 2]
                P.dma('sp', wb.r(), wout[l, fc], W=[wb], dkey='wf%d' % (fc % 2))
                for half in range(2):
                    sl = hs(half)
                    po = pbank()
                    for kc in range(8):
                        P.op('pe', 'matmul', po.ap(), wb.r()[:, kc, :], mg.r()[:, kc, sl], start=(kc == 0), stop=(kc == 7),
                             R=[wb, mg[kc]], W=[po])
                    P.op('dve', 'scalar_tensor_tensor', x.ap()[:, fc, sl], po.ap(), Gvec.ap()[:, l, 8 + fc:8 + fc + 1],
                         x.ap()[:, fc, sl], ALU.mult, ALU.add, R=[po, Gvec[l], x[fc]], W=[x[fc]])
            SR.release(mk0)

        SBt = [SB.get(1) for _ in range(2)]
        tmp2k = Tile(arena, 'sb', tmpb[0].off, (2048,))
        assert rstd[1].off == tmpb[0].off + 1536

        for l in range(NLAYERS_RUN):
            ffn(l, 0)
            if l == 0:
                tap('x1', x)
            if MIXER:
                mixer(l)
            ffn(l, 1)

        for half in range(2):
            sl = hs(half)
            rs = rstd[half]
            rms_rstd(x, half, rs)
            for fc in range(NFC):
                P.op('dve', 'scalar_tensor_tensor', x.ap()[:, fc, sl], x.ap()[:, fc, sl], fg.ap()[:, fc:fc + 1],
                     rs.ap(), ALU.mult, ALU.mult, R=[x[fc], rs, fg], W=[x[fc]])
        for fc in range(NFC):
            P.dma('sp', y_out[fc], x[fc].ap(), R=[x[fc]], dkey='yo%d' % (fc % 2))
        if True:
            for a in range(2 if not MIXER else 0):
                for l in range(L):
                    for d in range(2):
                        P.dma('sp', st_s5[a, l, d].rearrange("p a b -> p (a b)"), zer.ap()[:, 0:64], R=[zer], dkey='z0')
            for l in range(L):
                for d in range(2):
                    for s in range(4):
                        if not MIXER:
                            P.dma('sp', st_ssd[l, d, s], zer.ap(), R=[zer], dkey='z1')
                            pass
        P.final_wait('sp')
        P.emit()
        print("SBUF peak floats", SB.peak, SR.peak, "instr counts", P.cnt, flush=True)
    return nc


def _prep_shared(inp):
    f = np.float32
    sh = {}
    aw = inp['ada_w'].reshape(L, 8, 128, 36, 256)
    sh['adaw'] = np.ascontiguousarray(aw.transpose(0, 3, 2, 1, 4)).astype(f)
    sh['adab'] = np.ascontiguousarray(inp['ada_b'].reshape(L, 72, 128).transpose(0, 2, 1)).astype(f)
    sh['normg'] = np.ascontiguousarray(inp['norm_g'].reshape(L, 3, NFC, 128).transpose(3, 0, 1, 2)).astype(f)
    sh['fng'] = np.ascontiguousarray(inp['final_norm_g'].reshape(NFC, 128).T).astype(f)
    w = inp['ffn_wi'].reshape(L, 2, 8, 128, 2, NJ, 128)
    sh['wi'] = np.ascontiguousarray(w.transpose(0, 1, 5, 3, 2, 4, 6)).reshape(L, 2, NJ, 128, 8, 256).astype(f)
    w = inp['ffn_wo'].reshape(L, 2, NJ, 128, NFC, 128)
    sh['wo'] = np.ascontiguousarray(w.transpose(0, 1, 4, 3, 2, 5)).reshape(L, 2, NFC, 128, NJ * 128).astype(f)
    win = inp['w_in']
    cols = np.concatenate([np.arange(0, 512), np.arange(1024, 1792), np.arange(1808, 3344),
                           np.arange(3360, 3872), np.arange(3872, 6944), np.arange(512, 1024)])
    wsel = win[:, :, cols].reshape(L, 8, 128, 54, 128)
    scol = np.concatenate([np.arange(1792, 1808), np.arange(3344, 3360)])
    sh['wsm'] = np.ascontiguousarray(win[:, :, scol].reshape(L, 8, 128, 32).transpose(0, 2, 1, 3)).astype(f)
    bias = np.concatenate([inp['ssd_dt_bias'].reshape(L, 16), np.zeros((L, 8), f), inp['dn_dt_bias'].reshape(L, 8)], axis=1)
    sh['smb'] = np.ascontiguousarray(np.broadcast_to(bias[:, None, :], (L, 128, 32))).astype(f)
    alog = np.concatenate([inp['ssd_a_log'].reshape(L, 16), inp['dn_a_log'].reshape(L, 8)], axis=1)
    sh['sma'] = np.ascontiguousarray(np.broadcast_to(alog[:, None, :], (L, 128, 24))).astype(f)
    cw = np.concatenate([inp['ssd_conv_w'], inp['ssd_conv_b'][:, None, :]], axis=1)
    sh['ssdcw'] = np.ascontiguousarray(cw.reshape(L, 4, 6, 128).transpose(0, 3, 2, 1)).astype(f)
    dch = np.repeat(inp['ssd_d'], 64, axis=1)
    sv = np.stack([inp['ssd_norm_g'], dch], axis=-1)
    sh['ssdv'] = np.ascontiguousarray(sv.reshape(L, 4, 128, 2).transpose(0, 2, 1, 3)).astype(f)
    sh['dncw'] = np.ascontiguousarray(inp['dn_conv_w'].reshape(L, 3, 12, 128).transpose(0, 3, 2, 1)).astype(f)
    sh['dnv'] = np.ascontiguousarray(inp['dn_norm_g'].reshape(L, 128, 1)).astype(f)
    jj, ii = np.meshgrid(np.arange(128), np.arange(128), indexing='ij')
    NEG = -32768.0
    cst = np.stack([(jj <= ii), (jj >= ii), (jj == ii),
                    np.where(ii >= jj, 0.0, NEG), np.where(ii <= jj, 0.0, NEG),
                    np.where(ii > jj, 0.0, NEG), np.where(ii < jj, 0.0, NEG)], axis=1).astype(f)
    sh['consts'] = np.ascontiguousarray(cst)
    sh['wfm'] = np.ascontiguousarray(wsel.transpose(0, 3, 2, 1, 4)).astype(f)
    br = np.stack([inp['s5_glu'][:, 0], inp['s5_glu'][:, 1], inp['ssd_w_out'], inp['dn_w_out']], axis=1)
    br = br.reshape(L, 4, 4, 128, NFC, 128)
    sh['wbr'] = np.ascontiguousarray(br.transpose(0, 1, 4, 3, 2, 5)).astype(f)
    wo_ = inp['w_out'].reshape(L, 8, 128, NFC, 128)
    sh['wout'] = np.ascontiguousarray(wo_.transpose(0, 3, 2, 1, 4)).astype(f)
    ldt = np.broadcast_to(inp['s5_log_dt'][..., None], (L, 2, 32, 64))
    p3 = np.stack([inp['s5_lam_re'], inp['s5_lam_im'], ldt], axis=2).reshape(L, 2, 3, 16, 2, 64)
    sh['s5p'] = np.ascontiguousarray(p3.transpose(0, 1, 4, 5, 2, 3)).reshape(L, 2, 128, 3, 16).astype(f)
    sb_ = np.zeros((L, 2, 2, 128, 4, 128), f)
    scc = np.zeros((L, 2, 2, 128, 16, 128), f)
    for ri, (bb, cc) in enumerate(((inp['s5_b_re'], inp['s5_c_re']), (inp['s5_b_im'], inp['s5_c_im']))):
        for g in range(32):
            sb_[:, :, ri, 16 * (g % 8):16 * (g % 8) + 16, g // 8, 64 * (g % 2):64 * (g % 2) + 64] = \
                bb[:, :, g].transpose(0, 1, 3, 2)
            scc[:, :, ri, 64 * (g % 2):64 * (g % 2) + 64, g // 2, 16 * (g % 8):16 * (g % 8) + 16] = \
                cc[:, :, g].transpose(0, 1, 3, 2)
    sh['s5b'] = sb_.reshape(L, 2, 2, 128, 512)
    sh['s5c'] = scc
    sh['s5d'] = np.ascontiguousarray(inp['s5_d'].reshape(L, 4, 128).transpose(0, 2, 1)).astype(f)
    return sh


def _core_inputs(inp, core):
    f = np.float32
    d = {}
    if core < 4:
        xs = inp['x_prompt'][core * 4:(core + 1) * 4].reshape(T, D)
        cond = inp['c_ctx']
    elif core < 6:
        xs = inp['x_sample'][core - 4]
        cond = inp['c'][core - 4]
    else:
        return None
    W_ = 256 if core < 4 else 64
    tt = np.arange(T)
    m0 = np.ones(T, f); m0[1:] = (tt[1:] % W_ != 0)
    m1 = (tt % W_ != 0).astype(f)
    d['cmask'] = np.ascontiguousarray(np.broadcast_to(np.stack([m0, m1])[None], (128, 2, T))).astype(f)
    h0s = np.zeros((L, 2, 128, 512), f)
    if core >= 4:
        st = inp['state_ssd'][core - 4]
        for hh in range(8):
            gr = hh // 4
            h0s[:, :, 64 * gr:64 * gr + 64, 64 * hh:64 * hh + 64] = st[:, :, hh].transpose(0, 1, 3, 2)
    d['ssdh0'] = h0s
    if core >= 4:
        d['dnh0'] = np.ascontiguousarray(inp['state_dn'][core - 4]).astype(f)
    else:
        d['dnh0'] = np.zeros((L, 2, 4, 128, 128), f)
    if core < 4:
        d['s5h0'] = np.zeros((L, 2, 128, 2, 16), f)
        d['cf'] = np.zeros((128, 1), f)
    else:
        b = core - 4
        hh = np.stack([inp['state_s5_re'][b], inp['state_s5_im'][b]], axis=2)
        hh = hh.reshape(L, 2, 2, 16, 2, 64).transpose(0, 1, 4, 5, 2, 3)
        d['s5h0'] = np.ascontiguousarray(hh).reshape(L, 2, 128, 2, 16).astype(f)
        d['cf'] = np.ones((128, 1), f)
    d['x_t'] = np.ascontiguousarray(xs.T.reshape(NFC, 128, T)).astype(f)
    c2 = cond.reshape(NFC, 128).T
    d['cond'] = np.ascontiguousarray(np.stack([c2, c2], axis=-1)).astype(f)
    return d


_NC_CACHE = {}


def kernel(**inputs):
    inp = {k: np.asarray(v) for k, v in inputs.items()}
    if 'nc' not in _NC_CACHE:
        _NC_CACHE['nc'] = build_program()
    nc = _NC_CACHE['nc']
    shared = _prep_shared(inp)
    in_maps = []
    for core in range(N_CORES):
        d = _core_inputs(inp, core)
        if d is None:
            d = in_maps[0]
            in_maps.append(d)
            continue
        d.update(shared)
        in_maps.append(d)
    res = run_bass_kernel_spmd(nc, in_maps, core_ids=list(range(N_CORES)))
    R = res.results
    B, S = 16, 256
    y_prompt = np.zeros((B, S, D), np.float32)
    for c in range(4):
        y_prompt[c * 4:(c + 1) * 4] = R[c]['y_t'].reshape(D, T).T.reshape(4, S, D)
    y_sample = np.stack([R[4 + b]['y_t'].reshape(D, T).T for b in range(2)]).astype(np.float32)
    s5re = np.zeros((B, L, 2, 32, 64), np.float32)
    s5im = np.zeros((B, L, 2, 32, 64), np.float32)
    ssd = np.zeros((B, L, 2, 8, 64, 64), np.float32)
    dn = np.zeros((B, L, 2, 4, 128, 128), np.float32)
    for c in range(4):
        s5 = R[c]['st_s5'].reshape(2, L, 2, 2, 64, 16, 4)
        s5 = s5.transpose(0, 6, 1, 2, 5, 3, 4).reshape(2, 4, L, 2, 32, 64)
        s5re[c * 4:(c + 1) * 4] = s5[0]
        s5im[c * 4:(c + 1) * 4] = s5[1]
        sd = R[c]['st_ssd'].reshape(L, 2, 4, 2, 64, 8, 64)
        for hh in range(8):
            ssd[c * 4:(c + 1) * 4, :, :, hh] = sd[:, :, :, hh // 4, :, hh, :].transpose(2, 0, 1, 4, 3)
        dd = R[c]['st_dn']
        dn[c * 4:(c + 1) * 4] = dd.transpose(2, 0, 1, 3, 4, 5)
    return (y_prompt, y_sample, s5re, s5im, ssd, dn)
```

```python
import numpy as np
from contextlib import ExitStack
import concourse.bass as bass
import concourse.mybir as mybir
from concourse.bass_utils import run_bass_kernel_spmd

F32 = mybir.dt.float32
F32R = mybir.dt.float32r
ALU = mybir.AluOpType
AF = mybir.ActivationFunctionType

D = 1024
T = 1024
L = 4
NFC = 8
DFF = 2816
NJ = 22
HALF = 512
EPS = 1e-6
N_CORES = 8
MIXER = True
SAME_ENGINE_RAW_ONLY = True
EMBED_WAIT = True
DEBUG = False
import os
STAGE = int(os.environ.get('KSTAGE', '99'))
NLAYERS_RUN = L if False else 4
NF_SB = 15600
NR_SB = 37600


class Tile:
    def __init__(self, arena, space, off, shape):
        self.arena, self.space, self.off, self.shape = arena, space, off, tuple(shape)
        n = 1
        for s in shape:
            n *= s
        self.n = n

    def ap(self, dt=None):
        a = self.arena[:, self.off:self.off + self.n]
        if len(self.shape) == 2:
            a = a.rearrange("p (a b) -> p a b", b=self.shape[1])
        elif len(self.shape) == 3:
            a = a.rearrange("p (a b c) -> p a b c", b=self.shape[1], c=self.shape[2])
        if dt is not None:
            a = a.bitcast(dt)
        return a

    def r(self):
        return self.ap(F32R)

    def __getitem__(self, i):
        sub = self.shape[1:]
        n = self.n // self.shape[0]
        return Tile(self.arena, self.space, self.off + i * n, sub if sub else (n,))

    def cells(self):
        G = 32
        return [(self.space, c) for c in range(self.off // G, (self.off + self.n - 1) // G + 1)]


class Alloc:
    def __init__(self, arena, space, size):
        self.arena, self.space, self.size, self.top = arena, space, size, 0
        self.peak = 0

    def get(self, *shape):
        n = 1
        for s in shape:
            n *= s
        n2 = (n + 31) // 32 * 32
        t = Tile(self.arena, self.space, self.top, shape)
        self.top += n2
        self.peak = max(self.peak, self.top)
        assert self.top <= self.size, (self.space, self.top, self.size)
        return t

    def mark(self):
        return self.top

    def release(self, m):
        self.top = m


class Prog:
    ENG = ('pe', 'act', 'dve', 'pool', 'sp')

    def __init__(self, nc, same_engine_sync=True):
        self.nc = nc
        self.ops = {e: [] for e in self.ENG}
        self.cnt = {e: 0 for e in self.ENG}
        self.dcnt = {}
        self.last_w = {}
        self.last_r = {}
        self.seen = {e: {} for e in self.ENG}
        self.ses = same_engine_sync
        self.raw_only = SAME_ENGINE_RAW_ONLY

    @staticmethod
    def _cells(items):
        out = []
        for it in items:
            if isinstance(it, Tile):
                out.extend(it.cells())
            else:
                out.append(it)
        return out

    def _deps(self, eng, reads, writes, is_dma):
        deps = {}
        same_raw = 0

        def add(s, n):
            if deps.get(s, 0) < n:
                deps[s] = n
        for r in reads:
            ev = self.last_w.get(r)
            if ev:
                add(*ev)
                if ev[0] == eng and ev[1] > same_raw:
                    same_raw = ev[1]
        for w in writes:
            ev = self.last_w.get(w)
            if ev:
                add(*ev)
            for s, n in self.last_r.get(w, {}).items():
                add(s, n)
        waits = []
        for s, n in deps.items():
            if s == eng and not is_dma:
                if eng == 'pe' or not self.ses:
                    continue
                if self.raw_only:
                    n = same_raw
                    if n == 0:
                        continue
            if self.seen[eng].get(s, 0) >= n:
                continue
            self.seen[eng][s] = n
            waits.append((s, n))
        return waits

    def _commit(self, ev, reads, writes):
        s, n = ev
        for r in reads:
            d = self.last_r.setdefault(r, {})
            if d.get(s, 0) < n:
                d[s] = n
        for w in writes:
            self.last_w[w] = ev
            self.last_r[w] = {}

    def op(self, eng, fn, *args, R=(), W=(), **kw):
        if isinstance(fn, str):
            name = fn
            fn = (lambda e, name=name, args=args, kw=kw: getattr(e, name)(*args, **kw))
        reads, writes = self._cells(R), self._cells(W)
        waits = self._deps(eng, reads, writes, False)
        self.cnt[eng] += 1
        ev = (eng, self.cnt[eng])
        self.ops[eng].append((waits, fn, ev))
        self._commit(ev, reads, writes)

    def dma(self, eng, out, in_, R=(), W=(), dkey=None, **kw):
        reads, writes = self._cells(R), self._cells(W)
        dk = 'd:' + dkey
        n = self.dcnt.get(dk, 0)
        waits = self._deps(eng, reads, writes, True)
        if n > 0 and self.seen[eng].get(dk, 0) < n:
            self.seen[eng][dk] = n
            waits.append((dk, n))
        self.dcnt[dk] = n + 16
        ev = (dk, n + 16)
        self.ops[eng].append((waits, (lambda e: e.dma_start(out=out, in_=in_, **kw)), ev))
        self._commit(ev, reads, writes)

    def final_wait(self, eng='sp'):
        waits = [(dk, n) for dk, n in self.dcnt.items()]
        for e in self.ENG:
            if e != eng and self.cnt[e] > 0:
                waits.append((e, self.cnt[e]))
        self.ops[eng].append((waits, None, None))

    def emit(self):
        nc = self.nc
        names = list(self.ENG) + sorted(self.dcnt.keys())
        with ExitStack() as st:
            sems = {nm: st.enter_context(nc.semaphore('s_' + nm.replace(':', '_'))) for nm in names}
            block = st.enter_context(nc.Block())

            def run(engname, e):
                for waits, fn, ev in self.ops[engname]:
                    emb = None
                    if EMBED_WAIT and fn is not None and waits and not ev[0].startswith('d:'):
                        emb = waits[-1]
                        waits = waits[:-1]
                    for s, n in waits:
                        e.wait_ge(sems[s], n)
                    if fn is None:
                        continue
                    ins = fn(e)
                    if emb is not None:
                        ins._wait_ge(sems[emb[0]], emb[1])
                    ins.then_inc(sems[ev[0]], 16 if ev[0].startswith('d:') else 1)

            @block.tensor
            def _(e):
                run('pe', e)

            @block.scalar
            def _(e):
                run('act', e)

            @block.vector
            def _(e):
                run('dve', e)

            @block.gpsimd
            def _(e):
                run('pool', e)

            @block.sync
            def _(e):
                run('sp', e)


def build_program():
    nc = bass.Bass("TRN2", target_bir_lowering=False)
    nc.dge_precook = False

    def din(name, shape, dt=F32):
        return nc.dram_tensor(name, list(shape), dt, kind="ExternalInput").ap()

    def dout(name, shape):
        return nc.dram_tensor(name, list(shape), F32, kind="ExternalOutput").ap()

    x_in = din("x_t", [NFC, 128, T])
    cond_in = din("cond", [128, NFC, 2])
    adaw = din("adaw", [L, 36, 128, 8, 256], F32R)
    adab = din("adab", [L, 128, 72])
    normg = din("normg", [128, L, 3, NFC])
    fng = din("fng", [128, NFC])
    wi = din("wi", [L, 2, NJ, 128, 8, 256], F32R)
    wo = din("wo", [L, 2, NFC, 128, NJ * 128], F32R)
    wsm = din("wsm", [L, 128, 8, 32], F32R)
    smb = din("smb", [L, 128, 32])
    sma = din("sma", [L, 128, 24])
    ssdcw = din("ssdcw", [L, 128, 6, 4])
    ssdv = din("ssdv", [L, 128, 4, 2])
    consts_in = din("consts", [128, 7, 128], F32R)
    cmask_in = din("cmask", [128, 2, T], F32R)
    ssdh0 = din("ssdh0", [L, 2, 128, 512], F32R)
    dncw = din("dncw", [L, 128, 12, 3])
    dnv = din("dnv", [L, 128, 1])
    dnh0 = din("dnh0", [L, 2, 4, 128, 128], F32R)
    wfm = din("wfm", [L, 54, 128, 8, 128], F32R)
    wbr = din("wbr", [L, 4, NFC, 128, 4, 128], F32R)
    wout = din("wout", [L, NFC, 128, 8, 128], F32R)
    s5p = din("s5p", [L, 2, 128, 3, 16])
    s5b = din("s5b", [L, 2, 2, 128, 512], F32R)
    s5c = din("s5c", [L, 2, 2, 128, 16, 128])
    s5d = din("s5d", [L, 128, 4])
    s5h0 = din("s5h0", [L, 2, 128, 2, 16])
    cf_in = din("cf", [128, 1])
    y_out = dout("y_t", [NFC, 128, T])
    st_s5 = dout("st_s5", [2, L, 2, 128, 16, 4])
    st_ssd = dout("st_ssd", [L, 2, 4, 128, 512])
    st_dn = dout("st_dn", [L, 2, 4, 4, 128, 128])

    with ExitStack() as st:
        arena = st.enter_context(nc.sbuf_tensor("arena", [128, NF_SB], F32))
        arena_r = st.enter_context(nc.sbuf_tensor("arena_r", [128, NR_SB], F32))
        psum = st.enter_context(nc.psum_tensor("psum", [128, 8 * 512], F32))
        SB = Alloc(arena, 'sb', NF_SB)
        SR = Alloc(arena_r, 'sr', NR_SB)
        PSA = Alloc(psum, 'ps', 4096)
        banks = [PSA.get(512) for _ in range(8)]
        P = Prog(nc)
        bank_i = [0]

        nrot = [8]
        pin_i = [0]

        def pbank():
            b = banks[bank_i[0] % nrot[0]]
            bank_i[0] += 1
            return b

        def pinbank():
            b = banks[6 + pin_i[0] % 2]
            pin_i[0] += 1
            return b

        def tap(name, tile, n=None):
            if not DEBUG:
                return
            n = n or tile.n
            dd = nc.dram_tensor("dbg_" + name, [128, n], F32, kind="ExternalOutput").ap()
            P.dma('sp', dd, tile.arena[:, tile.off:tile.off + n], R=[tile], dkey='dbg')

        ones = SB.get(128)
        P.op('pool', 'memset', ones.ap(), 1.0, W=[ones])
        onesm = SR.get(128)
        P.op('dve', 'tensor_copy', onesm.r(), ones.ap(), R=[ones], W=[onesm])
        epst_b = SB.get(1)
        P.op('pool', 'memset', epst_b.ap(), EPS, W=[epst_b])
        zer = SB.get(512 if not MIXER else 1)
        P.op('pool', 'memset', zer.ap(), 0.0, W=[zer])

        x = SB.get(NFC, T)
        for fc in range(NFC):
            P.dma('sp', x[fc].ap(), x_in[fc], W=[x[fc]], dkey='xin%d' % (fc % 2))
        modT = SB.get(L, 72)
        ng = SB.get(L * 3 * NFC)
        P.dma('sp', ng.ap(), normg.rearrange("p l i f -> p (l i f)"), W=[ng], dkey='c0')
        fg = SB.get(NFC)
        P.dma('sp', fg.ap(), fng, W=[fg], dkey='c1')
        ab = SB.get(L, 72)
        for l in range(L):
            P.dma('sp', ab[l].ap(), adab[l], W=[ab[l]], dkey='c0')

        m0 = SB.mark()
        mr0 = SR.mark()
        cnd = SB.get(NFC, 2)
        P.dma('sp', cnd.ap(), cond_in, W=[cnd], dkey='c1')
        sc = SR.get(NFC, 2)
        P.op('act', 'activation', sc.r(), cnd.ap(), AF.Silu, R=[cnd], W=[sc])
        NB = 3
        abuf = [SR.get(8, 256) for _ in range(NB)]
        it = 0
        for l in range(L):
            for mp in range(36):
                wb = abuf[it % NB]
                it += 1
                P.dma('sp', wb.r(), adaw[l, mp], W=[wb], dkey='aw%d' % (it % NB))
                pb = pbank()
                for s in range(2):
                    for kc in range(8):
                        P.op('pe', 'matmul', pb.ap()[:, 2 * s:2 * s + 2], wb.r()[:, kc, s * 128:(s + 1) * 128],
                             sc.r()[:, kc, :], start=(kc == 0), stop=(kc == 7), R=[wb, sc], W=[pb])
                col = mp * 2
                P.op('dve', 'tensor_tensor', modT.ap()[:, l, col:col + 2], pb.ap()[:, 0:4:2],
                     ab.ap()[:, l, col:col + 2], ALU.add, R=[pb, ab[l]], W=[modT[l]])
        SB.release(m0)
        SR.release(mr0)

        Avec = SB.get(L, 3 * NFC)
        Gvec = SB.get(L, 3 * NFC)
        for l in range(L):
            for i in range(3):
                P.op('dve', 'scalar_tensor_tensor', Avec.ap()[:, l, i * 8:(i + 1) * 8],
                     modT.ap()[:, l, (3 * i + 1) * 8:(3 * i + 2) * 8], 1.0,
                     ng.ap()[:, (l * 3 + i) * 8:(l * 3 + i + 1) * 8], ALU.add, ALU.mult,
                     R=[modT[l], ng], W=[Avec[l]])
                P.op('dve', 'tensor_scalar', Gvec.ap()[:, l, i * 8:(i + 1) * 8],
                     modT.ap()[:, l, (3 * i + 2) * 8:(3 * i + 3) * 8], (1.0 if i == 1 else 0.5), None, ALU.mult,
                     R=[modT[l]], W=[Gvec[l]])
        tap('modT', modT)
        tap('Avec', Avec)
        tap('Gvec', Gvec)

        sqb = [SR.get(HALF) for _ in range(2)]
        tmpb = [SB.get(HALF) for _ in range(2)]
        rstd = [SB.get(HALF) for _ in range(2)]
        h = SR.get(NFC, T)
        sqi = [0]

        def hs(half):
            return slice(half * HALF, (half + 1) * HALF)

        def rms_rstd(src, half, out_rstd, nch=NFC):
            pb = pbank()
            sl = hs(half)
            for fc in range(nch):
                sq = sqb[sqi[0] % 2]
                sqi[0] += 1
                P.op('act', 'activation', sq.r(), src.ap()[:, fc, sl], AF.Square, R=[src[fc]], W=[sq])
                P.op('pe', 'matmul', pb.ap(), onesm.r(), sq.r(), start=(fc == 0), stop=(fc == nch - 1),
                     R=[sq, onesm], W=[pb])
            P.op('act', 'activation', out_rstd.ap(), pb.ap(), AF.Sqrt, bias=epst_b.ap(), scale=1.0 / (128 * nch),
                 R=[pb, epst_b], W=[out_rstd])
            P.op('dve', 'reciprocal', out_rstd.ap(), out_rstd.ap(), R=[out_rstd], W=[out_rstd])

        def norm_mod(l, i, hdst):
            for half in range(2):
                sl = hs(half)
                rs = rstd[half]
                rms_rstd(x, half, rs)
                for fc in range(NFC):
                    tm = tmpb[fc % 2]
                    P.op('dve', 'tensor_tensor', tm.ap(), x.ap()[:, fc, sl], rs.ap(), ALU.mult,
                         R=[x[fc], rs], W=[tm])
                    P.op('act', 'activation', hdst.r()[:, fc, sl], tm.ap(), AF.Identity,
                         scale=Avec.ap()[:, l, i * 8 + fc:i * 8 + fc + 1],
                         bias=modT.ap()[:, l, 3 * i * 8 + fc:3 * i * 8 + fc + 1],
                         R=[tm, Avec[l], modT[l]], W=[hdst[fc]])

        def ffn(l, f):
            i = 0 if f == 0 else 2
            norm_mod(l, i, h)
            if l == 0 and f == 0:
                tap('h0', h)
            m1 = SR.mark()
            act = SR.get(NJ, T)
            wbufs = [SR.get(8, 256) for _ in range(2)]
            for j in range(NJ):
                wb = wbufs[j % 2]
                P.dma('sp', wb.r(), wi[l, f, j], W=[wb], dkey='wi%d' % (j % 2))
                for half in range(2):
                    sl = hs(half)
                    pa, pbk = pbank(), pbank()
                    for s, pp in ((0, pa), (1, pbk)):
                        for kc in range(8):
                            P.op('pe', 'matmul', pp.ap(), wb.r()[:, kc, s * 128:(s + 1) * 128], h.r()[:, kc, sl],
                                 start=(kc == 0), stop=(kc == 7), R=[wb, h[kc]], W=[pp])
                    tm = tmpb[half]
                    P.op('act', 'activation', tm.ap(), pa.ap(), AF.Silu, R=[pa], W=[tm])
                    P.op('dve', 'tensor_tensor', act.r()[:, j, sl], tm.ap(), pbk.ap(), ALU.mult,
                         R=[tm, pbk], W=[act[j]])
            SR.release(m1)
            act = SR.get(NJ, T)
            wobufs = [SR.get(NJ, 128) for _ in range(2)]
            for mc in range(NFC):
                wb = wobufs[mc % 2]
                P.dma('sp', wb.r(), wo[l, f, mc].rearrange("p (j m) -> p j m", m=128), W=[wb], dkey='wo%d' % (mc % 2))
                for half in range(2):
                    sl = hs(half)
                    po = pbank()
                    for j in range(NJ):
                        P.op('pe', 'matmul', po.ap(), wb.r()[:, j, :], act.r()[:, j, sl],
                             start=(j == 0), stop=(j == NJ - 1), R=[wb, act[j]], W=[po])
                    P.op('dve', 'scalar_tensor_tensor', x.ap()[:, mc, sl], po.ap(),
                         Gvec.ap()[:, l, i * 8 + mc:i * 8 + mc + 1], x.ap()[:, mc, sl], ALU.mult, ALU.add,
                         R=[po, Gvec[l], x[mc]], W=[x[mc]])
            SR.release(m1)


        def lockstep(gens, lag=1):
            live = [True] * len(gens)
            step = 0
            while any(live):
                for gi_ in range(len(gens)):
                    if not live[gi_]:
                        continue
                    if gi_ * lag > step and live[0]:
                        continue
                    try:
                        next(gens[gi_])
                    except StopIteration:
                        live[gi_] = False
                step += 1

        cft = SB.get(1)
        P.dma('sp', cft.ap(), cf_in, W=[cft], dkey='c1')
        MAGIC = 12582912.0
        TWO_PI_HI = 6.28125
        TWO_PI_LO = 6.283185307179586 - 6.28125

        class VS:
            def __init__(self, n):
                self.n = n

            def new(self):
                return SB.get(self.n)

            def tt(self, a, b, op):
                o = self.new()
                P.op('dve', 'tensor_tensor', o.ap(), a.ap(), b.ap(), op, R=[a, b], W=[o])
                return o

            def mul(self, a, b): return self.tt(a, b, ALU.mult)
            def add(self, a, b): return self.tt(a, b, ALU.add)
            def sub(self, a, b): return self.tt(a, b, ALU.subtract)

            def ts(self, a, s1, s2, op0, op1=None):
                o = self.new()
                if op1 is None:
                    P.op('dve', 'tensor_scalar', o.ap(), a.ap(), s1, None, op0, R=[a], W=[o])
                else:
                    P.op('dve', 'tensor_scalar', o.ap(), a.ap(), s1, s2, op0, op1, R=[a], W=[o])
                return o

            def stt(self, a, sc_, b, op0, op1):
                o = self.new()
                P.op('dve', 'scalar_tensor_tensor', o.ap(), a.ap(), sc_, b.ap(), op0, op1, R=[a, b], W=[o])
                return o

            def act(self, a, func, scale=1.0):
                o = self.new()
                P.op('act', 'activation', o.ap(), a.ap(), func, scale=scale, R=[a], W=[o])
                return o

            def recip(self, a):
                o = self.new()
                P.op('dve', 'reciprocal', o.ap(), a.ap(), R=[a], W=[o])
                return o

            def cmul(self, ar, ai, br, bi):
                return (self.sub(self.mul(ar, br), self.mul(ai, bi)), self.add(self.mul(ar, bi), self.mul(ai, br)))

        def bc3(t2, n):
            return t2.ap().unsqueeze(2).to_broadcast([128, t2.n, n])

        def finalize(l, b, actb, first, mg):
            mk = SR.mark()
            nb = 2 if b == 0 else 1
            wbs = [[SR.get(4, 128) for _ in range(nb)] for _ in range(2)]
            wgs = [SR.get(8, 128) for _ in range(2)]
            for fc in range(NFC):
                wg = wgs[fc % 2]
                P.dma('sp', wg.r(), wfm[l, 26 + b * 8 + fc], W=[wg], dkey='fg%d' % (fc % 2))
                wb_ = wbs[fc % 2]
                for k in range(nb):
                    bi = (k if b == 0 else b + 1)
                    P.dma('sp', wb_[k].r(), wbr[l, bi, fc], W=[wb_[k]], dkey='fb%d' % (fc % 2))
                for half in range(2):
                    sl = hs(half)
                    pg = pbank()
                    for kc in range(8):
                        P.op('pe', 'matmul', pg.ap(), wg.r()[:, kc, :], h.r()[:, kc, sl], start=(kc == 0), stop=(kc == 7),
                             R=[wg, h[kc]], W=[pg])
                    gt = tmpb[0]
                    P.op('act', 'activation', gt.ap(), pg.ap(), AF.Sigmoid, R=[pg], W=[gt])
                    p0 = pbank()
                    for kc in range(4):
                        P.op('pe', 'matmul', p0.ap(), wb_[0].r()[:, kc, :], actb.r()[:, kc, sl], start=(kc == 0), stop=(kc == 3),
                             R=[wb_[0], actb[kc]], W=[p0])
                    if b == 0:
                        p1 = pbank()
                        for kc in range(4):
                            P.op('pe', 'matmul', p1.ap(), wb_[1].r()[:, kc, :], actb.r()[:, kc, sl], start=(kc == 0), stop=(kc == 3),
                                 R=[wb_[1], actb[kc]], W=[p1])
                        s1 = tmpb[1]
                        P.op('act', 'activation', s1.ap(), p1.ap(), AF.Sigmoid, R=[p1], W=[s1])
                        P.op('dve', 'tensor_tensor', s1.ap(), s1.ap(), p0.ap(), ALU.mult, R=[s1, p0], W=[s1])
                        brs, brt = s1.ap(), s1
                    else:
                        brs, brt = p0.ap(), p0
                    if first:
                        P.op('dve', 'tensor_tensor', mg.r()[:, fc, sl], gt.ap(), brs, ALU.mult, R=[gt, brt], W=[mg[fc]])
                    else:
                        P.op('dve', 'tensor_tensor', gt.ap(), gt.ap(), brs, ALU.mult, R=[gt, brt], W=[gt])
                        P.op('dve', 'tensor_tensor', mg.r()[:, fc, sl], mg.ap()[:, fc, sl], gt.ap(), ALU.add,
                             R=[gt, mg[fc]], W=[mg[fc]])
            SR.release(mk)

        def proj_fm(l, chunk, dst_fn, wfb):
            wb = wfb[chunk % 2]
            P.dma('sp', wb.r(), wfm[l, chunk], W=[wb], dkey='wf%d' % (chunk % 2))
            for half in range(2):
                sl = hs(half)
                pp = pbank()
                for kc in range(8):
                    P.op('pe', 'matmul', pp.ap(), wb.r()[:, kc, :], h.r()[:, kc, sl], start=(kc == 0), stop=(kc == 7),
                         R=[wb, h[kc]], W=[pp])
                dst_fn(half, pp)

        def s5_branch(l, u):
            mkF, mkR = SB.mark(), SR.mark()
            nrot[0] = 6
            wfb = [SR.get(8, 128) for _ in range(2)]
            for q in range(4):
                def ev(half, pp, q=q):
                    P.op('act', 'activation', u.r()[:, q, hs(half)], pp.ap(), AF.Identity, R=[pp], W=[u[q]])
                proj_fm(l, q, ev, wfb)
            SR.release(mkR)
            ys = SR.get(4, T)
            dsk = SB.get(4)
            P.dma('sp', dsk.ap(), s5d[l], W=[dsk], dkey='c0')
            Ec, Es = SR.get(16, 128), SR.get(16, 128)
            Bb = [SR.get(4, 128) for _ in range(2)]
            Cb = [SR.get(16, 128) for _ in range(2)]
            nCb0 = SR.get(16, 128)
            wk = [SR.get(512) for _ in range(6)]
            wkall = Tile(arena_r, 'sr', wk[0].off, (3072,))
            wk2 = [wk + [SR.get(512), SR.get(512)], [SR.get(512) for _ in range(8)]]
            SBt2 = [[SB.get(2), SB.get(2)], [SB.get(2), SB.get(2)]]
            hst = [SB.get(16, 4) for _ in range(2)]
            V = VS(16)
            for d in range(2):
                mkd = SB.mark()
                prm = SB.get(3, 16)
                P.dma('sp', prm.ap(), s5p[l, d], W=[prm], dkey='c1')
                h0 = SB.get(2, 16)
                P.dma('sp', h0.ap(), s5h0[l, d], W=[h0], dkey='c0')
                for ri in range(2):
                    P.dma('sp', Bb[ri].r().rearrange("p a b -> p (a b)"), s5b[l, d, ri], W=[Bb[ri]], dkey='sb%d' % ri)
                lamr, lami, ldt = prm[0], prm[1], prm[2]
                dt = V.act(ldt, AF.Exp)
                ar = V.mul(lamr, dt)
                th = V.mul(lami, dt)
                rr = V.act(ar, AF.Exp)

                def red(tx):
                    k = V.ts(tx, 1.0 / (2 * np.pi), MAGIC, ALU.mult, ALU.add)
                    k = V.ts(k, -MAGIC, None, ALU.add)
                    t_ = V.stt(k, -TWO_PI_HI, tx, ALU.mult, ALU.add)
                    return V.stt(k, -TWO_PI_LO, t_, ALU.mult, ALU.add)
                sn = V.act(red(th), AF.Sin)
                cs = V.act(red(V.ts(th, np.pi / 2, None, ALU.add)), AF.Sin)
                lbr, lbi = V.mul(rr, cs), V.mul(rr, sn)
                den = V.add(V.mul(lamr, lamr), V.mul(lami, lami))
                inv = V.recip(den)
                lm1 = V.ts(lbr, -1.0, None, ALU.add)
                cr = V.mul(V.add(V.mul(lm1, lamr), V.mul(lbi, lami)), inv)
                ci = V.mul(V.sub(V.mul(lbi, lamr), V.mul(lm1, lami)), inv)
                ncr, nci = V.ts(cr, -1.0, None, ALU.mult), V.ts(ci, -1.0, None, ALU.mult)
                cinv = V.recip(V.add(V.mul(cr, cr), V.mul(ci, ci)))
                qr, qi = V.cmul(h0[0], h0[1], cr, nci)
                qr, qi = V.mul(qr, cinv), V.mul(qi, cinv)
                i0r, i0i = V.cmul(qr, qi, cs, sn)
                P.op('dve', 'tensor_scalar', Ec.r()[:, :, 0:1], ones.ap()[:, 0:16].unsqueeze(2), 1.0, None, ALU.mult, R=[ones], W=[Ec])
                P.op('dve', 'tensor_scalar', Es.r()[:, :, 0:1], ones.ap()[:, 0:16].unsqueeze(2), 0.0, None, ALU.mult, R=[ones], W=[Es])
                wc, ws = cs, sn
                for k in range(7):
                    n = 1 << k
                    def tv(i):
                        return wkall.r()[:, 1024 * i:1024 * i + 16 * n].rearrange("p (a b) -> p a b", a=16), wkall
                    (a1, T1), (a2, T2), (a3, T3) = tv(0), tv(1), tv(2)
                    a4, T4 = tmp2k.ap()[:, 0:16 * n].rearrange("p (a b) -> p a b", a=16), tmp2k
                    P.op('dve', 'tensor_tensor', a1, Ec.ap()[:, :, 0:n], bc3(wc, n), ALU.mult, R=[Ec, wc], W=[T1])
                    P.op('dve', 'tensor_tensor', a2, Es.ap()[:, :, 0:n], bc3(ws, n), ALU.mult, R=[Es, ws], W=[T2])
                    P.op('dve', 'tensor_tensor', a3, Ec.ap()[:, :, 0:n], bc3(ws, n), ALU.mult, R=[Ec, ws], W=[T3])
                    P.op('dve', 'tensor_tensor', a4, Es.ap()[:, :, 0:n], bc3(wc, n), ALU.mult, R=[Es, wc], W=[T4])
                    P.op('dve', 'tensor_tensor', Ec.r()[:, :, n:2 * n], a1, a2, ALU.subtract, R=[T1, T2], W=[Ec])
                    P.op('dve', 'tensor_tensor', Es.r()[:, :, n:2 * n], a3, a4, ALU.add, R=[T3, T4], W=[Es])
                    wc, ws = V.sub(V.mul(wc, wc), V.mul(ws, ws)), V.ts(V.mul(wc, ws), 2.0, None, ALU.mult)
                wLr, wLi = wc, ws
                wLrc, wLic = V.new(), V.new()
                P.op('dve', 'tensor_scalar', wLrc.ap(), wLr.ap(), cft.ap()[:, 0:1], None, ALU.mult, R=[wLr, cft], W=[wLrc])
                P.op('dve', 'tensor_scalar', wLic.ap(), wLi.ap(), cft.ap()[:, 0:1], None, ALU.mult, R=[wLi, cft], W=[wLic])
                for qq in range(4):
                    cre = tmp2k.ap()[:, 0:512].rearrange("p (a b) -> p a b", a=4)
                    cim = tmp2k.ap()[:, 512:1024].rearrange("p (a b) -> p a b", a=4)
                    P.dma('sp', cre, s5c[l, d, 0, :, 4 * qq:4 * qq + 4, :], W=[tmp2k], dkey='sc0')
                    P.dma('sp', cim, s5c[l, d, 1, :, 4 * qq:4 * qq + 4, :], W=[tmp2k], dkey='sc1')
                    tq = [wk[i].r().rearrange("p (a b) -> p a b", a=4) for i in range(4)]
                    def b4(v):
                        return v.ap()[:, 4 * qq:4 * qq + 4].unsqueeze(2).to_broadcast([128, 4, 128])
                    P.op('dve', 'tensor_tensor', tq[0], cre, b4(cr), ALU.mult, R=[tmp2k, cr], W=[wk[0]])
                    P.op('dve', 'tensor_tensor', tq[1], cim, b4(ci), ALU.mult, R=[tmp2k, ci], W=[wk[1]])
                    P.op('dve', 'tensor_tensor', tq[2], cim, b4(ncr), ALU.mult, R=[tmp2k, ncr], W=[wk[2]])
                    P.op('dve', 'tensor_tensor', tq[3], cre, b4(ci), ALU.mult, R=[tmp2k, ci], W=[wk[3]])
                    P.op('dve', 'tensor_tensor', Cb[0].r()[:, 4 * qq:4 * qq + 4, :], tq[0], tq[1], ALU.subtract,
                         R=[wk[0], wk[1]], W=[Cb[0]])
                    P.op('dve', 'tensor_tensor', Cb[1].r()[:, 4 * qq:4 * qq + 4, :], tq[2], tq[3], ALU.subtract,
                         R=[wk[2], wk[3]], W=[Cb[1]])
                    P.op('dve', 'tensor_tensor', nCb0.r()[:, 4 * qq:4 * qq + 4, :], tq[1], tq[0], ALU.subtract,
                         R=[wk[0], wk[1]], W=[nCb0])
                car = SB.get(16, 2)
                P.op('dve', 'tensor_copy', car.ap()[:, :, 0], i0r.ap(), R=[i0r], W=[car])
                P.op('dve', 'tensor_copy', car.ap()[:, :, 1], i0i.ap(), R=[i0i], W=[car])
                wpN, wpC = SB.get(16, 2), SB.get(16, 2)
                for wp_, wi_src in ((wpN, wLi), (wpC, wLic)):
                    P.op('dve', 'tensor_scalar', wp_.ap()[:, :, 0], wi_src.ap(), -1.0, None, ALU.mult, R=[wi_src], W=[wp_])
                    P.op('dve', 'tensor_copy', wp_.ap()[:, :, 1], wi_src.ap(), R=[wi_src], W=[wp_])
                rev = (d == 1)
                halves = [1, 0] if rev else [0, 1]
                for half in halves:
                    sl = hs(half)
                    for q in range(4):
                        py = pinbank()
                        def unit(ch, mi):
                            mc = 4 * q + mi
                            W_ = wk2[ch]
                            vrT, viT, grT, giT, t1T, t2T, t3T, t4T = W_
                            vr, vi, gr, gi, t1, t2, t3, t4 = [w_.r() for w_ in W_]
                            pa, pb_ = pbank(), pbank()
                            r0 = 32 * mi
                            for ri, pp in ((0, pa), (1, pb_)):
                                P.op('pe', 'matmul', pp.ap(), Bb[ri].r()[r0:r0 + 32, q, :], u.r()[r0:r0 + 32, q, sl],
                                     start=True, stop=True, tile_position=(r0, 0), R=[Bb[ri], u[q]], W=[pp])
                            yield
                            tabc = Ec.ap()[:, mc, ::-1] if rev else Ec.ap()[:, mc, :]
                            tabs = Es.ap()[:, mc, ::-1] if rev else Es.ap()[:, mc, :]
                            tc4 = tabc.unsqueeze(1).to_broadcast([128, 4, 128])
                            tsn4 = tabs.unsqueeze(1).to_broadcast([128, 4, 128])
                            def v3(a):
                                return a.rearrange("p (a b) -> p a b", a=4)
                            P.op('act', 'activation', gr, pa.ap(), AF.Identity, R=[pa], W=[grT])
                            P.op('act', 'activation', gi, pb_.ap(), AF.Identity, R=[pb_], W=[giT])
                            yield
                            P.op('dve', 'tensor_tensor', v3(t1), v3(gr), tc4, ALU.mult, R=[grT, Ec], W=[t1T])
                            P.op('dve', 'tensor_tensor', v3(t2), v3(gi), tsn4, ALU.mult, R=[giT, Es], W=[t2T])
                            P.op('pool', 'tensor_tensor', v3(t3), v3(gi), tc4, ALU.mult, R=[giT, Ec], W=[t3T])
                            P.op('pool', 'tensor_tensor', v3(t4), v3(gr), tsn4, ALU.mult, R=[grT, Es], W=[t4T])
                            yield
                            P.op('dve', 'tensor_tensor', vr, t1, t2, ALU.add, R=[t1T, t2T], W=[vrT])
                            yield
                            P.op('dve', 'tensor_tensor', vi, t3, t4, ALU.subtract, R=[t3T, t4T], W=[viT])
                            yield
                            corder = [3, 2, 1, 0] if rev else [0, 1, 2, 3]
                            for ci_, c in enumerate(corder):
                                cs_ = slice(c * 128, (c + 1) * 128)
                                def dv(a):
                                    a = a[:, cs_]
                                    return a[:, ::-1] if rev else a
                                rb = rr.ap()[:, mc:mc + 1].to_broadcast([128, 128])
                                P.op('dve', 'tensor_tensor_scan', dv(gr), rb, dv(vr), car.ap()[:, mc, 0:1], ALU.mult, ALU.add,
                                     R=[rr, vrT, car], W=[grT])
                                P.op('dve', 'tensor_tensor_scan', dv(gi), rb, dv(vi), car.ap()[:, mc, 1:2], ALU.mult, ALU.add,
                                     R=[rr, viT, car], W=[giT])
                                yield
                                gc = c * 128 if rev else c * 128 + 127
                                gch = 4 * half + c
                                nxt = gch - 1 if rev else gch + 1
                                segb = (gch % 2 == 0) if rev else (nxt % 2 == 0)
                                wr_, wp_ = (wLrc, wpC) if segb else (wLr, wpN)
                                g0, g1 = grT.off + gc, giT.off + gc
                                assert g1 == g0 + 512
                                glast = arena_r[:, g0:g0 + 513:512]
                                glrev = arena_r[:, g1:g0 - 1:-512]
                                tP = SBt2[ch][0].ap()[:, 0:2]
                                P.op('dve', 'tensor_tensor', tP, glrev, wp_.ap()[:, mc, :], ALU.mult,
                                     R=[grT, giT, wp_], W=[SBt2[ch][0]])
                                yield
                                P.op('dve', 'scalar_tensor_tensor', car.ap()[:, mc, :], glast, wr_.ap()[:, mc:mc + 1], tP,
                                     ALU.mult, ALU.add, R=[grT, giT, wr_, SBt2[ch][0]], W=[car])
                                yield
                            P.op('pool', 'tensor_tensor', v3(t1), v3(gr), tc4, ALU.mult, R=[grT, Ec], W=[t1T])
                            P.op('pool', 'tensor_tensor', v3(t2), v3(gi), tsn4, ALU.mult, R=[giT, Es], W=[t2T])
                            yield
                            P.op('pool', 'tensor_tensor', v3(t3), v3(gi), tc4, ALU.mult, R=[giT, Ec], W=[t3T])
                            P.op('pool', 'tensor_tensor', v3(t4), v3(gr), tsn4, ALU.mult, R=[grT, Es], W=[t4T])
                            yield
                            c0 = 0 if rev else 255
                            P.op('pool', 'tensor_tensor', hst[0].ap()[:, mc, 2 * half:2 * half + 2], t1[:, c0::256], t2[:, c0::256], ALU.subtract,
                                 R=[t1T, t2T], W=[hst[0]])
                            P.op('pool', 'tensor_tensor', hst[1].ap()[:, mc, 2 * half:2 * half + 2], t3[:, c0::256], t4[:, c0::256], ALU.add,
                                 R=[t3T, t4T], W=[hst[1]])
                            yield
                            P.op('pe', 'matmul', py.ap(), Cb[0].r()[:, mc, :], t1, start=(mi == 0), stop=False, R=[Cb[0], t1T], W=[py])
                            P.op('pe', 'matmul', py.ap(), nCb0.r()[:, mc, :], t2, start=False, stop=False, R=[nCb0, t2T], W=[py])
                            P.op('pe', 'matmul', py.ap(), Cb[1].r()[:, mc, :], t3, start=False, stop=False, R=[Cb[1], t3T], W=[py])
                            P.op('pe', 'matmul', py.ap(), Cb[1].r()[:, mc, :], t4, start=False, stop=(mi == 3), R=[Cb[1], t4T], W=[py])
                            yield

                        for pair in ((0, 1), (2, 3)):
                            gens = [unit(0, pair[0]), unit(1, pair[1])]
                            live = [True, True]
                            step = 0
                            while any(live):
                                for gi_ in range(2):
                                    if not live[gi_]:
                                        continue
                                    if gi_ == 1 and step < 2 and live[0]:
                                        continue
                                    try:
                                        next(gens[gi_])
                                    except StopIteration:
                                        live[gi_] = False
                                step += 1
                        if d == 0:
                            P.op('dve', 'scalar_tensor_tensor', ys.r()[:, q, sl], u.ap()[:, q, sl], dsk.ap()[:, q:q + 1], py.ap(),
                                 ALU.mult, ALU.add, R=[u[q], dsk, py], W=[ys[q]])
                        else:
                            P.op('dve', 'tensor_tensor', ys.r()[:, q, sl], ys.ap()[:, q, sl], py.ap(), ALU.add,
                                 R=[ys[q], py], W=[ys[q]])
                V64 = VS(64)
                so = [V64.new(), V64.new()]
                crb = cr.ap().unsqueeze(2).to_broadcast([128, 16, 4])
                cib = ci.ap().unsqueeze(2).to_broadcast([128, 16, 4])
                def s3(t_):
                    return t_.ap().rearrange("p (a b) -> p a b", a=16)
                ta, tb_ = V64.new(), V64.new()
                P.op('dve', 'tensor_tensor', s3(ta), hst[0].ap(), crb, ALU.mult, R=[hst[0], cr], W=[ta])
                P.op('dve', 'tensor_tensor', s3(tb_), hst[1].ap(), cib, ALU.mult, R=[hst[1], ci], W=[tb_])
                P.op('dve', 'tensor_tensor', so[0].ap(), ta.ap(), tb_.ap(), ALU.subtract, R=[ta, tb_], W=[so[0]])
                P.op('dve', 'tensor_tensor', s3(ta), hst[0].ap(), cib, ALU.mult, R=[hst[0], ci], W=[ta])
                P.op('dve', 'tensor_tensor', s3(tb_), hst[1].ap(), crb, ALU.mult, R=[hst[1], cr], W=[tb_])
                P.op('dve', 'tensor_tensor', so[1].ap(), ta.ap(), tb_.ap(), ALU.add, R=[ta, tb_], W=[so[1]])
                for ri in range(2):
                    P.dma('sp', st_s5[ri, l, d].rearrange("p a b -> p (a b)"), so[ri].ap(), R=[so[ri]], dkey='so%d' % ri)
                SB.release(mkd)
            if l == 0:
                tap('ys', ys)
            for q in range(4):
                for half in range(2):
                    sl = hs(half)
                    t1, t2 = tmpb[0], tmpb[1]
                    P.op('act', 'activation', t1.ap(), ys.ap()[:, q, sl], AF.Square, R=[ys[q]], W=[t1])
                    P.op('dve', 'tensor_scalar', t1.ap(), t1.ap(), 0.044715, 1.0, ALU.mult, ALU.add, R=[t1], W=[t1])
                    P.op('dve', 'tensor_tensor', t1.ap(), t1.ap(), ys.ap()[:, q, sl], ALU.mult, R=[t1, ys[q]], W=[t1])
                    P.op('act', 'activation', t2.ap(), t1.ap(), AF.Sigmoid, scale=1.5957691216057308, R=[t1], W=[t2])
                    P.op('dve', 'tensor_tensor', u.r()[:, q, sl], t2.ap(), ys.ap()[:, q, sl], ALU.mult, R=[t2, ys[q]], W=[u[q]])
            SB.release(mkF)
            SR.release(mkR)
            nrot[0] = 8


        def load_consts():
            cst = SR.get(7, 128)
            P.dma('sp', cst.r(), consts_in, W=[cst], dkey='c0')
            return cst

        def small_params(l):
            wsb = SR.get(8, 32)
            P.dma('sp', wsb.r(), wsm[l], W=[wsb], dkey='c1')
            bia = SB.get(32)
            P.dma('sp', bia.ap(), smb[l], W=[bia], dkey='c0')
            alg = SB.get(24)
            P.dma('sp', alg.ap(), sma[l], W=[alg], dkey='c1')
            pb = pbank()
            for tc in range(8):
                for kc in range(8):
                    P.op('pe', 'matmul', pb.ap()[:, tc * 32:(tc + 1) * 32], h.r()[:, kc, tc * 128:(tc + 1) * 128], wsb.r()[:, kc, :],
                         start=(kc == 0), stop=(kc == 7), R=[h[kc], wsb], W=[pb])
            sp = SB.get(8, 32)
            beta = SB.get(8, 8)
            na = SB.get(24)
            da = SB.get(8, 16)
            gdn = SB.get(8, 8)
            mkt = SB.mark()
            raw = SB.get(8, 32)
            P.op('dve', 'tensor_tensor', raw.ap(), pb.ap()[:, 0:256].rearrange("p (a b) -> p a b", a=8),
                 bia.ap().unsqueeze(1).to_broadcast([128, 8, 32]), ALU.add, R=[pb, bia], W=[raw])
            ex = SB.get(8, 32)
            P.op('act', 'activation', ex.ap(), raw.ap(), AF.Exp, R=[raw], W=[ex])
            P.op('act', 'activation', sp.ap(), ex.ap(), AF.Ln, bias=1.0, R=[ex], W=[sp])
            P.op('act', 'activation', beta.ap(), raw.ap()[:, :, 16:24], AF.Sigmoid, R=[raw], W=[beta])
            P.op('act', 'activation', na.ap(), alg.ap(), AF.Exp, R=[alg], W=[na])
            P.op('dve', 'tensor_scalar', na.ap(), na.ap(), -1.0, None, ALU.mult, R=[na], W=[na])
            P.op('dve', 'tensor_tensor', da.ap(), sp.ap()[:, :, 0:16], na.ap()[:, 0:16].unsqueeze(1).to_broadcast([128, 8, 16]),
                 ALU.mult, R=[sp, na], W=[da])
            P.op('dve', 'tensor_tensor', gdn.ap(), sp.ap()[:, :, 24:32], na.ap()[:, 16:24].unsqueeze(1).to_broadcast([128, 8, 8]),
                 ALU.mult, R=[sp, na], W=[gdn])
            SB.release(mkt)
            return dict(dt=sp, da=da, beta=beta, gdn=gdn)

        def conv_chunk(l, wchunk, cw, ci, dst, msk, tmps, wfb, bias=True):
            raw, xl, xr, acc = tmps
            def ev(half, pp):
                P.op('act', 'activation', raw.r()[:, hs(half)], pp.ap(), AF.Identity, R=[pp], W=[raw])
            proj_fm(l, wchunk, ev, wfb)
            P.op('pool', 'tensor_tensor', xl.r()[:, 0:T - 1], raw.ap()[:, 0:T - 1], msk.ap()[:, 0, 1:T], ALU.mult, R=[raw, msk], W=[xl])
            P.op('pool', 'tensor_tensor', xr.r()[:, 1:T], raw.ap()[:, 1:T], msk.ap()[:, 1, 1:T], ALU.mult, R=[raw, msk], W=[xr])
            if bias:
                P.op('act', 'activation', acc.r(), raw.ap(), AF.Identity, scale=cw.ap()[:, ci, 1:2], bias=cw.ap()[:, ci, 3:4],
                     R=[raw, cw], W=[acc])
            else:
                P.op('act', 'activation', acc.r(), raw.ap(), AF.Identity, scale=cw.ap()[:, ci, 1:2], R=[raw, cw], W=[acc])
            P.op('dve', 'scalar_tensor_tensor', acc.r()[:, 1:T], xl.ap()[:, 0:T - 1], cw.ap()[:, ci, 0:1], acc.ap()[:, 1:T],
                 ALU.mult, ALU.add, R=[xl, cw, acc], W=[acc])
            P.op('dve', 'scalar_tensor_tensor', acc.r()[:, 0:T - 1], xr.ap()[:, 1:T], cw.ap()[:, ci, 2:3], acc.ap()[:, 0:T - 1],
                 ALU.mult, ALU.add, R=[xr, cw, acc], W=[acc])
            P.op('act', 'activation', dst.r(), acc.ap(), AF.Silu, R=[acc], W=[dst])

        def decay_T(cst, d, strict, da_col, acs_col, out_tile, xt, tt_):
            TRI = cst[d]
            MN = cst[(5 if strict else 3) + d]
            P.op('dve', 'tensor_scalar', xt.ap(), TRI.ap(), da_col, None, ALU.mult, R=[TRI], W=[xt])
            bc = pbank()
            P.op('pe', 'matmul', bc.ap()[:, 0:128], ones.ap(), xt.ap(), start=True, stop=True, R=[ones, xt], W=[bc])
            P.op('dve', 'scalar_tensor_tensor', tt_.ap(), bc.ap()[:, 0:128], acs_col, MN.ap(), ALU.subtract, ALU.add,
                 R=[bc, MN], W=[tt_])
            P.op('act', 'activation', out_tile.ap(), tt_.ap(), AF.Exp, R=[tt_], W=[out_tile])
            return bc

        def ssd_branch(l, actb, sm, cst):
            mkF, mkR = SB.mark(), SR.mark()
            nrot[0] = 3
            B3, B4, B5, B6, B7 = banks[3], banks[4], banks[5], banks[6], banks[7]
            SCB = [B6, B3]
            xbc = SR.get(6, T)
            xs, Bm, Cm = [xbc[i] for i in range(4)], xbc[4], xbc[5]
            mk2 = SR.mark()
            wfb = [SR.get(8, 128) for _ in range(2)]
            msk = SR.get(2, T)
            P.dma('sp', msk.r(), cmask_in, W=[msk], dkey='c0')
            cw = SB.get(6, 4)
            P.dma('sp', cw.ap(), ssdcw[l], W=[cw], dkey='c1')
            tmps = [SR.get(T) for _ in range(4)]
            for ci in range(6):
                conv_chunk(l, 4 + ci, cw, ci, xbc[ci], msk, tmps, wfb)
            SR.release(mk2)
            if l == 0:
                tap('xbc', xbc)
            ST = [SR.get(512) for _ in range(2)]
            xdt, xdtw, BmT = SR.get(512), SR.get(512), SR.get(128)
            PT = [SR.get(128) for _ in range(2)]
            Xt = [SB.get(128) for _ in range(2)]
            Tt = [SB.get(128) for _ in range(2)]
            Dc = [SB.get(128) for _ in range(2)]
            yt = SB.get(512)
            acs, tot, eA, toe, cdd = SB.get(8), SB.get(8), SB.get(8), SB.get(8), SB.get(8)
            ident = cst[2]
            for d in range(2):
                P.dma('sp', ST[d].r(), ssdh0[l, d], W=[ST[d]], dkey='sh%d' % d)
                order = range(8) if d == 0 else range(7, -1, -1)
                for c in order:
                    tok = slice(c * 128, (c + 1) * 128)
                    if STAGE < 1:
                        continue
                    for kc in range(4):
                        P.op('pe', 'transpose', B4.ap()[:, kc * 128:(kc + 1) * 128], xs[kc].ap()[:, tok], ident.ap(),
                             R=[xs[kc], ident], W=[B4])
                    dtb = sm['dt'].ap()[:, c, 8 * d:8 * d + 8].unsqueeze(2).to_broadcast([128, 8, 64])
                    P.op('dve', 'tensor_tensor', xdt.r().rearrange("p (a b) -> p a b", a=8),
                         B4.ap().rearrange("p (a b) -> p a b", a=8), dtb, ALU.mult, R=[B4, sm['dt']], W=[xdt])
                    if STAGE < 2:
                        continue
                    P.op('pe', 'transpose', B5.ap()[:, 0:128], Bm.ap()[:, tok], ident.ap(), R=[Bm, ident], W=[B5])
                    P.op('act', 'activation', BmT.r(), B5.ap()[:, 0:128], AF.Identity, R=[B5], W=[BmT])
                    if STAGE < 3:
                        continue
                    dac = sm['da'].ap()[:, c, 8 * d:8 * d + 8]
                    P.op('pe', 'matmul', B5.ap()[:, 128:136], cst[d].ap(), dac, start=True, stop=True, R=[cst[d], sm['da']], W=[B5])
                    P.op('pe', 'matmul', B5.ap()[:, 136:144], ones.ap(), dac, start=True, stop=True, R=[ones, sm['da']], W=[B5])
                    P.op('dve', 'tensor_copy', acs.ap(), B5.ap()[:, 128:136], R=[B5], W=[acs])
                    P.op('dve', 'tensor_copy', tot.ap(), B5.ap()[:, 136:144], R=[B5], W=[tot])
                    P.op('act', 'activation', eA.ap(), acs.ap(), AF.Exp, R=[acs], W=[eA])
                    P.op('act', 'activation', cdd.ap(), tot.ap(), AF.Exp, R=[tot], W=[cdd])
                    P.op('dve', 'tensor_tensor', toe.ap(), tot.ap(), acs.ap(), ALU.subtract, R=[tot, acs], W=[toe])
                    P.op('act', 'activation', toe.ap(), toe.ap(), AF.Exp, R=[toe], W=[toe])
                    if STAGE < 4:
                        continue
                    for gr in range(2):
                        P.op('pe', 'matmul', SCB[gr].ap()[:, 0:128], Bm.r()[64 * gr:64 * gr + 64, tok],
                             Cm.r()[64 * gr:64 * gr + 64, tok], start=True, stop=True, tile_position=(64 * gr, 0),
                             R=[Bm, Cm], W=[SCB[gr]])
                    if STAGE < 5:
                        continue
                    def head_unit(hh):
                        gr = hh // 4
                        dc_, xt_, tt_, pt = Dc[hh % 2], Xt[hh % 2], Tt[hh % 2], PT[hh % 2]
                        da_col = sm['da'].ap()[:, c, 8 * d + hh:8 * d + hh + 1]
                        P.op('dve', 'tensor_scalar', xt_.ap(), cst[d].ap(), da_col, None, ALU.mult, R=[cst[d], sm['da']], W=[xt_])
                        bc = pbank()
                        P.op('pe', 'matmul', bc.ap()[:, 0:128], ones.ap(), xt_.ap(), start=True, stop=True, R=[ones, xt_], W=[bc])
                        yield
                        P.op('dve', 'scalar_tensor_tensor', tt_.ap(), bc.ap()[:, 0:128], acs.ap()[:, hh:hh + 1], cst[3 + d].ap(),
                             ALU.subtract, ALU.add, R=[bc, acs, cst[3 + d]], W=[tt_])
                        yield
                        P.op('act', 'activation', dc_.ap(), tt_.ap(), AF.Exp, R=[tt_], W=[dc_])
                        yield
                        P.op('dve', 'tensor_tensor', pt.r(), SCB[gr].ap()[:, 0:128], dc_.ap(), ALU.mult,
                             R=[SCB[gr], dc_], W=[pt])
                        yield
                        P.op('pe', 'matmul', B7.ap()[:, 64 * hh:64 * hh + 64], pt.r(), xdt.r()[:, 64 * hh:64 * hh + 64],
                             start=True, stop=True, R=[pt, xdt], W=[B7])
                        yield
                    for hp in range(4):
                        lockstep([head_unit(2 * hp), head_unit(2 * hp + 1)], lag=1)
                    if STAGE < 6:
                        continue
                    yo = pbank()
                    P.op('pe', 'matmul', yo.ap(), Cm.r()[:, tok], ST[d].r(), start=True, stop=True, R=[Cm, ST[d]], W=[yo])
                    eab = eA.ap().unsqueeze(2).to_broadcast([128, 8, 64])
                    P.op('dve', 'tensor_tensor', yt.ap().rearrange("p (a b) -> p a b", a=8),
                         yo.ap().rearrange("p (a b) -> p a b", a=8), eab, ALU.mult, R=[yo, eA], W=[yt])
                    P.op('dve', 'tensor_tensor', yt.ap(), yt.ap(), B7.ap(), ALU.add, R=[yt, B7], W=[yt])
                    if STAGE < 7:
                        continue
                    for kc in range(4):
                        P.op('pe', 'transpose', B4.ap()[:, kc * 128:(kc + 1) * 128], yt.ap()[:, kc * 128:(kc + 1) * 128], ident.ap(),
                             R=[yt, ident], W=[B4])
                    b4v = B4.ap().rearrange("p (a b) -> p a b", a=4)
                    if d == 0:
                        P.op('act', 'activation', actb.r()[:, :, tok], b4v, AF.Identity, R=[B4], W=[actb[k_] for k_ in range(4)])
                    else:
                        P.op('dve', 'tensor_tensor', actb.r()[:, :, tok], actb.ap()[:, :, tok], b4v, ALU.add,
                             R=[B4] + [actb[k_] for k_ in range(4)], W=[actb[k_] for k_ in range(4)])
                    if STAGE < 8:
                        continue
                    teb = toe.ap().unsqueeze(2).to_broadcast([128, 8, 64])
                    P.op('dve', 'tensor_tensor', xdtw.r().rearrange("p (a b) -> p a b", a=8),
                         xdt.ap().rearrange("p (a b) -> p a b", a=8), teb, ALU.mult, R=[xdt, toe], W=[xdtw])
                    sn = pbank()
                    P.op('pe', 'matmul', sn.ap(), BmT.r(), xdtw.r(), start=True, stop=True, R=[BmT, xdtw], W=[sn])
                    for gr in range(2):
                        ps_ = slice(64 * gr, 64 * gr + 64)
                        fs_ = slice(256 * gr, 256 * gr + 256)
                        cdb = cdd.ap()[ps_, 4 * gr:4 * gr + 4].unsqueeze(2).to_broadcast([64, 4, 64])
                        P.op('dve', 'tensor_tensor', ST[d].r()[ps_, fs_].rearrange("p (a b) -> p a b", a=4),
                             ST[d].ap()[ps_, fs_].rearrange("p (a b) -> p a b", a=4), cdb, ALU.mult,
                             R=[ST[d], cdd], W=[ST[d]])
                        P.op('dve', 'tensor_tensor', ST[d].r()[ps_, fs_], ST[d].ap()[ps_, fs_], sn.ap()[ps_, fs_], ALU.add,
                             R=[ST[d], sn], W=[ST[d]])
                    seg_end = (c % 2 == 1) if d == 0 else (c % 2 == 0)
                    if seg_end:
                        P.dma('sp', st_ssd[l, d, c // 2], ST[d].ap(), R=[ST[d]], dkey='so%d' % d)
                        P.op('dve', 'tensor_scalar', ST[d].r(), ST[d].ap(), cft.ap()[:, 0:1], None, ALU.mult,
                             R=[ST[d], cft], W=[ST[d]])
            if l == 0:
                tap('yssd', actb)
            sv = SB.get(4, 2)
            P.dma('sp', sv.ap(), ssdv[l], W=[sv], dkey='c1')
            wfb = [SR.get(8, 128) for _ in range(2)]
            for kc in range(4):
                P.op('dve', 'scalar_tensor_tensor', actb.r()[:, kc, :], xs[kc].ap(), sv.ap()[:, kc, 1:2], actb.ap()[:, kc, :],
                     ALU.mult, ALU.add, R=[xs[kc], sv, actb[kc]], W=[actb[kc]])
                def evz(half, pp, kc=kc):
                    tm = tmpb[half]
                    P.op('act', 'activation', tm.ap(), pp.ap(), AF.Silu, R=[pp], W=[tm])
                    P.op('dve', 'tensor_tensor', actb.r()[:, kc, hs(half)], actb.ap()[:, kc, hs(half)], tm.ap(), ALU.mult,
                         R=[tm, actb[kc]], W=[actb[kc]])
                proj_fm(l, 50 + kc, evz, wfb)
            nrot[0] = 8
            for half in range(2):
                sl = hs(half)
                rs = rstd[half]
                rms_rstd(actb, half, rs, nch=4)
                for kc in range(4):
                    P.op('dve', 'scalar_tensor_tensor', actb.r()[:, kc, sl], actb.ap()[:, kc, sl], sv.ap()[:, kc, 0:1], rs.ap(),
                         ALU.mult, ALU.mult, R=[actb[kc], sv, rs], W=[actb[kc]])
            SB.release(mkF)
            SR.release(mkR)


        def dn_branch(l, actb, sm, cst):
            mkF, mkR = SB.mark(), SR.mark()
            ident = cst[2]
            gd = sm['gdn']
            pb = pbank()
            for c in range(8):
                for d in range(2):
                    P.op('pe', 'matmul', pb.ap()[:, 64 * d + c * 8:64 * d + c * 8 + 8], cst[d].ap(), gd.ap()[:, c, :],
                         start=True, stop=True, R=[cst[d], gd], W=[pb])
                P.op('pe', 'matmul', pb.ap()[:, 128 + c * 8:128 + c * 8 + 8], ones.ap(), gd.ap()[:, c, :],
                     start=True, stop=True, R=[ones, gd], W=[pb])
            gcs, gto = SB.get(8, 8), SB.get(8, 8)
            for d in range(2):
                P.op('dve', 'tensor_copy', gcs.ap()[:, :, 4 * d:4 * d + 4],
                     pb.ap()[:, 64 * d:64 * d + 64].rearrange("p (a b) -> p a b", a=8)[:, :, 4 * d:4 * d + 4], R=[pb], W=[gcs])
            P.op('dve', 'tensor_copy', gto.ap(), pb.ap()[:, 128:192].rearrange("p (a b) -> p a b", a=8), R=[pb], W=[gto])
            egc, ed, egt, bg, nbe = SB.get(8, 8), SB.get(8, 8), SB.get(8, 8), SB.get(8, 8), SB.get(8, 8)
            P.op('act', 'activation', egc.ap(), gcs.ap(), AF.Exp, R=[gcs], W=[egc])
            P.op('act', 'activation', egt.ap(), gto.ap(), AF.Exp, R=[gto], W=[egt])
            P.op('dve', 'tensor_tensor', ed.ap(), gto.ap(), gcs.ap(), ALU.subtract, R=[gto, gcs], W=[ed])
            P.op('act', 'activation', ed.ap(), ed.ap(), AF.Exp, R=[ed], W=[ed])
            P.op('dve', 'tensor_tensor', bg.ap(), sm['beta'].ap(), egc.ap(), ALU.mult, R=[sm['beta'], egc], W=[bg])
            P.op('dve', 'tensor_scalar', nbe.ap(), sm['beta'].ap(), -1.0, None, ALU.mult, R=[sm['beta']], W=[nbe])
            cw = SB.get(12, 3)
            P.dma('sp', cw.ap(), dncw[l], W=[cw], dkey='c1')
            gv = SB.get(1)
            P.dma('sp', gv.ap(), dnv[l], W=[gv], dkey='c0')
            Xt = [SB.get(128) for _ in range(2)]
            Tt = [SB.get(128) for _ in range(2)]
            DT = [SB.get(128) for _ in range(2)]
            DM = [SB.get(128) for _ in range(2)]
            Eg = [SB.get(128) for _ in range(2)]
            X, XT, TT, Xn = SB.get(2, 128), SB.get(2, 128), SB.get(2, 128), SB.get(2, 128)
            for hd in range(4):
                mkh = SR.mark()
                qkv = SR.get(3, T)
                q, k, v = qkv[0], qkv[1], qkv[2]
                mk2 = SR.mark()
                nrot[0] = 8
                wfb = [SR.get(8, 128) for _ in range(2)]
                msk = SR.get(2, T)
                P.dma('sp', msk.r(), cmask_in, W=[msk], dkey='c0')
                tmps = [SR.get(T) for _ in range(4)]
                for j3 in range(3):
                    conv_chunk(l, 10 + 4 * j3 + hd, cw, 4 * j3 + hd, qkv[j3], msk, tmps, wfb, bias=False)
                SR.release(mk2)
                for j3 in range(2):
                    for half in range(2):
                        sl = hs(half)
                        sq = sqb[half]
                        pbk = pbank()
                        P.op('act', 'activation', sq.r(), qkv[j3].ap()[:, sl], AF.Square, R=[qkv[j3]], W=[sq])
                        P.op('pe', 'matmul', pbk.ap(), onesm.r(), sq.r(), start=True, stop=True, R=[sq, onesm], W=[pbk])
                        rs = rstd[half]
                        P.op('act', 'activation', rs.ap(), pbk.ap(), AF.Sqrt, bias=epst_b.ap(), scale=1.0, R=[pbk, epst_b], W=[rs])
                        P.op('dve', 'reciprocal', rs.ap(), rs.ap(), R=[rs], W=[rs])
                        P.op('dve', 'scalar_tensor_tensor', qkv[j3].r()[:, sl], qkv[j3].ap()[:, sl],
                             (128.0 ** -0.5 if j3 == 0 else 1.0), rs.ap(), ALU.mult, ALU.mult, R=[qkv[j3], rs], W=[qkv[j3]])
                if l == 0 and hd == 0:
                    tap('dnqkv', qkv)
                nrot[0] = 4
                AT, vb, kbg, kd, nwT, vn, qg = [SR.get(2, 128) for _ in range(7)]
                S = SR.get(2, 128)
                for d in range(2):
                    P.dma('sp', S[d].r(), dnh0[l, d, hd], W=[S[d]], dkey='sh%d' % d)
                KB = [banks[4], banks[5]]
                TB = [banks[6], banks[7]]
                for t in range(8 if STAGE > 10 else 0):
                    cs = [t, 7 - t]
                    for d in range(2):
                        c = cs[d]
                        tok = slice(c * 128, (c + 1) * 128)
                        col = slice(4 * d + hd, 4 * d + hd + 1)
                        kb_ = KB[d]
                        tb_ = TB[d]
                        P.op('pe', 'matmul', kb_.ap()[:, 0:128], k.r()[:, tok], k.r()[:, tok], start=True, stop=True, R=[k], W=[kb_])
                        P.op('pe', 'matmul', kb_.ap()[:, 128:256], k.r()[:, tok], q.r()[:, tok], start=True, stop=True, R=[k, q], W=[kb_])
                        P.op('pe', 'transpose', tb_.ap()[:, 0:128], k.ap()[:, tok], ident.ap(), R=[k, ident], W=[tb_])
                        P.op('pe', 'transpose', tb_.ap()[:, 128:256], v.ap()[:, tok], ident.ap(), R=[v, ident], W=[tb_])
                        TRI = cst[d]
                        P.op('dve', 'tensor_scalar', Xt[d].ap(), TRI.ap(), gd.ap()[:, c, col], None, ALU.mult, R=[TRI, gd], W=[Xt[d]])
                        bc = pbank()
                        P.op('pe', 'matmul', bc.ap()[:, 0:128], ones.ap(), Xt[d].ap(), start=True, stop=True, R=[ones, Xt[d]], W=[bc])
                        gcc = gcs.ap()[:, c, col]
                        P.op('dve', 'scalar_tensor_tensor', Tt[d].ap(), bc.ap()[:, 0:128], gcc, cst[3 + d].ap(), ALU.subtract, ALU.add,
                             R=[bc, gcs, cst[3 + d]], W=[Tt[d]])
                        P.op('act', 'activation', DT[d].ap(), Tt[d].ap(), AF.Exp, R=[Tt[d]], W=[DT[d]])
                        P.op('dve', 'scalar_tensor_tensor', Tt[d].ap(), bc.ap()[:, 0:128], -1.0, cst[6 - d].ap(), ALU.mult, ALU.add,
                             R=[bc, cst[6 - d]], W=[Tt[d]])
                        P.op('act', 'activation', DM[d].ap(), Tt[d].ap(), AF.Exp, bias=gcc, R=[Tt[d], gcs], W=[DM[d]])
                        P.op('act', 'activation', Eg[d].ap(), bc.ap()[:, 0:128], AF.Exp, R=[bc], W=[Eg[d]])
                        P.op('dve', 'scalar_tensor_tensor', X.ap()[:, d, :], kb_.ap()[:, 0:128], nbe.ap()[:, c, col], DM[d].ap(),
                             ALU.mult, ALU.mult, R=[kb_, nbe, DM[d]], W=[X[d]])
                        P.op('dve', 'tensor_tensor', AT.r()[:, d, :], kb_.ap()[:, 128:256], DT[d].ap(), ALU.mult, R=[kb_, DT[d]], W=[AT[d]])
                        P.op('dve', 'tensor_tensor', qg.r()[:, d, :], q.ap()[:, tok], Eg[d].ap(), ALU.mult, R=[q, Eg[d]], W=[qg[d]])
                        P.op('act', 'activation', kbg.r()[:, d, :], tb_.ap()[:, 0:128], AF.Identity, scale=bg.ap()[:, c, col],
                             R=[tb_, bg], W=[kbg[d]])
                        P.op('act', 'activation', kd.r()[:, d, :], tb_.ap()[:, 0:128], AF.Identity, scale=ed.ap()[:, c, col],
                             R=[tb_, ed], W=[kd[d]])
                        P.op('act', 'activation', vb.r()[:, d, :], tb_.ap()[:, 128:256], AF.Identity, scale=sm['beta'].ap()[:, c, col],
                             R=[tb_, sm['beta']], W=[vb[d]])
                    if STAGE < 12:
                        continue
                    pt_ = TB[0]
                    for d in range(2):
                        P.op('pe', 'transpose', pt_.ap()[:, 256 + 128 * d:256 + 128 * d + 128], X.ap()[:, d, :], ident.ap(), R=[X[d], ident], W=[pt_])
                    if STAGE == 12:
                        continue
                    for d in range(2):
                        pc = pt_.ap()[:, 256 + 128 * d:256 + 128 * d + 128]
                        P.op('act', 'activation', XT.ap()[:, d, :], pc, AF.Identity, R=[pt_], W=[XT[d]])
                        P.op('dve', 'tensor_tensor', TT.ap()[:, d, :], XT.ap()[:, d, :], ident.ap(), ALU.add, R=[XT[d], ident], W=[TT[d]])
                    Xc, Xo = X, Xn
                    if STAGE < 13 or STAGE > 100:
                        continue
                    for kk in range(1, 7):
                        p1 = pbank()
                        for d in range(2):
                            P.op('pe', 'matmul', p1.ap()[:, 128 * d:128 * d + 128], XT.ap()[:, d, :], Xc.ap()[:, d, :], start=True, stop=True,
                                 R=[XT[d], Xc[d]], W=[p1])
                        if kk < 6:
                            p2 = pbank()
                            for d in range(2):
                                P.op('pe', 'matmul', p2.ap()[:, 128 * d:128 * d + 128], Xc.ap()[:, d, :], XT.ap()[:, d, :], start=True, stop=True,
                                     R=[XT[d], Xc[d]], W=[p2])
                        P.op('act', 'activation', Xo.ap(), p1.ap()[:, 0:256].rearrange("p (a b) -> p a b", a=2), AF.Identity, R=[p1], W=[Xo])
                        if kk < 6:
                            P.op('dve', 'tensor_copy', XT.ap(), p2.ap()[:, 0:256].rearrange("p (a b) -> p a b", a=2), R=[p2], W=[XT])
                        Xc, Xo = Xo, Xc
                        p3 = pbank()
                        for d in range(2):
                            P.op('pe', 'matmul', p3.ap()[:, 128 * d:128 * d + 128], Xc.ap()[:, d, :], TT.ap()[:, d, :], start=True, stop=True,
                                 R=[Xc[d], TT[d]], W=[p3])
                        P.op('dve', 'tensor_tensor', TT.ap(), TT.ap(), p3.ap()[:, 0:256].rearrange("p (a b) -> p a b", a=2), ALU.add,
                             R=[TT, p3], W=[TT])
                    if STAGE < 14:
                        continue
                    for d in range(2):
                        c = cs[d]
                        tok = slice(c * 128, (c + 1) * 128)
                        col = slice(4 * d + hd, 4 * d + hd + 1)
                        pw = pbank()
                        P.op('pe', 'matmul', pw.ap()[:, 0:128], kbg.ap()[:, d, :], TT.ap()[:, d, :], start=True, stop=True, R=[kbg[d], TT[d]], W=[pw])
                        P.op('act', 'activation', nwT.r()[:, d, :], pw.ap()[:, 0:128], AF.Identity, scale=-1.0, R=[pw], W=[nwT[d]])
                        if STAGE < 15:
                            continue
                        pv = pbank()
                        P.op('pe', 'matmul', pv.ap()[:, 0:128], TT.ap()[:, d, :], vb.ap()[:, d, :], start=True, stop=False, R=[TT[d], vb[d]], W=[pv])
                        P.op('pe', 'matmul', pv.ap()[:, 0:128], nwT.ap()[:, d, :], S.ap()[:, d, :], start=False, stop=True, R=[nwT[d], S[d]], W=[pv])
                        P.op('act', 'activation', vn.r()[:, d, :], pv.ap()[:, 0:128], AF.Identity, R=[pv], W=[vn[d]])
                        if STAGE < 16:
                            continue
                        po = pbank()
                        P.op('pe', 'matmul', po.ap()[:, 0:128], S.r()[:, d, :], qg.r()[:, d, :], start=True, stop=False, R=[S[d], qg[d]], W=[po])
                        P.op('pe', 'matmul', po.ap()[:, 0:128], vn.r()[:, d, :], AT.r()[:, d, :], start=False, stop=True, R=[vn[d], AT[d]], W=[po])
                        first = (d == 0 and t < 4) or (d == 1 and t <= 3)
                        first = (t < 7 - t) if d == 0 else (t < 7 - t)
                        if first:
                            P.op('act', 'activation', actb.r()[:, hd, tok], po.ap()[:, 0:128], AF.Identity, R=[po], W=[actb[hd]])
                        else:
                            P.op('dve', 'tensor_tensor', actb.r()[:, hd, tok], actb.ap()[:, hd, tok], po.ap()[:, 0:128], ALU.add,
                                 R=[po, actb[hd]], W=[actb[hd]])
                        if STAGE < 17:
                            continue
                        ps_ = pbank()
                        P.op('pe', 'matmul', ps_.ap()[:, 0:128], kd.r()[:, d, :], vn.r()[:, d, :], start=True, stop=True, R=[kd[d], vn[d]], W=[ps_])
                        P.op('dve', 'scalar_tensor_tensor', S.r()[:, d, :], S.ap()[:, d, :], egt.ap()[:, c, col], ps_.ap()[:, 0:128],
                             ALU.mult, ALU.add, R=[S[d], egt, ps_], W=[S[d]])
                        seg_end = (c % 2 == 1) if d == 0 else (c % 2 == 0)
                        if seg_end and STAGE > 17:
                            P.dma('sp', st_dn[l, d, c // 2, hd], S.ap()[:, d, :], R=[S[d]], dkey='sd%d' % d)
                            P.op('dve', 'tensor_scalar', S.r()[:, d, :], S.ap()[:, d, :], cft.ap()[:, 0:1], None, ALU.mult,
                                 R=[S[d], cft], W=[S[d]])
                SR.release(mkh)
            nrot[0] = 8
            if l == 0:
                tap('dno', actb)
            wfb = [SR.get(8, 128) for _ in range(2)]
            for hd in range(4):
                for half in range(2):
                    sl = hs(half)
                    sq = sqb[half]
                    pbk = pbank()
                    P.op('act', 'activation', sq.r(), actb.ap()[:, hd, sl], AF.Square, R=[actb[hd]], W=[sq])
                    P.op('pe', 'matmul', pbk.ap(), onesm.r(), sq.r(), start=True, stop=True, R=[sq, onesm], W=[pbk])
                    rs = rstd[half]
                    P.op('act', 'activation', rs.ap(), pbk.ap(), AF.Sqrt, bias=epst_b.ap(), scale=1.0 / 128, R=[pbk, epst_b], W=[rs])
                    P.op('dve', 'reciprocal', rs.ap(), rs.ap(), R=[rs], W=[rs])
                    P.op('dve', 'scalar_tensor_tensor', actb.r()[:, hd, sl], actb.ap()[:, hd, sl], gv.ap()[:, 0:1], rs.ap(),
                         ALU.mult, ALU.mult, R=[actb[hd], gv, rs], W=[actb[hd]])
                def evg(half, pp, hd=hd):
                    tm = tmpb[half]
                    P.op('act', 'activation', tm.ap(), pp.ap(), AF.Silu, R=[pp], W=[tm])
                    P.op('dve', 'tensor_tensor', actb.r()[:, hd, hs(half)], actb.ap()[:, hd, hs(half)], tm.ap(), ALU.mult,
                         R=[tm, actb[hd]], W=[actb[hd]])
                proj_fm(l, 22 + hd, evg, wfb)
            SB.release(mkF)
            SR.release(mkR)

        def mixer(l):
            norm_mod(l, 1, h)
            mk0 = SR.mark()
            actb = SR.get(4, T)
            s5_branch(l, actb)
            mg = SR.get(NFC, T)
            finalize(l, 0, actb, True, mg)
            if l == 0:
                tap('mg', mg)
            mkS = SB.mark()
            cst = load_consts()
            sm = small_params(l)
            ssd_branch(l, actb, sm, cst)
            if l == 0:
                tap('actb_ssd', actb)
            finalize(l, 1, actb, False, mg)
            if l == 0:
                tap('mg2', mg)
            dn_branch(l, actb, sm, cst)
            finalize(l, 2, actb, False, mg)
            if l == 0:
                tap('mg3', mg)
            SB.release(mkS)
            wfb = [SR.get(8, 128) for _ in range(2)]
            for fc in range(NFC):
                wb = wfb[fc % 2]
                P.dma('sp', wb.r(), wout[l, fc], W=[wb], dkey='wf%d' % (fc % 2))
                for half in range(2):
                    sl = hs(half)
                    po = pbank()
                    for kc in range(8):
                        P.op('pe', 'matmul', po.ap(), wb.r()[:, kc, :], mg.r()[:, kc, sl], start=(kc == 0), stop=(kc == 7),
                             R=[wb, mg[kc]], W=[po])
                    P.op('dve', 'scalar_tensor_tensor', x.ap()[:, fc, sl], po.ap(), Gvec.ap()[:, l, 8 + fc:8 + fc + 1],
                         x.ap()[:, fc, sl], ALU.mult, ALU.add, R=[po, Gvec[l], x[fc]], W=[x[fc]])
            SR.release(mk0)

        SBt = [SB.get(1) for _ in range(2)]
        tmp2k = Tile(arena, 'sb', tmpb[0].off, (2048,))
        assert rstd[1].off == tmpb[0].off + 1536

        for l in range(NLAYERS_RUN):
            ffn(l, 0)
            if l == 0:
                tap('x1', x)
            if MIXER:
                mixer(l)
            ffn(l, 1)

        for half in range(2):
            sl = hs(half)
            rs = rstd[half]
            rms_rstd(x, half, rs)
            for fc in range(NFC):
                P.op('dve', 'scalar_tensor_tensor', x.ap()[:, fc, sl], x.ap()[:, fc, sl], fg.ap()[:, fc:fc + 1],
                     rs.ap(), ALU.mult, ALU.mult, R=[x[fc], rs, fg], W=[x[fc]])
        for fc in range(NFC):
            P.dma('sp', y_out[fc], x[fc].ap(), R=[x[fc]], dkey='yo%d' % (fc % 2))
        if True:
            for a in range(2 if not MIXER else 0):
                for l in range(L):
                    for d in range(2):
                        P.dma('sp', st_s5[a, l, d].rearrange("p a b -> p (a b)"), zer.ap()[:, 0:64], R=[zer], dkey='z0')
            for l in range(L):
                for d in range(2):
                    for s in range(4):
                        if not MIXER:
                            P.dma('sp', st_ssd[l, d, s], zer.ap(), R=[zer], dkey='z1')
                            pass
        P.final_wait('sp')
        P.emit()
        print("SBUF peak floats", SB.peak, SR.peak, "instr counts", P.cnt, flush=True)
    return nc


def _prep_shared(inp):
    f = np.float32
    sh = {}
    aw = inp['ada_w'].reshape(L, 8, 128, 36, 256)
    sh['adaw'] = np.ascontiguousarray(aw.transpose(0, 3, 2, 1, 4)).astype(f)
    sh['adab'] = np.ascontiguousarray(inp['ada_b'].reshape(L, 72, 128).transpose(0, 2, 1)).astype(f)
    sh['normg'] = np.ascontiguousarray(inp['norm_g'].reshape(L, 3, NFC, 128).transpose(3, 0, 1, 2)).astype(f)
    sh['fng'] = np.ascontiguousarray(inp['final_norm_g'].reshape(NFC, 128).T).astype(f)
    w = inp['ffn_wi'].reshape(L, 2, 8, 128, 2, NJ, 128)
    sh['wi'] = np.ascontiguousarray(w.transpose(0, 1, 5, 3, 2, 4, 6)).reshape(L, 2, NJ, 128, 8, 256).astype(f)
    w = inp['ffn_wo'].reshape(L, 2, NJ, 128, NFC, 128)
    sh['wo'] = np.ascontiguousarray(w.transpose(0, 1, 4, 3, 2, 5)).reshape(L, 2, NFC, 128, NJ * 128).astype(f)
    win = inp['w_in']
    cols = np.concatenate([np.arange(0, 512), np.arange(1024, 1792), np.arange(1808, 3344),
                           np.arange(3360, 3872), np.arange(3872, 6944), np.arange(512, 1024)])
    wsel = win[:, :, cols].reshape(L, 8, 128, 54, 128)
    scol = np.concatenate([np.arange(1792, 1808), np.arange(3344, 3360)])
    sh['wsm'] = np.ascontiguousarray(win[:, :, scol].reshape(L, 8, 128, 32).transpose(0, 2, 1, 3)).astype(f)
    bias = np.concatenate([inp['ssd_dt_bias'].reshape(L, 16), np.zeros((L, 8), f), inp['dn_dt_bias'].reshape(L, 8)], axis=1)
    sh['smb'] = np.ascontiguousarray(np.broadcast_to(bias[:, None, :], (L, 128, 32))).astype(f)
    alog = np.concatenate([inp['ssd_a_log'].reshape(L, 16), inp['dn_a_log'].reshape(L, 8)], axis=1)
    sh['sma'] = np.ascontiguousarray(np.broadcast_to(alog[:, None, :], (L, 128, 24))).astype(f)
    cw = np.concatenate([inp['ssd_conv_w'], inp['ssd_conv_b'][:, None, :]], axis=1)
    sh['ssdcw'] = np.ascontiguousarray(cw.reshape(L, 4, 6, 128).transpose(0, 3, 2, 1)).astype(f)
    dch = np.repeat(inp['ssd_d'], 64, axis=1)
    sv = np.stack([inp['ssd_norm_g'], dch], axis=-1)
    sh['ssdv'] = np.ascontiguousarray(sv.reshape(L, 4, 128, 2).transpose(0, 2, 1, 3)).astype(f)
    sh['dncw'] = np.ascontiguousarray(inp['dn_conv_w'].reshape(L, 3, 12, 128).transpose(0, 3, 2, 1)).astype(f)
    sh['dnv'] = np.ascontiguousarray(inp['dn_norm_g'].reshape(L, 128, 1)).astype(f)
    jj, ii = np.meshgrid(np.arange(128), np.arange(128), indexing='ij')
    NEG = -32768.0
    cst = np.stack([(jj <= ii), (jj >= ii), (jj == ii),
                    np.where(ii >= jj, 0.0, NEG), np.where(ii <= jj, 0.0, NEG),
                    np.where(ii > jj, 0.0, NEG), np.where(ii < jj, 0.0, NEG)], axis=1).astype(f)
    sh['consts'] = np.ascontiguousarray(cst)
    sh['wfm'] = np.ascontiguousarray(wsel.transpose(0, 3, 2, 1, 4)).astype(f)
    br = np.stack([inp['s5_glu'][:, 0], inp['s5_glu'][:, 1], inp['ssd_w_out'], inp['dn_w_out']], axis=1)
    br = br.reshape(L, 4, 4, 128, NFC, 128)
    sh['wbr'] = np.ascontiguousarray(br.transpose(0, 1, 4, 3, 2, 5)).astype(f)
    wo_ = inp['w_out'].reshape(L, 8, 128, NFC, 128)
    sh['wout'] = np.ascontiguousarray(wo_.transpose(0, 3, 2, 1, 4)).astype(f)
    ldt = np.broadcast_to(inp['s5_log_dt'][..., None], (L, 2, 32, 64))
    p3 = np.stack([inp['s5_lam_re'], inp['s5_lam_im'], ldt], axis=2).reshape(L, 2, 3, 16, 2, 64)
    sh['s5p'] = np.ascontiguousarray(p3.transpose(0, 1, 4, 5, 2, 3)).reshape(L, 2, 128, 3, 16).astype(f)
    sb_ = np.zeros((L, 2, 2, 128, 4, 128), f)
    scc = np.zeros((L, 2, 2, 128, 16, 128), f)
    for ri, (bb, cc) in enumerate(((inp['s5_b_re'], inp['s5_c_re']), (inp['s5_b_im'], inp['s5_c_im']))):
        for g in range(32):
            sb_[:, :, ri, 16 * (g % 8):16 * (g % 8) + 16, g // 8, 64 * (g % 2):64 * (g % 2) + 64] = \
                bb[:, :, g].transpose(0, 1, 3, 2)
            scc[:, :, ri, 64 * (g % 2):64 * (g % 2) + 64, g // 2, 16 * (g % 8):16 * (g % 8) + 16] = \
                cc[:, :, g].transpose(0, 1, 3, 2)
    sh['s5b'] = sb_.reshape(L, 2, 2, 128, 512)
    sh['s5c'] = scc
    sh['s5d'] = np.ascontiguousarray(inp['s5_d'].reshape(L, 4, 128).transpose(0, 2, 1)).astype(f)
    return sh


def _core_inputs(inp, core):
    f = np.float32
    d = {}
    if core < 4:
        xs = inp['x_prompt'][core * 4:(core + 1) * 4].reshape(T, D)
        cond = inp['c_ctx']
    elif core < 6:
        xs = inp['x_sample'][core - 4]
        cond = inp['c'][core - 4]
    else:
        return None
    W_ = 256 if core < 4 else 64
    tt = np.arange(T)
    m0 = np.ones(T, f); m0[1:] = (tt[1:] % W_ != 0)
    m1 = (tt % W_ != 0).astype(f)
    d['cmask'] = np.ascontiguousarray(np.broadcast_to(np.stack([m0, m1])[None], (128, 2, T))).astype(f)
    h0s = np.zeros((L, 2, 128, 512), f)
    if core >= 4:
        st = inp['state_ssd'][core - 4]
        for hh in range(8):
            gr = hh // 4
            h0s[:, :, 64 * gr:64 * gr + 64, 64 * hh:64 * hh + 64] = st[:, :, hh].transpose(0, 1, 3, 2)
    d['ssdh0'] = h0s
    if core >= 4:
        d['dnh0'] = np.ascontiguousarray(inp['state_dn'][core - 4]).astype(f)
    else:
        d['dnh0'] = np.zeros((L, 2, 4, 128, 128), f)
    if core < 4:
        d['s5h0'] = np.zeros((L, 2, 128, 2, 16), f)
        d['cf'] = np.zeros((128, 1), f)
    else:
        b = core - 4
        hh = np.stack([inp['state_s5_re'][b], inp['state_s5_im'][b]], axis=2)
        hh = hh.reshape(L, 2, 2, 16, 2, 64).transpose(0, 1, 4, 5, 2, 3)
        d['s5h0'] = np.ascontiguousarray(hh).reshape(L, 2, 128, 2, 16).astype(f)
        d['cf'] = np.ones((128, 1), f)
    d['x_t'] = np.ascontiguousarray(xs.T.reshape(NFC, 128, T)).astype(f)
    c2 = cond.reshape(NFC, 128).T
    d['cond'] = np.ascontiguousarray(np.stack([c2, c2], axis=-1)).astype(f)
    return d


_NC_CACHE = {}


def kernel(**inputs):
    inp = {k: np.asarray(v) for k, v in inputs.items()}
    if 'nc' not in _NC_CACHE:
        _NC_CACHE['nc'] = build_program()
    nc = _NC_CACHE['nc']
    shared = _prep_shared(inp)
    in_maps = []
    for core in range(N_CORES):
        d = _core_inputs(inp, core)
        if d is None:
            d = in_maps[0]
            in_maps.append(d)
            continue
        d.update(shared)
        in_maps.append(d)
    res = run_bass_kernel_spmd(nc, in_maps, core_ids=list(range(N_CORES)))
    R = res.results
    B, S = 16, 256
    y_prompt = np.zeros((B, S, D), np.float32)
    for c in range(4):
        y_prompt[c * 4:(c + 1) * 4] = R[c]['y_t'].reshape(D, T).T.reshape(4, S, D)
    y_sample = np.stack([R[4 + b]['y_t'].reshape(D, T).T for b in range(2)]).astype(np.float32)
    s5re = np.zeros((B, L, 2, 32, 64), np.float32)
    s5im = np.zeros((B, L, 2, 32, 64), np.float32)
    ssd = np.zeros((B, L, 2, 8, 64, 64), np.float32)
    dn = np.zeros((B, L, 2, 4, 128, 128), np.float32)
    for c in range(4):
        s5 = R[c]['st_s5'].reshape(2, L, 2, 2, 64, 16, 4)
        s5 = s5.transpose(0, 6, 1, 2, 5, 3, 4).reshape(2, 4, L, 2, 32, 64)
        s5re[c * 4:(c + 1) * 4] = s5[0]
        s5im[c * 4:(c + 1) * 4] = s5[1]
        sd = R[c]['st_ssd'].reshape(L, 2, 4, 2, 64, 8, 64)
        for hh in range(8):
            ssd[c * 4:(c + 1) * 4, :, :, hh] = sd[:, :, :, hh // 4, :, hh, :].transpose(2, 0, 1, 4, 3)
        dd = R[c]['st_dn']
        dn[c * 4:(c + 1) * 4] = dd.transpose(2, 0, 1, 3, 4, 5)
    return (y_prompt, y_sample, s5re, s5im, ssd, dn)
```

```python
import numpy as np
from contextlib import ExitStack
import concourse.bass as bass
import concourse.mybir as mybir
from concourse.bass_utils import run_bass_kernel_spmd

F32 = mybir.dt.float32
F32R = mybir.dt.float32r
ALU = mybir.AluOpType
AF = mybir.ActivationFunctionType

D = 1024
T = 1024
L = 4
NFC = 8
DFF = 2816
NJ = 22
HALF = 512
EPS = 1e-6
N_CORES = 8
MIXER = True
SAME_ENGINE_RAW_ONLY = True
EMBED_WAIT = True
DEBUG = False
import os
STAGE = int(os.environ.get('KSTAGE', '99'))
NLAYERS_RUN = L if False else 4
NF_SB = 15600
NR_SB = 37600


class Tile:
    def __init__(self, arena, space, off, shape):
        self.arena, self.space, self.off, self.shape = arena, space, off, tuple(shape)
        n = 1
        for s in shape:
            n *= s
        self.n = n

    def ap(self, dt=None):
        a = self.arena[:, self.off:self.off + self.n]
        if len(self.shape) == 2:
            a = a.rearrange("p (a b) -> p a b", b=self.shape[1])
        elif len(self.shape) == 3:
            a = a.rearrange("p (a b c) -> p a b c", b=self.shape[1], c=self.shape[2])
        if dt is not None:
            a = a.bitcast(dt)
        return a

    def r(self):
        return self.ap(F32R)

    def __getitem__(self, i):
        sub = self.shape[1:]
        n = self.n // self.shape[0]
        return Tile(self.arena, self.space, self.off + i * n, sub if sub else (n,))

    def cells(self):
        G = 32
        return [(self.space, c) for c in range(self.off // G, (self.off + self.n - 1) // G + 1)]


class Alloc:
    def __init__(self, arena, space, size):
        self.arena, self.space, self.size, self.top = arena, space, size, 0
        self.peak = 0

    def get(self, *shape):
        n = 1
        for s in shape:
            n *= s
        n2 = (n + 31) // 32 * 32
        t = Tile(self.arena, self.space, self.top, shape)
        self.top += n2
        self.peak = max(self.peak, self.top)
        assert self.top <= self.size, (self.space, self.top, self.size)
        return t

    def mark(self):
        return self.top

    def release(self, m):
        self.top = m


class Prog:
    ENG = ('pe', 'act', 'dve', 'pool', 'sp')

    def __init__(self, nc, same_engine_sync=True):
        self.nc = nc
        self.ops = {e: [] for e in self.ENG}
        self.cnt = {e: 0 for e in self.ENG}
        self.dcnt = {}
        self.last_w = {}
        self.last_r = {}
        self.seen = {e: {} for e in self.ENG}
        self.ses = same_engine_sync
        self.raw_only = SAME_ENGINE_RAW_ONLY

    @staticmethod
    def _cells(items):
        out = []
        for it in items:
            if isinstance(it, Tile):
                out.extend(it.cells())
            else:
                out.append(it)
        return out

    def _deps(self, eng, reads, writes, is_dma):
        deps = {}
        same_raw = 0

        def add(s, n):
            if deps.get(s, 0) < n:
                deps[s] = n
        for r in reads:
            ev = self.last_w.get(r)
            if ev:
                add(*ev)
                if ev[0] == eng and ev[1] > same_raw:
                    same_raw = ev[1]
        for w in writes:
            ev = self.last_w.get(w)
            if ev:
                add(*ev)
            for s, n in self.last_r.get(w, {}).items():
                add(s, n)
        waits = []
        for s, n in deps.items():
            if s == eng and not is_dma:
                if eng == 'pe' or not self.ses:
                    continue
                if self.raw_only:
                    n = same_raw
                    if n == 0:
                        continue
            if self.seen[eng].get(s, 0) >= n:
                continue
            self.seen[eng][s] = n
            waits.append((s, n))
        return waits

    def _commit(self, ev, reads, writes):
        s, n = ev
        for r in reads:
            d = self.last_r.setdefault(r, {})
            if d.get(s, 0) < n:
                d[s] = n
        for w in writes:
            self.last_w[w] = ev
            self.last_r[w] = {}

    def op(self, eng, fn, *args, R=(), W=(), **kw):
        if isinstance(fn, str):
            name = fn
            fn = (lambda e, name=name, args=args, kw=kw: getattr(e, name)(*args, **kw))
        reads, writes = self._cells(R), self._cells(W)
        waits = self._deps(eng, reads, writes, False)
        self.cnt[eng] += 1
        ev = (eng, self.cnt[eng])
        self.ops[eng].append((waits, fn, ev))
        self._commit(ev, reads, writes)

    def dma(self, eng, out, in_, R=(), W=(), dkey=None, **kw):
        reads, writes = self._cells(R), self._cells(W)
        dk = 'd:' + dkey
        n = self.dcnt.get(dk, 0)
        waits = self._deps(eng, reads, writes, True)
        if n > 0 and self.seen[eng].get(dk, 0) < n:
            self.seen[eng][dk] = n
            waits.append((dk, n))
        self.dcnt[dk] = n + 16
        ev = (dk, n + 16)
        self.ops[eng].append((waits, (lambda e: e.dma_start(out=out, in_=in_, **kw)), ev))
        self._commit(ev, reads, writes)

    def final_wait(self, eng='sp'):
        waits = [(dk, n) for dk, n in self.dcnt.items()]
        for e in self.ENG:
            if e != eng and self.cnt[e] > 0:
                waits.append((e, self.cnt[e]))
        self.ops[eng].append((waits, None, None))

    def emit(self):
        nc = self.nc
        names = list(self.ENG) + sorted(self.dcnt.keys())
        with ExitStack() as st:
            sems = {nm: st.enter_context(nc.semaphore('s_' + nm.replace(':', '_'))) for nm in names}
            block = st.enter_context(nc.Block())

            def run(engname, e):
                for waits, fn, ev in self.ops[engname]:
                    emb = None
                    if EMBED_WAIT and fn is not None and waits and not ev[0].startswith('d:'):
                        emb = waits[-1]
                        waits = waits[:-1]
                    for s, n in waits:
                        e.wait_ge(sems[s], n)
                    if fn is None:
                        continue
                    ins = fn(e)
                    if emb is not None:
                        ins._wait_ge(sems[emb[0]], emb[1])
                    ins.then_inc(sems[ev[0]], 16 if ev[0].startswith('d:') else 1)

            @block.tensor
            def _(e):
                run('pe', e)

            @block.scalar
            def _(e):
                run('act', e)

            @block.vector
            def _(e):
                run('dve', e)

            @block.gpsimd
            def _(e):
                run('pool', e)

            @block.sync
            def _(e):
                run('sp', e)


def build_program():
    nc = bass.Bass("TRN2", target_bir_lowering=False)
    nc.dge_precook = False

    def din(name, shape, dt=F32):
        return nc.dram_tensor(name, list(shape), dt, kind="ExternalInput").ap()

    def dout(name, shape):
        return nc.dram_tensor(name, list(shape), F32, kind="ExternalOutput").ap()

    x_in = din("x_t", [NFC, 128, T])
    cond_in = din("cond", [128, NFC, 2])
    adaw = din("adaw", [L, 36, 128, 8, 256], F32R)
    adab = din("adab", [L, 128, 72])
    normg = din("normg", [128, L, 3, NFC])
    fng = din("fng", [128, NFC])
    wi = din("wi", [L, 2, NJ, 128, 8, 256], F32R)
    wo = din("wo", [L, 2, NFC, 128, NJ * 128], F32R)
    wsm = din("wsm", [L, 128, 8, 32], F32R)
    smb = din("smb", [L, 128, 32])
    sma = din("sma", [L, 128, 24])
    ssdcw = din("ssdcw", [L, 128, 6, 4])
    ssdv = din("ssdv", [L, 128, 4, 2])
    consts_in = din("consts", [128, 7, 128], F32R)
    cmask_in = din("cmask", [128, 2, T], F32R)
    ssdh0 = din("ssdh0", [L, 2, 128, 512], F32R)
    dncw = din("dncw", [L, 128, 12, 3])
    dnv = din("dnv", [L, 128, 1])
    dnh0 = din("dnh0", [L, 2, 4, 128, 128], F32R)
    wfm = din("wfm", [L, 54, 128, 8, 128], F32R)
    wbr = din("wbr", [L, 4, NFC, 128, 4, 128], F32R)
    wout = din("wout", [L, NFC, 128, 8, 128], F32R)
    s5p = din("s5p", [L, 2, 128, 3, 16])
    s5b = din("s5b", [L, 2, 2, 128, 512], F32R)
    s5c = din("s5c", [L, 2, 2, 128, 16, 128])
    s5d = din("s5d", [L, 128, 4])
    s5h0 = din("s5h0", [L, 2, 128, 2, 16])
    cf_in = din("cf", [128, 1])
    y_out = dout("y_t", [NFC, 128, T])
    st_s5 = dout("st_s5", [2, L, 2, 128, 16, 4])
    st_ssd = dout("st_ssd", [L, 2, 4, 128, 512])
    st_dn = dout("st_dn", [L, 2, 4, 4, 128, 128])

    with ExitStack() as st:
        arena = st.enter_context(nc.sbuf_tensor("arena", [128, NF_SB], F32))
        arena_r = st.enter_context(nc.sbuf_tensor("arena_r", [128, NR_SB], F32))
        psum = st.enter_context(nc.psum_tensor("psum", [128, 8 * 512], F32))
        SB = Alloc(arena, 'sb', NF_SB)
        SR = Alloc(arena_r, 'sr', NR_SB)
        PSA = Alloc(psum, 'ps', 4096)
        banks = [PSA.get(512) for _ in range(8)]
        P = Prog(nc)
        bank_i = [0]

        nrot = [8]
        pin_i = [0]

        def pbank():
            b = banks[bank_i[0] % nrot[0]]
            bank_i[0] += 1
            return b

        def pinbank():
            b = banks[6 + pin_i[0] % 2]
            pin_i[0] += 1
            return b

        def tap(name, tile, n=None):
            if not DEBUG:
                return
            n = n or tile.n
            dd = nc.dram_tensor("dbg_" + name, [128, n], F32, kind="ExternalOutput").ap()
            P.dma('sp', dd, tile.arena[:, tile.off:tile.off + n], R=[tile], dkey='dbg')

        ones = SB.get(128)
        P.op('pool', 'memset', ones.ap(), 1.0, W=[ones])
        onesm = SR.get(128)
        P.op('dve', 'tensor_copy', onesm.r(), ones.ap(), R=[ones], W=[onesm])
        epst_b = SB.get(1)
        P.op('pool', 'memset', epst_b.ap(), EPS, W=[epst_b])
        zer = SB.get(512 if not MIXER else 1)
        P.op('pool', 'memset', zer.ap(), 0.0, W=[zer])

        x = SB.get(NFC, T)
        for fc in range(NFC):
            P.dma('sp', x[fc].ap(), x_in[fc], W=[x[fc]], dkey='xin%d' % (fc % 2))
        modT = SB.get(L, 72)
        ng = SB.get(L * 3 * NFC)
        P.dma('sp', ng.ap(), normg.rearrange("p l i f -> p (l i f)"), W=[ng], dkey='c0')
        fg = SB.get(NFC)
        P.dma('sp', fg.ap(), fng, W=[fg], dkey='c1')
        ab = SB.get(L, 72)
        for l in range(L):
            P.dma('sp', ab[l].ap(), adab[l], W=[ab[l]], dkey='c0')

        m0 = SB.mark()
        mr0 = SR.mark()
        cnd = SB.get(NFC, 2)
        P.dma('sp', cnd.ap(), cond_in, W=[cnd], dkey='c1')
        sc = SR.get(NFC, 2)
        P.op('act', 'activation', sc.r(), cnd.ap(), AF.Silu, R=[cnd], W=[sc])
        NB = 3
        abuf = [SR.get(8, 256) for _ in range(NB)]
        it = 0
        for l in range(L):
            for mp in range(36):
                wb = abuf[it % NB]
                it += 1
                P.dma('sp', wb.r(), adaw[l, mp], W=[wb], dkey='aw%d' % (it % NB))
                pb = pbank()
                for s in range(2):
                    for kc in range(8):
                        P.op('pe', 'matmul', pb.ap()[:, 2 * s:2 * s + 2], wb.r()[:, kc, s * 128:(s + 1) * 128],
                             sc.r()[:, kc, :], start=(kc == 0), stop=(kc == 7), R=[wb, sc], W=[pb])
                col = mp * 2
                P.op('dve', 'tensor_tensor', modT.ap()[:, l, col:col + 2], pb.ap()[:, 0:4:2],
                     ab.ap()[:, l, col:col + 2], ALU.add, R=[pb, ab[l]], W=[modT[l]])
        SB.release(m0)
        SR.release(mr0)

        Avec = SB.get(L, 3 * NFC)
        Gvec = SB.get(L, 3 * NFC)
        for l in range(L):
            for i in range(3):
                P.op('dve', 'scalar_tensor_tensor', Avec.ap()[:, l, i * 8:(i + 1) * 8],
                     modT.ap()[:, l, (3 * i + 1) * 8:(3 * i + 2) * 8], 1.0,
                     ng.ap()[:, (l * 3 + i) * 8:(l * 3 + i + 1) * 8], ALU.add, ALU.mult,
                     R=[modT[l], ng], W=[Avec[l]])
                P.op('dve', 'tensor_scalar', Gvec.ap()[:, l, i * 8:(i + 1) * 8],
                     modT.ap()[:, l, (3 * i + 2) * 8:(3 * i + 3) * 8], (1.0 if i == 1 else 0.5), None, ALU.mult,
                     R=[modT[l]], W=[Gvec[l]])
        tap('modT', modT)
        tap('Avec', Avec)
        tap('Gvec', Gvec)

        sqb = [SR.get(HALF) for _ in range(2)]
        tmpb = [SB.get(HALF) for _ in range(2)]
        rstd = [SB.get(HALF) for _ in range(2)]
        h = SR.get(NFC, T)
        sqi = [0]

        def hs(half):
            return slice(half * HALF, (half + 1) * HALF)

        def rms_rstd(src, half, out_rstd, nch=NFC):
            pb = pbank()
            sl = hs(half)
            for fc in range(nch):
                sq = sqb[sqi[0] % 2]
                sqi[0] += 1
                P.op('act', 'activation', sq.r(), src.ap()[:, fc, sl], AF.Square, R=[src[fc]], W=[sq])
                P.op('pe', 'matmul', pb.ap(), onesm.r(), sq.r(), start=(fc == 0), stop=(fc == nch - 1),
                     R=[sq, onesm], W=[pb])
            P.op('act', 'activation', out_rstd.ap(), pb.ap(), AF.Sqrt, bias=epst_b.ap(), scale=1.0 / (128 * nch),
                 R=[pb, epst_b], W=[out_rstd])
            P.op('dve', 'reciprocal', out_rstd.ap(), out_rstd.ap(), R=[out_rstd], W=[out_rstd])

        def norm_mod(l, i, hdst):
            for half in range(2):
                sl = hs(half)
                rs = rstd[half]
                rms_rstd(x, half, rs)
                for fc in range(NFC):
                    tm = tmpb[fc % 2]
                    P.op('dve', 'tensor_tensor', tm.ap(), x.ap()[:, fc, sl], rs.ap(), ALU.mult,
                         R=[x[fc], rs], W=[tm])
                    P.op('act', 'activation', hdst.r()[:, fc, sl], tm.ap(), AF.Identity,
                         scale=Avec.ap()[:, l, i * 8 + fc:i * 8 + fc + 1],
                         bias=modT.ap()[:, l, 3 * i * 8 + fc:3 * i * 8 + fc + 1],
                         R=[tm, Avec[l], modT[l]], W=[hdst[fc]])

        def ffn(l, f):
            i = 0 if f == 0 else 2
            norm_mod(l, i, h)
            if l == 0 and f == 0:
                tap('h0', h)
            m1 = SR.mark()
            act = SR.get(NJ, T)
            wbufs = [SR.get(8, 256) for _ in range(2)]
            for j in range(NJ):
                wb = wbufs[j % 2]
                P.dma('sp', wb.r(), wi[l, f, j], W=[wb], dkey='wi%d' % (j % 2))
                for half in range(2):
                    sl = hs(half)
                    pa, pbk = pbank(), pbank()
                    for s, pp in ((0, pa), (1, pbk)):
                        for kc in range(8):
                            P.op('pe', 'matmul', pp.ap(), wb.r()[:, kc, s * 128:(s + 1) * 128], h.r()[:, kc, sl],
                                 start=(kc == 0), stop=(kc == 7), R=[wb, h[kc]], W=[pp])
                    tm = tmpb[half]
                    P.op('act', 'activation', tm.ap(), pa.ap(), AF.Silu, R=[pa], W=[tm])
                    P.op('dve', 'tensor_tensor', act.r()[:, j, sl], tm.ap(), pbk.ap(), ALU.mult,
                         R=[tm, pbk], W=[act[j]])
            SR.release(m1)
            act = SR.get(NJ, T)
            wobufs = [SR.get(NJ, 128) for _ in range(2)]
            for mc in range(NFC):
                wb = wobufs[mc % 2]
                P.dma('sp', wb.r(), wo[l, f, mc].rearrange("p (j m) -> p j m", m=128), W=[wb], dkey='wo%d' % (mc % 2))
                for half in range(2):
                    sl = hs(half)
                    po = pbank()
                    for j in range(NJ):
                        P.op('pe', 'matmul', po.ap(), wb.r()[:, j, :], act.r()[:, j, sl],
                             start=(j == 0), stop=(j == NJ - 1), R=[wb, act[j]], W=[po])
                    P.op('dve', 'scalar_tensor_tensor', x.ap()[:, mc, sl], po.ap(),
                         Gvec.ap()[:, l, i * 8 + mc:i * 8 + mc + 1], x.ap()[:, mc, sl], ALU.mult, ALU.add,
                         R=[po, Gvec[l], x[mc]], W=[x[mc]])
            SR.release(m1)


        def lockstep(gens, lag=1):
            live = [True] * len(gens)
            step = 0
            while any(live):
                for gi_ in range(len(gens)):
                    if not live[gi_]:
                        continue
                    if gi_ * lag > step and live[0]:
                        continue
                    try:
                        next(gens[gi_])
                    except StopIteration:
                        live[gi_] = False
                step += 1

        cft = SB.get(1)
        P.dma('sp', cft.ap(), cf_in, W=[cft], dkey='c1')
        MAGIC = 12582912.0
        TWO_PI_HI = 6.28125
        TWO_PI_LO = 6.283185307179586 - 6.28125

        class VS:
            def __init__(self, n):
                self.n = n

            def new(self):
                return SB.get(self.n)

            def tt(self, a, b, op):
                o = self.new()
                P.op('dve', 'tensor_tensor', o.ap(), a.ap(), b.ap(), op, R=[a, b], W=[o])
                return o

            def mul(self, a, b): return self.tt(a, b, ALU.mult)
            def add(self, a, b): return self.tt(a, b, ALU.add)
            def sub(self, a, b): return self.tt(a, b, ALU.subtract)

            def ts(self, a, s1, s2, op0, op1=None):
                o = self.new()
                if op1 is None:
                    P.op('dve', 'tensor_scalar', o.ap(), a.ap(), s1, None, op0, R=[a], W=[o])
                else:
                    P.op('dve', 'tensor_scalar', o.ap(), a.ap(), s1, s2, op0, op1, R=[a], W=[o])
                return o

            def stt(self, a, sc_, b, op0, op1):
                o = self.new()
                P.op('dve', 'scalar_tensor_tensor', o.ap(), a.ap(), sc_, b.ap(), op0, op1, R=[a, b], W=[o])
                return o

            def act(self, a, func, scale=1.0):
                o = self.new()
                P.op('act', 'activation', o.ap(), a.ap(), func, scale=scale, R=[a], W=[o])
                return o

            def recip(self, a):
                o = self.new()
                P.op('dve', 'reciprocal', o.ap(), a.ap(), R=[a], W=[o])
                return o

            def cmul(self, ar, ai, br, bi):
                return (self.sub(self.mul(ar, br), self.mul(ai, bi)), self.add(self.mul(ar, bi), self.mul(ai, br)))

        def bc3(t2, n):
            return t2.ap().unsqueeze(2).to_broadcast([128, t2.n, n])

        def finalize(l, b, actb, first, mg):
            mk = SR.mark()
            nb = 2 if b == 0 else 1
            wbs = [[SR.get(4, 128) for _ in range(nb)] for _ in range(2)]
            wgs = [SR.get(8, 128) for _ in range(2)]
            for fc in range(NFC):
                wg = wgs[fc % 2]
                P.dma('sp', wg.r(), wfm[l, 26 + b * 8 + fc], W=[wg], dkey='fg%d' % (fc % 2))
                wb_ = wbs[fc % 2]
                for k in range(nb):
                    bi = (k if b == 0 else b + 1)
                    P.dma('sp', wb_[k].r(), wbr[l, bi, fc], W=[wb_[k]], dkey='fb%d' % (fc % 2))
                for half in range(2):
                    sl = hs(half)
                    pg = pbank()
                    for kc in range(8):
                        P.op('pe', 'matmul', pg.ap(), wg.r()[:, kc, :], h.r()[:, kc, sl], start=(kc == 0), stop=(kc == 7),
                             R=[wg, h[kc]], W=[pg])
                    gt = tmpb[0]
                    P.op('act', 'activation', gt.ap(), pg.ap(), AF.Sigmoid, R=[pg], W=[gt])
                    p0 = pbank()
                    for kc in range(4):
                        P.op('pe', 'matmul', p0.ap(), wb_[0].r()[:, kc, :], actb.r()[:, kc, sl], start=(kc == 0), stop=(kc == 3),
                             R=[wb_[0], actb[kc]], W=[p0])
                    if b == 0:
                        p1 = pbank()
                        for kc in range(4):
                            P.op('pe', 'matmul', p1.ap(), wb_[1].r()[:, kc, :], actb.r()[:, kc, sl], start=(kc == 0), stop=(kc == 3),
                                 R=[wb_[1], actb[kc]], W=[p1])
                        s1 = tmpb[1]
                        P.op('act', 'activation', s1.ap(), p1.ap(), AF.Sigmoid, R=[p1], W=[s1])
                        P.op('dve', 'tensor_tensor', s1.ap(), s1.ap(), p0.ap(), ALU.mult, R=[s1, p0], W=[s1])
                        brs, brt = s1.ap(), s1
                    else:
                        brs, brt = p0.ap(), p0
                    if first:
                        P.op('dve', 'tensor_tensor', mg.r()[:, fc, sl], gt.ap(), brs, ALU.mult, R=[gt, brt], W=[mg[fc]])
                    else:
                        P.op('dve', 'tensor_tensor', gt.ap(), gt.ap(), brs, ALU.mult, R=[gt, brt], W=[gt])
                        P.op('dve', 'tensor_tensor', mg.r()[:, fc, sl], mg.ap()[:, fc, sl], gt.ap(), ALU.add,
                             R=[gt, mg[fc]], W=[mg[fc]])
            SR.release(mk)

        def proj_fm(l, chunk, dst_fn, wfb):
            wb = wfb[chunk % 2]
            P.dma('sp', wb.r(), wfm[l, chunk], W=[wb], dkey='wf%d' % (chunk % 2))
            for half in range(2):
                sl = hs(half)
                pp = pbank()
                for kc in range(8):
                    P.op('pe', 'matmul', pp.ap(), wb.r()[:, kc, :], h.r()[:, kc, sl], start=(kc == 0), stop=(kc == 7),
                         R=[wb, h[kc]], W=[pp])
                dst_fn(half, pp)

        def s5_branch(l, u):
            mkF, mkR = SB.mark(), SR.mark()
            nrot[0] = 6
            wfb = [SR.get(8, 128) for _ in range(2)]
            for q in range(4):
                def ev(half, pp, q=q):
                    P.op('act', 'activation', u.r()[:, q, hs(half)], pp.ap(), AF.Identity, R=[pp], W=[u[q]])
                proj_fm(l, q, ev, wfb)
            SR.release(mkR)
            ys = SR.get(4, T)
            dsk = SB.get(4)
            P.dma('sp', dsk.ap(), s5d[l], W=[dsk], dkey='c0')
            Ec, Es = SR.get(16, 128), SR.get(16, 128)
            Bb = [SR.get(4, 128) for _ in range(2)]
            Cb = [SR.get(16, 128) for _ in range(2)]
            nCb0 = SR.get(16, 128)
            identR, nidentR = SR.get(128), SR.get(128)
            P.dma('sp', identR.r(), consts_in[:, 2, :], W=[identR], dkey='c0')
            P.op('dve', 'tensor_scalar', nidentR.r(), identR.ap(), -1.0, None, ALU.mult, R=[identR], W=[nidentR])
            wk = [SR.get(512) for _ in range(6)]
            wkall = Tile(arena_r, 'sr', wk[0].off, (3072,))
            wk2 = [wk + [SR.get(512), SR.get(512)], [SR.get(512) for _ in range(8)]]
            SBt2 = [[SB.get(2), SB.get(2)], [SB.get(2), SB.get(2)]]
            hst = [SB.get(16, 4) for _ in range(2)]
            V = VS(16)
            for d in range(2):
                mkd = SB.mark()
                prm = SB.get(3, 16)
                P.dma('sp', prm.ap(), s5p[l, d], W=[prm], dkey='c1')
                h0 = SB.get(2, 16)
                P.dma('sp', h0.ap(), s5h0[l, d], W=[h0], dkey='c0')
                for ri in range(2):
                    P.dma('sp', Bb[ri].r().rearrange("p a b -> p (a b)"), s5b[l, d, ri], W=[Bb[ri]], dkey='sb%d' % ri)
                lamr, lami, ldt = prm[0], prm[1], prm[2]
                dt = V.act(ldt, AF.Exp)
                ar = V.mul(lamr, dt)
                th = V.mul(lami, dt)
                rr = V.act(ar, AF.Exp)

                def red(tx):
                    k = V.ts(tx, 1.0 / (2 * np.pi), MAGIC, ALU.mult, ALU.add)
                    k = V.ts(k, -MAGIC, None, ALU.add)
                    t_ = V.stt(k, -TWO_PI_HI, tx, ALU.mult, ALU.add)
                    return V.stt(k, -TWO_PI_LO, t_, ALU.mult, ALU.add)
                sn = V.act(red(th), AF.Sin)
                cs = V.act(red(V.ts(th, np.pi / 2, None, ALU.add)), AF.Sin)
                lbr, lbi = V.mul(rr, cs), V.mul(rr, sn)
                den = V.add(V.mul(lamr, lamr), V.mul(lami, lami))
                inv = V.recip(den)
                lm1 = V.ts(lbr, -1.0, None, ALU.add)
                cr = V.mul(V.add(V.mul(lm1, lamr), V.mul(lbi, lami)), inv)
                ci = V.mul(V.sub(V.mul(lbi, lamr), V.mul(lm1, lami)), inv)
                ncr, nci = V.ts(cr, -1.0, None, ALU.mult), V.ts(ci, -1.0, None, ALU.mult)
                cinv = V.recip(V.add(V.mul(cr, cr), V.mul(ci, ci)))
                qr, qi = V.cmul(h0[0], h0[1], cr, nci)
                qr, qi = V.mul(qr, cinv), V.mul(qi, cinv)
                i0r, i0i = V.cmul(qr, qi, cs, sn)
                P.op('dve', 'tensor_scalar', Ec.r()[:, :, 0:1], ones.ap()[:, 0:16].unsqueeze(2), 1.0, None, ALU.mult, R=[ones], W=[Ec])
                P.op('dve', 'tensor_scalar', Es.r()[:, :, 0:1], ones.ap()[:, 0:16].unsqueeze(2), 0.0, None, ALU.mult, R=[ones], W=[Es])
                wc, ws = cs, sn
                for k in range(7):
                    n = 1 << k
                    def tv(i):
                        return wkall.r()[:, 1024 * i:1024 * i + 16 * n].rearrange("p (a b) -> p a b", a=16), wkall
                    (a1, T1), (a2, T2), (a3, T3) = tv(0), tv(1), tv(2)
                    a4, T4 = tmp2k.ap()[:, 0:16 * n].rearrange("p (a b) -> p a b", a=16), tmp2k
                    P.op('dve', 'tensor_tensor', a1, Ec.ap()[:, :, 0:n], bc3(wc, n), ALU.mult, R=[Ec, wc], W=[T1])
                    P.op('dve', 'tensor_tensor', a2, Es.ap()[:, :, 0:n], bc3(ws, n), ALU.mult, R=[Es, ws], W=[T2])
                    P.op('dve', 'tensor_tensor', a3, Ec.ap()[:, :, 0:n], bc3(ws, n), ALU.mult, R=[Ec, ws], W=[T3])
                    P.op('dve', 'tensor_tensor', a4, Es.ap()[:, :, 0:n], bc3(wc, n), ALU.mult, R=[Es, wc], W=[T4])
                    P.op('dve', 'tensor_tensor', Ec.r()[:, :, n:2 * n], a1, a2, ALU.subtract, R=[T1, T2], W=[Ec])
                    P.op('dve', 'tensor_tensor', Es.r()[:, :, n:2 * n], a3, a4, ALU.add, R=[T3, T4], W=[Es])
                    wc, ws = V.sub(V.mul(wc, wc), V.mul(ws, ws)), V.ts(V.mul(wc, ws), 2.0, None, ALU.mult)
                wLr, wLi = wc, ws
                wLrc, wLic = V.new(), V.new()
                P.op('dve', 'tensor_scalar', wLrc.ap(), wLr.ap(), cft.ap()[:, 0:1], None, ALU.mult, R=[wLr, cft], W=[wLrc])
                P.op('dve', 'tensor_scalar', wLic.ap(), wLi.ap(), cft.ap()[:, 0:1], None, ALU.mult, R=[wLi, cft], W=[wLic])
                for qq in range(4):
                    cre = tmp2k.ap()[:, 0:512].rearrange("p (a b) -> p a b", a=4)
                    cim = tmp2k.ap()[:, 512:1024].rearrange("p (a b) -> p a b", a=4)
                    P.dma('sp', cre, s5c[l, d, 0, :, 4 * qq:4 * qq + 4, :], W=[tmp2k], dkey='sc0')
                    P.dma('sp', cim, s5c[l, d, 1, :, 4 * qq:4 * qq + 4, :], W=[tmp2k], dkey='sc1')
                    tq = [wk[i].r().rearrange("p (a b) -> p a b", a=4) for i in range(4)]
                    def b4(v):
                        return v.ap()[:, 4 * qq:4 * qq + 4].unsqueeze(2).to_broadcast([128, 4, 128])
                    P.op('dve', 'tensor_tensor', tq[0], cre, b4(cr), ALU.mult, R=[tmp2k, cr], W=[wk[0]])
                    P.op('dve', 'tensor_tensor', tq[1], cim, b4(ci), ALU.mult, R=[tmp2k, ci], W=[wk[1]])
                    P.op('dve', 'tensor_tensor', tq[2], cim, b4(ncr), ALU.mult, R=[tmp2k, ncr], W=[wk[2]])
                    P.op('dve', 'tensor_tensor', tq[3], cre, b4(ci), ALU.mult, R=[tmp2k, ci], W=[wk[3]])
                    P.op('dve', 'tensor_tensor', Cb[0].r()[:, 4 * qq:4 * qq + 4, :], tq[0], tq[1], ALU.subtract,
                         R=[wk[0], wk[1]], W=[Cb[0]])
                    P.op('dve', 'tensor_tensor', Cb[1].r()[:, 4 * qq:4 * qq + 4, :], tq[2], tq[3], ALU.subtract,
                         R=[wk[2], wk[3]], W=[Cb[1]])
                    P.op('dve', 'tensor_tensor', nCb0.r()[:, 4 * qq:4 * qq + 4, :], tq[1], tq[0], ALU.subtract,
                         R=[wk[0], wk[1]], W=[nCb0])
                car = SB.get(16, 2)
                P.op('dve', 'tensor_copy', car.ap()[:, :, 0], i0r.ap(), R=[i0r], W=[car])
                P.op('dve', 'tensor_copy', car.ap()[:, :, 1], i0i.ap(), R=[i0i], W=[car])
                wpN, wpC = SB.get(16, 2), SB.get(16, 2)
                for wp_, wi_src in ((wpN, wLi), (wpC, wLic)):
                    P.op('dve', 'tensor_scalar', wp_.ap()[:, :, 0], wi_src.ap(), -1.0, None, ALU.mult, R=[wi_src], W=[wp_])
                    P.op('dve', 'tensor_copy', wp_.ap()[:, :, 1], wi_src.ap(), R=[wi_src], W=[wp_])
                rev = (d == 1)
                halves = [1, 0] if rev else [0, 1]
                for half in halves:
                    sl = hs(half)
                    for q in range(4):
                        py = pinbank()
                        def unit(ch, mi):
                            mc = 4 * q + mi
                            W_ = wk2[ch]
                            vrT, viT, grT, giT, t1T, t2T, t3T, t4T = W_
                            vr, vi, gr, gi, t1, t2, t3, t4 = [w_.r() for w_ in W_]
                            pa, pb_ = pbank(), pbank()
                            r0 = 32 * mi
                            for ri, pp in ((0, pa), (1, pb_)):
                                P.op('pe', 'matmul', pp.ap(), Bb[ri].r()[r0:r0 + 32, q, :], u.r()[r0:r0 + 32, q, sl],
                                     start=True, stop=True, tile_position=(r0, 0), R=[Bb[ri], u[q]], W=[pp])
                            yield
                            tabc = Ec.ap()[:, mc, ::-1] if rev else Ec.ap()[:, mc, :]
                            tabs = Es.ap()[:, mc, ::-1] if rev else Es.ap()[:, mc, :]
                            tc4 = tabc.unsqueeze(1).to_broadcast([128, 4, 128])
                            tsn4 = tabs.unsqueeze(1).to_broadcast([128, 4, 128])
                            def v3(a):
                                return a.rearrange("p (a b) -> p a b", a=4)
                            P.op('act', 'activation', gr, pa.ap(), AF.Identity, R=[pa], W=[grT])
                            P.op('act', 'activation', gi, pb_.ap(), AF.Identity, R=[pb_], W=[giT])
                            yield
                            P.op('dve', 'tensor_tensor', v3(t1), v3(gr), tc4, ALU.mult, R=[grT, Ec], W=[t1T])
                            P.op('dve', 'tensor_tensor', v3(t2), v3(gi), tsn4, ALU.mult, R=[giT, Es], W=[t2T])
                            P.op('pool', 'tensor_tensor', v3(t3), v3(gi), tc4, ALU.mult, R=[giT, Ec], W=[t3T])
                            P.op('pool', 'tensor_tensor', v3(t4), v3(gr), tsn4, ALU.mult, R=[grT, Es], W=[t4T])
                            yield
                            pvr, pvi = pbank(), pbank()
                            P.op('pe', 'matmul', pvr.ap(), identR.r(), t1, start=True, stop=False, R=[identR, t1T], W=[pvr])
                            P.op('pe', 'matmul', pvr.ap(), identR.r(), t2, start=False, stop=True, R=[identR, t2T], W=[pvr])
                            yield
                            P.op('pe', 'matmul', pvi.ap(), identR.r(), t3, start=True, stop=False, R=[identR, t3T], W=[pvi])
                            P.op('pe', 'matmul', pvi.ap(), nidentR.r(), t4, start=False, stop=True, R=[nidentR, t4T], W=[pvi])
                            yield
                            vr, vi, vrT, viT = pvr.ap(), pvi.ap(), pvr, pvi
                            corder = [3, 2, 1, 0] if rev else [0, 1, 2, 3]
                            for ci_, c in enumerate(corder):
                                cs_ = slice(c * 128, (c + 1) * 128)
                                def dv(a):
                                    a = a[:, cs_]
                                    return a[:, ::-1] if rev else a
                                rb = rr.ap()[:, mc:mc + 1].to_broadcast([128, 128])
                                P.op('dve', 'tensor_tensor_scan', dv(gr), rb, dv(vr), car.ap()[:, mc, 0:1], ALU.mult, ALU.add,
                                     R=[rr, vrT, car], W=[grT])
                                P.op('dve', 'tensor_tensor_scan', dv(gi), rb, dv(vi), car.ap()[:, mc, 1:2], ALU.mult, ALU.add,
                                     R=[rr, viT, car], W=[giT])
                                yield
                                gc = c * 128 if rev else c * 128 + 127
                                gch = 4 * half + c
                                nxt = gch - 1 if rev else gch + 1
                                segb = (gch % 2 == 0) if rev else (nxt % 2 == 0)
                                wr_, wp_ = (wLrc, wpC) if segb else (wLr, wpN)
                                g0, g1 = grT.off + gc, giT.off + gc
                                assert g1 == g0 + 512
                                glast = arena_r[:, g0:g0 + 513:512]
                                glrev = arena_r[:, g1:g0 - 1:-512]
                                tP = SBt2[ch][0].ap()[:, 0:2]
                                P.op('dve', 'tensor_tensor', tP, glrev, wp_.ap()[:, mc, :], ALU.mult,
                                     R=[grT, giT, wp_], W=[SBt2[ch][0]])
                                yield
                                P.op('dve', 'scalar_tensor_tensor', car.ap()[:, mc, :], glast, wr_.ap()[:, mc:mc + 1], tP,
                                     ALU.mult, ALU.add, R=[grT, giT, wr_, SBt2[ch][0]], W=[car])
                                yield
                            P.op('pool', 'tensor_tensor', v3(t1), v3(gr), tc4, ALU.mult, R=[grT, Ec], W=[t1T])
                            P.op('pool', 'tensor_tensor', v3(t2), v3(gi), tsn4, ALU.mult, R=[giT, Es], W=[t2T])
                            yield
                            P.op('pool', 'tensor_tensor', v3(t3), v3(gi), tc4, ALU.mult, R=[giT, Ec], W=[t3T])
                            P.op('pool', 'tensor_tensor', v3(t4), v3(gr), tsn4, ALU.mult, R=[grT, Es], W=[t4T])
                            yield
                            c0 = 0 if rev else 255
                            P.op('pool', 'tensor_tensor', hst[0].ap()[:, mc, 2 * half:2 * half + 2], t1[:, c0::256], t2[:, c0::256], ALU.subtract,
                                 R=[t1T, t2T], W=[hst[0]])
                            P.op('pool', 'tensor_tensor', hst[1].ap()[:, mc, 2 * half:2 * half + 2], t3[:, c0::256], t4[:, c0::256], ALU.add,
                                 R=[t3T, t4T], W=[hst[1]])
                            yield
                            P.op('pe', 'matmul', py.ap(), Cb[0].r()[:, mc, :], t1, start=(mi == 0), stop=False, R=[Cb[0], t1T], W=[py])
                            P.op('pe', 'matmul', py.ap(), nCb0.r()[:, mc, :], t2, start=False, stop=False, R=[nCb0, t2T], W=[py])
                            P.op('pe', 'matmul', py.ap(), Cb[1].r()[:, mc, :], t3, start=False, stop=False, R=[Cb[1], t3T], W=[py])
                            P.op('pe', 'matmul', py.ap(), Cb[1].r()[:, mc, :], t4, start=False, stop=(mi == 3), R=[Cb[1], t4T], W=[py])
                            yield

                        for pair in ((0, 1), (2, 3)):
                            gens = [unit(0, pair[0]), unit(1, pair[1])]
                            live = [True, True]
                            step = 0
                            while any(live):
                                for gi_ in range(2):
                                    if not live[gi_]:
                                        continue
                                    if gi_ == 1 and step < 2 and live[0]:
                                        continue
                                    try:
                                        next(gens[gi_])
                                    except StopIteration:
                                        live[gi_] = False
                                step += 1
                        if d == 0:
                            P.op('dve', 'scalar_tensor_tensor', ys.r()[:, q, sl], u.ap()[:, q, sl], dsk.ap()[:, q:q + 1], py.ap(),
                                 ALU.mult, ALU.add, R=[u[q], dsk, py], W=[ys[q]])
                        else:
                            P.op('dve', 'tensor_tensor', ys.r()[:, q, sl], ys.ap()[:, q, sl], py.ap(), ALU.add,
                                 R=[ys[q], py], W=[ys[q]])
                V64 = VS(64)
                so = [V64.new(), V64.new()]
                crb = cr.ap().unsqueeze(2).to_broadcast([128, 16, 4])
                cib = ci.ap().unsqueeze(2).to_broadcast([128, 16, 4])
                def s3(t_):
                    return t_.ap().rearrange("p (a b) -> p a b", a=16)
                ta, tb_ = V64.new(), V64.new()
                P.op('dve', 'tensor_tensor', s3(ta), hst[0].ap(), crb, ALU.mult, R=[hst[0], cr], W=[ta])
                P.op('dve', 'tensor_tensor', s3(tb_), hst[1].ap(), cib, ALU.mult, R=[hst[1], ci], W=[tb_])
                P.op('dve', 'tensor_tensor', so[0].ap(), ta.ap(), tb_.ap(), ALU.subtract, R=[ta, tb_], W=[so[0]])
                P.op('dve', 'tensor_tensor', s3(ta), hst[0].ap(), cib, ALU.mult, R=[hst[0], ci], W=[ta])
                P.op('dve', 'tensor_tensor', s3(tb_), hst[1].ap(), crb, ALU.mult, R=[hst[1], cr], W=[tb_])
                P.op('dve', 'tensor_tensor', so[1].ap(), ta.ap(), tb_.ap(), ALU.add, R=[ta, tb_], W=[so[1]])
                for ri in range(2):
                    P.dma('sp', st_s5[ri, l, d].rearrange("p a b -> p (a b)"), so[ri].ap(), R=[so[ri]], dkey='so%d' % ri)
                SB.release(mkd)
            if l == 0:
                tap('ys', ys)
            for q in range(4):
                for half in range(2):
                    sl = hs(half)
                    t1, t2 = tmpb[0], tmpb[1]
                    P.op('act', 'activation', t1.ap(), ys.ap()[:, q, sl], AF.Square, R=[ys[q]], W=[t1])
                    P.op('dve', 'tensor_scalar', t1.ap(), t1.ap(), 0.044715, 1.0, ALU.mult, ALU.add, R=[t1], W=[t1])
                    P.op('dve', 'tensor_tensor', t1.ap(), t1.ap(), ys.ap()[:, q, sl], ALU.mult, R=[t1, ys[q]], W=[t1])
                    P.op('act', 'activation', t2.ap(), t1.ap(), AF.Sigmoid, scale=1.5957691216057308, R=[t1], W=[t2])
                    P.op('dve', 'tensor_tensor', u.r()[:, q, sl], t2.ap(), ys.ap()[:, q, sl], ALU.mult, R=[t2, ys[q]], W=[u[q]])
            SB.release(mkF)
            SR.release(mkR)
            nrot[0] = 8


        def load_consts():
            cst = SR.get(7, 128)
            P.dma('sp', cst.r(), consts_in, W=[cst], dkey='c0')
            return cst

        def small_params(l):
            wsb = SR.get(8, 32)
            P.dma('sp', wsb.r(), wsm[l], W=[wsb], dkey='c1')
            bia = SB.get(32)
            P.dma('sp', bia.ap(), smb[l], W=[bia], dkey='c0')
            alg = SB.get(24)
            P.dma('sp', alg.ap(), sma[l], W=[alg], dkey='c1')
            pb = pbank()
            for tc in range(8):
                for kc in range(8):
                    P.op('pe', 'matmul', pb.ap()[:, tc * 32:(tc + 1) * 32], h.r()[:, kc, tc * 128:(tc + 1) * 128], wsb.r()[:, kc, :],
                         start=(kc == 0), stop=(kc == 7), R=[h[kc], wsb], W=[pb])
            sp = SB.get(8, 32)
            beta = SB.get(8, 8)
            na = SB.get(24)
            da = SB.get(8, 16)
            gdn = SB.get(8, 8)
            mkt = SB.mark()
            raw = SB.get(8, 32)
            P.op('dve', 'tensor_tensor', raw.ap(), pb.ap()[:, 0:256].rearrange("p (a b) -> p a b", a=8),
                 bia.ap().unsqueeze(1).to_broadcast([128, 8, 32]), ALU.add, R=[pb, bia], W=[raw])
            ex = SB.get(8, 32)
            P.op('act', 'activation', ex.ap(), raw.ap(), AF.Exp, R=[raw], W=[ex])
            P.op('act', 'activation', sp.ap(), ex.ap(), AF.Ln, bias=1.0, R=[ex], W=[sp])
            P.op('act', 'activation', beta.ap(), raw.ap()[:, :, 16:24], AF.Sigmoid, R=[raw], W=[beta])
            P.op('act', 'activation', na.ap(), alg.ap(), AF.Exp, R=[alg], W=[na])
            P.op('dve', 'tensor_scalar', na.ap(), na.ap(), -1.0, None, ALU.mult, R=[na], W=[na])
            P.op('dve', 'tensor_tensor', da.ap(), sp.ap()[:, :, 0:16], na.ap()[:, 0:16].unsqueeze(1).to_broadcast([128, 8, 16]),
                 ALU.mult, R=[sp, na], W=[da])
            P.op('dve', 'tensor_tensor', gdn.ap(), sp.ap()[:, :, 24:32], na.ap()[:, 16:24].unsqueeze(1).to_broadcast([128, 8, 8]),
                 ALU.mult, R=[sp, na], W=[gdn])
            SB.release(mkt)
            return dict(dt=sp, da=da, beta=beta, gdn=gdn)

        def conv_chunk(l, wchunk, cw, ci, dst, msk, tmps, wfb, bias=True):
            raw, xl, xr, acc = tmps
            def ev(half, pp):
                P.op('act', 'activation', raw.r()[:, hs(half)], pp.ap(), AF.Identity, R=[pp], W=[raw])
            proj_fm(l, wchunk, ev, wfb)
            P.op('pool', 'tensor_tensor', xl.r()[:, 0:T - 1], raw.ap()[:, 0:T - 1], msk.ap()[:, 0, 1:T], ALU.mult, R=[raw, msk], W=[xl])
            P.op('pool', 'tensor_tensor', xr.r()[:, 1:T], raw.ap()[:, 1:T], msk.ap()[:, 1, 1:T], ALU.mult, R=[raw, msk], W=[xr])
            if bias:
                P.op('act', 'activation', acc.r(), raw.ap(), AF.Identity, scale=cw.ap()[:, ci, 1:2], bias=cw.ap()[:, ci, 3:4],
                     R=[raw, cw], W=[acc])
            else:
                P.op('act', 'activation', acc.r(), raw.ap(), AF.Identity, scale=cw.ap()[:, ci, 1:2], R=[raw, cw], W=[acc])
            P.op('dve', 'scalar_tensor_tensor', acc.r()[:, 1:T], xl.ap()[:, 0:T - 1], cw.ap()[:, ci, 0:1], acc.ap()[:, 1:T],
                 ALU.mult, ALU.add, R=[xl, cw, acc], W=[acc])
            P.op('dve', 'scalar_tensor_tensor', acc.r()[:, 0:T - 1], xr.ap()[:, 1:T], cw.ap()[:, ci, 2:3], acc.ap()[:, 0:T - 1],
                 ALU.mult, ALU.add, R=[xr, cw, acc], W=[acc])
            P.op('act', 'activation', dst.r(), acc.ap(), AF.Silu, R=[acc], W=[dst])

        def decay_T(cst, d, strict, da_col, acs_col, out_tile, xt, tt_):
            TRI = cst[d]
            MN = cst[(5 if strict else 3) + d]
            P.op('dve', 'tensor_scalar', xt.ap(), TRI.ap(), da_col, None, ALU.mult, R=[TRI], W=[xt])
            bc = pbank()
            P.op('pe', 'matmul', bc.ap()[:, 0:128], ones.ap(), xt.ap(), start=True, stop=True, R=[ones, xt], W=[bc])
            P.op('dve', 'scalar_tensor_tensor', tt_.ap(), bc.ap()[:, 0:128], acs_col, MN.ap(), ALU.subtract, ALU.add,
                 R=[bc, MN], W=[tt_])
            P.op('act', 'activation', out_tile.ap(), tt_.ap(), AF.Exp, R=[tt_], W=[out_tile])
            return bc

        def ssd_branch(l, actb, sm, cst):
            mkF, mkR = SB.mark(), SR.mark()
            nrot[0] = 3
            B3, B4, B5, B6, B7 = banks[3], banks[4], banks[5], banks[6], banks[7]
            SCB = [B6, B3]
            xbc = SR.get(6, T)
            xs, Bm, Cm = [xbc[i] for i in range(4)], xbc[4], xbc[5]
            mk2 = SR.mark()
            wfb = [SR.get(8, 128) for _ in range(2)]
            msk = SR.get(2, T)
            P.dma('sp', msk.r(), cmask_in, W=[msk], dkey='c0')
            cw = SB.get(6, 4)
            P.dma('sp', cw.ap(), ssdcw[l], W=[cw], dkey='c1')
            tmps = [SR.get(T) for _ in range(4)]
            for ci in range(6):
                conv_chunk(l, 4 + ci, cw, ci, xbc[ci], msk, tmps, wfb)
            SR.release(mk2)
            if l == 0:
                tap('xbc', xbc)
            ST = [SR.get(512) for _ in range(2)]
            xdt, xdtw, BmT = SR.get(512), SR.get(512), SR.get(128)
            PT = [SR.get(128) for _ in range(2)]
            Xt = [SB.get(128) for _ in range(2)]
            Tt = [SB.get(128) for _ in range(2)]
            Dc = [SB.get(128) for _ in range(2)]
            yt = SB.get(512)
            acs, tot, eA, toe, cdd = SB.get(8), SB.get(8), SB.get(8), SB.get(8), SB.get(8)
            ident = cst[2]
            for d in range(2):
                P.dma('sp', ST[d].r(), ssdh0[l, d], W=[ST[d]], dkey='sh%d' % d)
                order = range(8) if d == 0 else range(7, -1, -1)
                for c in order:
                    tok = slice(c * 128, (c + 1) * 128)
                    if STAGE < 1:
                        continue
                    for kc in range(4):
                        P.op('pe', 'transpose', B4.ap()[:, kc * 128:(kc + 1) * 128], xs[kc].ap()[:, tok], ident.ap(),
                             R=[xs[kc], ident], W=[B4])
                    dtb = sm['dt'].ap()[:, c, 8 * d:8 * d + 8].unsqueeze(2).to_broadcast([128, 8, 64])
                    P.op('dve', 'tensor_tensor', xdt.r().rearrange("p (a b) -> p a b", a=8),
                         B4.ap().rearrange("p (a b) -> p a b", a=8), dtb, ALU.mult, R=[B4, sm['dt']], W=[xdt])
                    if STAGE < 2:
                        continue
                    P.op('pe', 'transpose', B5.ap()[:, 0:128], Bm.ap()[:, tok], ident.ap(), R=[Bm, ident], W=[B5])
                    P.op('act', 'activation', BmT.r(), B5.ap()[:, 0:128], AF.Identity, R=[B5], W=[BmT])
                    if STAGE < 3:
                        continue
                    dac = sm['da'].ap()[:, c, 8 * d:8 * d + 8]
                    P.op('pe', 'matmul', B5.ap()[:, 128:136], cst[d].ap(), dac, start=True, stop=True, R=[cst[d], sm['da']], W=[B5])
                    P.op('pe', 'matmul', B5.ap()[:, 136:144], ones.ap(), dac, start=True, stop=True, R=[ones, sm['da']], W=[B5])
                    P.op('dve', 'tensor_copy', acs.ap(), B5.ap()[:, 128:136], R=[B5], W=[acs])
                    P.op('dve', 'tensor_copy', tot.ap(), B5.ap()[:, 136:144], R=[B5], W=[tot])
                    P.op('act', 'activation', eA.ap(), acs.ap(), AF.Exp, R=[acs], W=[eA])
                    P.op('act', 'activation', cdd.ap(), tot.ap(), AF.Exp, R=[tot], W=[cdd])
                    P.op('dve', 'tensor_tensor', toe.ap(), tot.ap(), acs.ap(), ALU.subtract, R=[tot, acs], W=[toe])
                    P.op('act', 'activation', toe.ap(), toe.ap(), AF.Exp, R=[toe], W=[toe])
                    if STAGE < 4:
                        continue
                    for gr in range(2):
                        P.op('pe', 'matmul', SCB[gr].ap()[:, 0:128], Bm.r()[64 * gr:64 * gr + 64, tok],
                             Cm.r()[64 * gr:64 * gr + 64, tok], start=True, stop=True, tile_position=(64 * gr, 0),
                             R=[Bm, Cm], W=[SCB[gr]])
                    if STAGE < 5:
                        continue
                    def head_unit(hh):
                        gr = hh // 4
                        dc_, xt_, tt_, pt = Dc[hh % 2], Xt[hh % 2], Tt[hh % 2], PT[hh % 2]
                        da_col = sm['da'].ap()[:, c, 8 * d + hh:8 * d + hh + 1]
                        P.op('dve', 'tensor_scalar', xt_.ap(), cst[d].ap(), da_col, None, ALU.mult, R=[cst[d], sm['da']], W=[xt_])
                        bc = pbank()
                        P.op('pe', 'matmul', bc.ap()[:, 0:128], ones.ap(), xt_.ap(), start=True, stop=True, R=[ones, xt_], W=[bc])
                        yield
                        P.op('dve', 'scalar_tensor_tensor', tt_.ap(), bc.ap()[:, 0:128], acs.ap()[:, hh:hh + 1], cst[3 + d].ap(),
                             ALU.subtract, ALU.add, R=[bc, acs, cst[3 + d]], W=[tt_])
                        yield
                        P.op('act', 'activation', dc_.ap(), tt_.ap(), AF.Exp, R=[tt_], W=[dc_])
                        yield
                        P.op('dve', 'tensor_tensor', pt.r(), SCB[gr].ap()[:, 0:128], dc_.ap(), ALU.mult,
                             R=[SCB[gr], dc_], W=[pt])
                        yield
                        P.op('pe', 'matmul', B7.ap()[:, 64 * hh:64 * hh + 64], pt.r(), xdt.r()[:, 64 * hh:64 * hh + 64],
                             start=True, stop=True, R=[pt, xdt], W=[B7])
                        yield
                    for hp in range(4):
                        lockstep([head_unit(2 * hp), head_unit(2 * hp + 1)], lag=1)
                    if STAGE < 6:
                        continue
                    yo = pbank()
                    P.op('pe', 'matmul', yo.ap(), Cm.r()[:, tok], ST[d].r(), start=True, stop=True, R=[Cm, ST[d]], W=[yo])
                    eab = eA.ap().unsqueeze(2).to_broadcast([128, 8, 64])
                    P.op('dve', 'tensor_tensor', yt.ap().rearrange("p (a b) -> p a b", a=8),
                         yo.ap().rearrange("p (a b) -> p a b", a=8), eab, ALU.mult, R=[yo, eA], W=[yt])
                    P.op('dve', 'tensor_tensor', yt.ap(), yt.ap(), B7.ap(), ALU.add, R=[yt, B7], W=[yt])
                    if STAGE < 7:
                        continue
                    for kc in range(4):
                        P.op('pe', 'transpose', B4.ap()[:, kc * 128:(kc + 1) * 128], yt.ap()[:, kc * 128:(kc + 1) * 128], ident.ap(),
                             R=[yt, ident], W=[B4])
                    b4v = B4.ap().rearrange("p (a b) -> p a b", a=4)
                    if d == 0:
                        P.op('act', 'activation', actb.r()[:, :, tok], b4v, AF.Identity, R=[B4], W=[actb[k_] for k_ in range(4)])
                    else:
                        P.op('dve', 'tensor_tensor', actb.r()[:, :, tok], actb.ap()[:, :, tok], b4v, ALU.add,
                             R=[B4] + [actb[k_] for k_ in range(4)], W=[actb[k_] for k_ in range(4)])
                    if STAGE < 8:
                        continue
                    teb = toe.ap().unsqueeze(2).to_broadcast([128, 8, 64])
                    P.op('dve', 'tensor_tensor', xdtw.r().rearrange("p (a b) -> p a b", a=8),
                         xdt.ap().rearrange("p (a b) -> p a b", a=8), teb, ALU.mult, R=[xdt, toe], W=[xdtw])
                    sn = pbank()
                    P.op('pe', 'matmul', sn.ap(), BmT.r(), xdtw.r(), start=True, stop=True, R=[BmT, xdtw], W=[sn])
                    for gr in range(2):
                        ps_ = slice(64 * gr, 64 * gr + 64)
                        fs_ = slice(256 * gr, 256 * gr + 256)
                        cdb = cdd.ap()[ps_, 4 * gr:4 * gr + 4].unsqueeze(2).to_broadcast([64, 4, 64])
                        P.op('dve', 'tensor_tensor', ST[d].r()[ps_, fs_].rearrange("p (a b) -> p a b", a=4),
                             ST[d].ap()[ps_, fs_].rearrange("p (a b) -> p a b", a=4), cdb, ALU.mult,
                             R=[ST[d], cdd], W=[ST[d]])
                        P.op('dve', 'tensor_tensor', ST[d].r()[ps_, fs_], ST[d].ap()[ps_, fs_], sn.ap()[ps_, fs_], ALU.add,
                             R=[ST[d], sn], W=[ST[d]])
                    seg_end = (c % 2 == 1) if d == 0 else (c % 2 == 0)
                    if seg_end:
                        P.dma('sp', st_ssd[l, d, c // 2], ST[d].ap(), R=[ST[d]], dkey='so%d' % d)
                        P.op('dve', 'tensor_scalar', ST[d].r(), ST[d].ap(), cft.ap()[:, 0:1], None, ALU.mult,
                             R=[ST[d], cft], W=[ST[d]])
            if l == 0:
                tap('yssd', actb)
            sv = SB.get(4, 2)
            P.dma('sp', sv.ap(), ssdv[l], W=[sv], dkey='c1')
            wfb = [SR.get(8, 128) for _ in range(2)]
            for kc in range(4):
                P.op('dve', 'scalar_tensor_tensor', actb.r()[:, kc, :], xs[kc].ap(), sv.ap()[:, kc, 1:2], actb.ap()[:, kc, :],
                     ALU.mult, ALU.add, R=[xs[kc], sv, actb[kc]], W=[actb[kc]])
                def evz(half, pp, kc=kc):
                    tm = tmpb[half]
                    P.op('act', 'activation', tm.ap(), pp.ap(), AF.Silu, R=[pp], W=[tm])
                    P.op('dve', 'tensor_tensor', actb.r()[:, kc, hs(half)], actb.ap()[:, kc, hs(half)], tm.ap(), ALU.mult,
                         R=[tm, actb[kc]], W=[actb[kc]])
                proj_fm(l, 50 + kc, evz, wfb)
            nrot[0] = 8
            for half in range(2):
                sl = hs(half)
                rs = rstd[half]
                rms_rstd(actb, half, rs, nch=4)
                for kc in range(4):
                    P.op('dve', 'scalar_tensor_tensor', actb.r()[:, kc, sl], actb.ap()[:, kc, sl], sv.ap()[:, kc, 0:1], rs.ap(),
                         ALU.mult, ALU.mult, R=[actb[kc], sv, rs], W=[actb[kc]])
            SB.release(mkF)
            SR.release(mkR)


        def dn_branch(l, actb, sm, cst):
            mkF, mkR = SB.mark(), SR.mark()
            ident = cst[2]
            gd = sm['gdn']
            pb = pbank()
            for c in range(8):
                for d in range(2):
                    P.op('pe', 'matmul', pb.ap()[:, 64 * d + c * 8:64 * d + c * 8 + 8], cst[d].ap(), gd.ap()[:, c, :],
                         start=True, stop=True, R=[cst[d], gd], W=[pb])
                P.op('pe', 'matmul', pb.ap()[:, 128 + c * 8:128 + c * 8 + 8], ones.ap(), gd.ap()[:, c, :],
                     start=True, stop=True, R=[ones, gd], W=[pb])
            gcs, gto = SB.get(8, 8), SB.get(8, 8)
            for d in range(2):
                P.op('dve', 'tensor_copy', gcs.ap()[:, :, 4 * d:4 * d + 4],
                     pb.ap()[:, 64 * d:64 * d + 64].rearrange("p (a b) -> p a b", a=8)[:, :, 4 * d:4 * d + 4], R=[pb], W=[gcs])
            P.op('dve', 'tensor_copy', gto.ap(), pb.ap()[:, 128:192].rearrange("p (a b) -> p a b", a=8), R=[pb], W=[gto])
            egc, ed, egt, bg, nbe = SB.get(8, 8), SB.get(8, 8), SB.get(8, 8), SB.get(8, 8), SB.get(8, 8)
            P.op('act', 'activation', egc.ap(), gcs.ap(), AF.Exp, R=[gcs], W=[egc])
            P.op('act', 'activation', egt.ap(), gto.ap(), AF.Exp, R=[gto], W=[egt])
            P.op('dve', 'tensor_tensor', ed.ap(), gto.ap(), gcs.ap(), ALU.subtract, R=[gto, gcs], W=[ed])
            P.op('act', 'activation', ed.ap(), ed.ap(), AF.Exp, R=[ed], W=[ed])
            P.op('dve', 'tensor_tensor', bg.ap(), sm['beta'].ap(), egc.ap(), ALU.mult, R=[sm['beta'], egc], W=[bg])
            P.op('dve', 'tensor_scalar', nbe.ap(), sm['beta'].ap(), -1.0, None, ALU.mult, R=[sm['beta']], W=[nbe])
            cw = SB.get(12, 3)
            P.dma('sp', cw.ap(), dncw[l], W=[cw], dkey='c1')
            gv = SB.get(1)
            P.dma('sp', gv.ap(), dnv[l], W=[gv], dkey='c0')
            Xt = [SB.get(128) for _ in range(2)]
            Tt = [SB.get(128) for _ in range(2)]
            DT = [SB.get(128) for _ in range(2)]
            DM = [SB.get(128) for _ in range(2)]
            Eg = [SB.get(128) for _ in range(2)]
            X, XT, TT, Xn = SB.get(2, 128), SB.get(2, 128), SB.get(2, 128), SB.get(2, 128)
            for hd in range(4):
                mkh = SR.mark()
                qkv = SR.get(3, T)
                q, k, v = qkv[0], qkv[1], qkv[2]
                mk2 = SR.mark()
                nrot[0] = 8
                wfb = [SR.get(8, 128) for _ in range(2)]
                msk = SR.get(2, T)
                P.dma('sp', msk.r(), cmask_in, W=[msk], dkey='c0')
                tmps = [SR.get(T) for _ in range(4)]
                for j3 in range(3):
                    conv_chunk(l, 10 + 4 * j3 + hd, cw, 4 * j3 + hd, qkv[j3], msk, tmps, wfb, bias=False)
                SR.release(mk2)
                for j3 in range(2):
                    for half in range(2):
                        sl = hs(half)
                        sq = sqb[half]
                        pbk = pbank()
                        P.op('act', 'activation', sq.r(), qkv[j3].ap()[:, sl], AF.Square, R=[qkv[j3]], W=[sq])
                        P.op('pe', 'matmul', pbk.ap(), onesm.r(), sq.r(), start=True, stop=True, R=[sq, onesm], W=[pbk])
                        rs = rstd[half]
                        P.op('act', 'activation', rs.ap(), pbk.ap(), AF.Sqrt, bias=epst_b.ap(), scale=1.0, R=[pbk, epst_b], W=[rs])
                        P.op('dve', 'reciprocal', rs.ap(), rs.ap(), R=[rs], W=[rs])
                        P.op('dve', 'scalar_tensor_tensor', qkv[j3].r()[:, sl], qkv[j3].ap()[:, sl],
                             (128.0 ** -0.5 if j3 == 0 else 1.0), rs.ap(), ALU.mult, ALU.mult, R=[qkv[j3], rs], W=[qkv[j3]])
                if l == 0 and hd == 0:
                    tap('dnqkv', qkv)
                nrot[0] = 4
                AT, vb, kbg, kd, nwT, vn, qg = [SR.get(2, 128) for _ in range(7)]
                S = SR.get(2, 128)
                for d in range(2):
                    P.dma('sp', S[d].r(), dnh0[l, d, hd], W=[S[d]], dkey='sh%d' % d)
                KB = [banks[4], banks[5]]
                TB = [banks[6], banks[7]]
                for t in range(8 if STAGE > 10 else 0):
                    cs = [t, 7 - t]
                    for d in range(2):
                        c = cs[d]
                        tok = slice(c * 128, (c + 1) * 128)
                        col = slice(4 * d + hd, 4 * d + hd + 1)
                        kb_ = KB[d]
                        tb_ = TB[d]
                        P.op('pe', 'matmul', kb_.ap()[:, 0:128], k.r()[:, tok], k.r()[:, tok], start=True, stop=True, R=[k], W=[kb_])
                        P.op('pe', 'matmul', kb_.ap()[:, 128:256], k.r()[:, tok], q.r()[:, tok], start=True, stop=True, R=[k, q], W=[kb_])
                        P.op('pe', 'transpose', tb_.ap()[:, 0:128], k.ap()[:, tok], ident.ap(), R=[k, ident], W=[tb_])
                        P.op('pe', 'transpose', tb_.ap()[:, 128:256], v.ap()[:, tok], ident.ap(), R=[v, ident], W=[tb_])
                        TRI = cst[d]
                        P.op('dve', 'tensor_scalar', Xt[d].ap(), TRI.ap(), gd.ap()[:, c, col], None, ALU.mult, R=[TRI, gd], W=[Xt[d]])
                        bc = pbank()
                        P.op('pe', 'matmul', bc.ap()[:, 0:128], ones.ap(), Xt[d].ap(), start=True, stop=True, R=[ones, Xt[d]], W=[bc])
                        gcc = gcs.ap()[:, c, col]
                        P.op('dve', 'scalar_tensor_tensor', Tt[d].ap(), bc.ap()[:, 0:128], gcc, cst[3 + d].ap(), ALU.subtract, ALU.add,
                             R=[bc, gcs, cst[3 + d]], W=[Tt[d]])
                        P.op('act', 'activation', DT[d].ap(), Tt[d].ap(), AF.Exp, R=[Tt[d]], W=[DT[d]])
                        P.op('dve', 'scalar_tensor_tensor', Tt[d].ap(), bc.ap()[:, 0:128], -1.0, cst[6 - d].ap(), ALU.mult, ALU.add,
                             R=[bc, cst[6 - d]], W=[Tt[d]])
                        P.op('act', 'activation', DM[d].ap(), Tt[d].ap(), AF.Exp, bias=gcc, R=[Tt[d], gcs], W=[DM[d]])
                        P.op('act', 'activation', Eg[d].ap(), bc.ap()[:, 0:128], AF.Exp, R=[bc], W=[Eg[d]])
                        P.op('dve', 'scalar_tensor_tensor', X.ap()[:, d, :], kb_.ap()[:, 0:128], nbe.ap()[:, c, col], DM[d].ap(),
                             ALU.mult, ALU.mult, R=[kb_, nbe, DM[d]], W=[X[d]])
                        P.op('dve', 'tensor_tensor', AT.r()[:, d, :], kb_.ap()[:, 128:256], DT[d].ap(), ALU.mult, R=[kb_, DT[d]], W=[AT[d]])
                        P.op('dve', 'tensor_tensor', qg.r()[:, d, :], q.ap()[:, tok], Eg[d].ap(), ALU.mult, R=[q, Eg[d]], W=[qg[d]])
                        P.op('act', 'activation', kbg.r()[:, d, :], tb_.ap()[:, 0:128], AF.Identity, scale=bg.ap()[:, c, col],
                             R=[tb_, bg], W=[kbg[d]])
                        P.op('act', 'activation', kd.r()[:, d, :], tb_.ap()[:, 0:128], AF.Identity, scale=ed.ap()[:, c, col],
                             R=[tb_, ed], W=[kd[d]])
                        P.op('act', 'activation', vb.r()[:, d, :], tb_.ap()[:, 128:256], AF.Identity, scale=sm['beta'].ap()[:, c, col],
                             R=[tb_, sm['beta']], W=[vb[d]])
                    if STAGE < 12:
                        continue
                    pt_ = TB[0]
                    for d in range(2):
                        P.op('pe', 'transpose', pt_.ap()[:, 256 + 128 * d:256 + 128 * d + 128], X.ap()[:, d, :], ident.ap(), R=[X[d], ident], W=[pt_])
                    if STAGE == 12:
                        continue
                    for d in range(2):
                        pc = pt_.ap()[:, 256 + 128 * d:256 + 128 * d + 128]
                        P.op('act', 'activation', XT.ap()[:, d, :], pc, AF.Identity, R=[pt_], W=[XT[d]])
                        P.op('dve', 'tensor_tensor', TT.ap()[:, d, :], XT.ap()[:, d, :], ident.ap(), ALU.add, R=[XT[d], ident], W=[TT[d]])
                    Xc, Xo = X, Xn
                    if STAGE < 13 or STAGE > 100:
                        continue
                    for kk in range(1, 7):
                        p1 = pbank()
                        for d in range(2):
                            P.op('pe', 'matmul', p1.ap()[:, 128 * d:128 * d + 128], XT.ap()[:, d, :], Xc.ap()[:, d, :], start=True, stop=True,
                                 R=[XT[d], Xc[d]], W=[p1])
                        if kk < 6:
                            p2 = pbank()
                            for d in range(2):
                                P.op('pe', 'matmul', p2.ap()[:, 128 * d:128 * d + 128], Xc.ap()[:, d, :], XT.ap()[:, d, :], start=True, stop=True,
                                     R=[XT[d], Xc[d]], W=[p2])
                        P.op('act', 'activation', Xo.ap(), p1.ap()[:, 0:256].rearrange("p (a b) -> p a b", a=2), AF.Identity, R=[p1], W=[Xo])
                        if kk < 6:
                            P.op('dve', 'tensor_copy', XT.ap(), p2.ap()[:, 0:256].rearrange("p (a b) -> p a b", a=2), R=[p2], W=[XT])
                        Xc, Xo = Xo, Xc
                        p3 = pbank()
                        for d in range(2):
                            P.op('pe', 'matmul', p3.ap()[:, 128 * d:128 * d + 128], Xc.ap()[:, d, :], TT.ap()[:, d, :], start=True, stop=True,
                                 R=[Xc[d], TT[d]], W=[p3])
                        P.op('dve', 'tensor_tensor', TT.ap(), TT.ap(), p3.ap()[:, 0:256].rearrange("p (a b) -> p a b", a=2), ALU.add,
                             R=[TT, p3], W=[TT])
                    if STAGE < 14:
                        continue
                    for d in range(2):
                        c = cs[d]
                        tok = slice(c * 128, (c + 1) * 128)
                        col = slice(4 * d + hd, 4 * d + hd + 1)
                        pw = pbank()
                        P.op('pe', 'matmul', pw.ap()[:, 0:128], kbg.ap()[:, d, :], TT.ap()[:, d, :], start=True, stop=True, R=[kbg[d], TT[d]], W=[pw])
                        P.op('act', 'activation', nwT.r()[:, d, :], pw.ap()[:, 0:128], AF.Identity, scale=-1.0, R=[pw], W=[nwT[d]])
                        if STAGE < 15:
                            continue
                        pv = pbank()
                        P.op('pe', 'matmul', pv.ap()[:, 0:128], TT.ap()[:, d, :], vb.ap()[:, d, :], start=True, stop=False, R=[TT[d], vb[d]], W=[pv])
                        P.op('pe', 'matmul', pv.ap()[:, 0:128], nwT.ap()[:, d, :], S.ap()[:, d, :], start=False, stop=True, R=[nwT[d], S[d]], W=[pv])
                        P.op('act', 'activation', vn.r()[:, d, :], pv.ap()[:, 0:128], AF.Identity, R=[pv], W=[vn[d]])
                        if STAGE < 16:
                            continue
                        po = pbank()
                        P.op('pe', 'matmul', po.ap()[:, 0:128], S.r()[:, d, :], qg.r()[:, d, :], start=True, stop=False, R=[S[d], qg[d]], W=[po])
                        P.op('pe', 'matmul', po.ap()[:, 0:128], vn.r()[:, d, :], AT.r()[:, d, :], start=False, stop=True, R=[vn[d], AT[d]], W=[po])
                        first = (d == 0 and t < 4) or (d == 1 and t <= 3)
                        first = (t < 7 - t) if d == 0 else (t < 7 - t)
                        if first:
                            P.op('act', 'activation', actb.r()[:, hd, tok], po.ap()[:, 0:128], AF.Identity, R=[po], W=[actb[hd]])
                        else:
                            P.op('dve', 'tensor_tensor', actb.r()[:, hd, tok], actb.ap()[:, hd, tok], po.ap()[:, 0:128], ALU.add,
                                 R=[po, actb[hd]], W=[actb[hd]])
                        if STAGE < 17:
                            continue
                        ps_ = pbank()
                        P.op('pe', 'matmul', ps_.ap()[:, 0:128], kd.r()[:, d, :], vn.r()[:, d, :], start=True, stop=True, R=[kd[d], vn[d]], W=[ps_])
                        P.op('dve', 'scalar_tensor_tensor', S.r()[:, d, :], S.ap()[:, d, :], egt.ap()[:, c, col], ps_.ap()[:, 0:128],
                             ALU.mult, ALU.add, R=[S[d], egt, ps_], W=[S[d]])
                        seg_end = (c % 2 == 1) if d == 0 else (c % 2 == 0)
                        if seg_end and STAGE > 17:
                            P.dma('sp', st_dn[l, d, c // 2, hd], S.ap()[:, d, :], R=[S[d]], dkey='sd%d' % d)
                            P.op('dve', 'tensor_scalar', S.r()[:, d, :], S.ap()[:, d, :], cft.ap()[:, 0:1], None, ALU.mult,
                                 R=[S[d], cft], W=[S[d]])
                SR.release(mkh)
            nrot[0] = 8
            if l == 0:
                tap('dno', actb)
            wfb = [SR.get(8, 128) for _ in range(2)]
            for hd in range(4):
                for half in range(2):
                    sl = hs(half)
                    sq = sqb[half]
                    pbk = pbank()
                    P.op('act', 'activation', sq.r(), actb.ap()[:, hd, sl], AF.Square, R=[actb[hd]], W=[sq])
                    P.op('pe', 'matmul', pbk.ap(), onesm.r(), sq.r(), start=True, stop=True, R=[sq, onesm], W=[pbk])
                    rs = rstd[half]
                    P.op('act', 'activation', rs.ap(), pbk.ap(), AF.Sqrt, bias=epst_b.ap(), scale=1.0 / 128, R=[pbk, epst_b], W=[rs])
                    P.op('dve', 'reciprocal', rs.ap(), rs.ap(), R=[rs], W=[rs])
                    P.op('dve', 'scalar_tensor_tensor', actb.r()[:, hd, sl], actb.ap()[:, hd, sl], gv.ap()[:, 0:1], rs.ap(),
                         ALU.mult, ALU.mult, R=[actb[hd], gv, rs], W=[actb[hd]])
                def evg(half, pp, hd=hd):
                    tm = tmpb[half]
                    P.op('act', 'activation', tm.ap(), pp.ap(), AF.Silu, R=[pp], W=[tm])
                    P.op('dve', 'tensor_tensor', actb.r()[:, hd, hs(half)], actb.ap()[:, hd, hs(half)], tm.ap(), ALU.mult,
                         R=[tm, actb[hd]], W=[actb[hd]])
                proj_fm(l, 22 + hd, evg, wfb)
            SB.release(mkF)
            SR.release(mkR)

        def mixer(l):
            norm_mod(l, 1, h)
            mk0 = SR.mark()
            actb = SR.get(4, T)
            s5_branch(l, actb)
            mg = SR.get(NFC, T)
            finalize(l, 0, actb, True, mg)
            if l == 0:
                tap('mg', mg)
            mkS = SB.mark()
            cst = load_consts()
            sm = small_params(l)
            ssd_branch(l, actb, sm, cst)
            if l == 0:
                tap('actb_ssd', actb)
            finalize(l, 1, actb, False, mg)
            if l == 0:
                tap('mg2', mg)
            dn_branch(l, actb, sm, cst)
            finalize(l, 2, actb, False, mg)
            if l == 0:
                tap('mg3', mg)
            SB.release(mkS)
            wfb = [SR.get(8, 128) for _ in range(2)]
            for fc in range(NFC):
                wb = wfb[fc % 2]
                P.dma('sp', wb.r(), wout[l, fc], W=[wb], dkey='wf%d' % (fc % 2))
                for half in range(2):
                    sl = hs(half)
                    po = pbank()
                    for kc in range(8):
                        P.op('pe', 'matmul', po.ap(), wb.r()[:, kc, :], mg.r()[:, kc, sl], start=(kc == 0), stop=(kc == 7),
                             R=[wb, mg[kc]], W=[po])
                    P.op('dve', 'scalar_tensor_tensor', x.ap()[:, fc, sl], po.ap(), Gvec.ap()[:, l, 8 + fc:8 + fc + 1],
                         x.ap()[:, fc, sl], ALU.mult, ALU.add, R=[po, Gvec[l], x[fc]], W=[x[fc]])
            SR.release(mk0)

        SBt = [SB.get(1) for _ in range(2)]
        tmp2k = Tile(arena, 'sb', tmpb[0].off, (2048,))
        assert rstd[1].off == tmpb[0].off + 1536

        for l in range(NLAYERS_RUN):
            ffn(l, 0)
            if l == 0:
                tap('x1', x)
            if MIXER:
                mixer(l)
            ffn(l, 1)

        for half in range(2):
            sl = hs(half)
            rs = rstd[half]
            rms_rstd(x, half, rs)
            for fc in range(NFC):
                P.op('dve', 'scalar_tensor_tensor', x.ap()[:, fc, sl], x.ap()[:, fc, sl], fg.ap()[:, fc:fc + 1],
                     rs.ap(), ALU.mult, ALU.mult, R=[x[fc], rs, fg], W=[x[fc]])
        for fc in range(NFC):
            P.dma('sp', y_out[fc], x[fc].ap(), R=[x[fc]], dkey='yo%d' % (fc % 2))
        if True:
            for a in range(2 if not MIXER else 0):
                for l in range(L):
                    for d in range(2):
                        P.dma('sp', st_s5[a, l, d].rearrange("p a b -> p (a b)"), zer.ap()[:, 0:64], R=[zer], dkey='z0')
            for l in range(L):
                for d in range(2):
                    for s in range(4):
                        if not MIXER:
                            P.dma('sp', st_ssd[l, d, s], zer.ap(), R=[zer], dkey='z1')
                            pass
        P.final_wait('sp')
        P.emit()
        print("SBUF peak floats", SB.peak, SR.peak, "instr counts", P.cnt, flush=True)
    return nc


def _prep_shared(inp):
    f = np.float32
    sh = {}
    aw = inp['ada_w'].reshape(L, 8, 128, 36, 256)
    sh['adaw'] = np.ascontiguousarray(aw.transpose(0, 3, 2, 1, 4)).astype(f)
    sh['adab'] = np.ascontiguousarray(inp['ada_b'].reshape(L, 72, 128).transpose(0, 2, 1)).astype(f)
    sh['normg'] = np.ascontiguousarray(inp['norm_g'].reshape(L, 3, NFC, 128).transpose(3, 0, 1, 2)).astype(f)
    sh['fng'] = np.ascontiguousarray(inp['final_norm_g'].reshape(NFC, 128).T).astype(f)
    w = inp['ffn_wi'].reshape(L, 2, 8, 128, 2, NJ, 128)
    sh['wi'] = np.ascontiguousarray(w.transpose(0, 1, 5, 3, 2, 4, 6)).reshape(L, 2, NJ, 128, 8, 256).astype(f)
    w = inp['ffn_wo'].reshape(L, 2, NJ, 128, NFC, 128)
    sh['wo'] = np.ascontiguousarray(w.transpose(0, 1, 4, 3, 2, 5)).reshape(L, 2, NFC, 128, NJ * 128).astype(f)
    win = inp['w_in']
    cols = np.concatenate([np.arange(0, 512), np.arange(1024, 1792), np.arange(1808, 3344),
                           np.arange(3360, 3872), np.arange(3872, 6944), np.arange(512, 1024)])
    wsel = win[:, :, cols].reshape(L, 8, 128, 54, 128)
    scol = np.concatenate([np.arange(1792, 1808), np.arange(3344, 3360)])
    sh['wsm'] = np.ascontiguousarray(win[:, :, scol].reshape(L, 8, 128, 32).transpose(0, 2, 1, 3)).astype(f)
    bias = np.concatenate([inp['ssd_dt_bias'].reshape(L, 16), np.zeros((L, 8), f), inp['dn_dt_bias'].reshape(L, 8)], axis=1)
    sh['smb'] = np.ascontiguousarray(np.broadcast_to(bias[:, None, :], (L, 128, 32))).astype(f)
    alog = np.concatenate([inp['ssd_a_log'].reshape(L, 16), inp['dn_a_log'].reshape(L, 8)], axis=1)
    sh['sma'] = np.ascontiguousarray(np.broadcast_to(alog[:, None, :], (L, 128, 24))).astype(f)
    cw = np.concatenate([inp['ssd_conv_w'], inp['ssd_conv_b'][:, None, :]], axis=1)
    sh['ssdcw'] = np.ascontiguousarray(cw.reshape(L, 4, 6, 128).transpose(0, 3, 2, 1)).astype(f)
    dch = np.repeat(inp['ssd_d'], 64, axis=1)
    sv = np.stack([inp['ssd_norm_g'], dch], axis=-1)
    sh['ssdv'] = np.ascontiguousarray(sv.reshape(L, 4, 128, 2).transpose(0, 2, 1, 3)).astype(f)
    sh['dncw'] = np.ascontiguousarray(inp['dn_conv_w'].reshape(L, 3, 12, 128).transpose(0, 3, 2, 1)).astype(f)
    sh['dnv'] = np.ascontiguousarray(inp['dn_norm_g'].reshape(L, 128, 1)).astype(f)
    jj, ii = np.meshgrid(np.arange(128), np.arange(128), indexing='ij')
    NEG = -32768.0
    cst = np.stack([(jj <= ii), (jj >= ii), (jj == ii),
                    np.where(ii >= jj, 0.0, NEG), np.where(ii <= jj, 0.0, NEG),
                    np.where(ii > jj, 0.0, NEG), np.where(ii < jj, 0.0, NEG)], axis=1).astype(f)
    sh['consts'] = np.ascontiguousarray(cst)
    sh['wfm'] = np.ascontiguousarray(wsel.transpose(0, 3, 2, 1, 4)).astype(f)
    br = np.stack([inp['s5_glu'][:, 0], inp['s5_glu'][:, 1], inp['ssd_w_out'], inp['dn_w_out']], axis=1)
    br = br.reshape(L, 4, 4, 128, NFC, 128)
    sh['wbr'] = np.ascontiguousarray(br.transpose(0, 1, 4, 3, 2, 5)).astype(f)
    wo_ = inp['w_out'].reshape(L, 8, 128, NFC, 128)
    sh['wout'] = np.ascontiguousarray(wo_.transpose(0, 3, 2, 1, 4)).astype(f)
    ldt = np.broadcast_to(inp['s5_log_dt'][..., None], (L, 2, 32, 64))
    p3 = np.stack([inp['s5_lam_re'], inp['s5_lam_im'], ldt], axis=2).reshape(L, 2, 3, 16, 2, 64)
    sh['s5p'] = np.ascontiguousarray(p3.transpose(0, 1, 4, 5, 2, 3)).reshape(L, 2, 128, 3, 16).astype(f)
    sb_ = np.zeros((L, 2, 2, 128, 4, 128), f)
    scc = np.zeros((L, 2, 2, 128, 16, 128), f)
    for ri, (bb, cc) in enumerate(((inp['s5_b_re'], inp['s5_c_re']), (inp['s5_b_im'], inp['s5_c_im']))):
        for g in range(32):
            sb_[:, :, ri, 16 * (g % 8):16 * (g % 8) + 16, g // 8, 64 * (g % 2):64 * (g % 2) + 64] = \
                bb[:, :, g].transpose(0, 1, 3, 2)
            scc[:, :, ri, 64 * (g % 2):64 * (g % 2) + 64, g // 2, 16 * (g % 8):16 * (g % 8) + 16] = \
                cc[:, :, g].transpose(0, 1, 3, 2)
    sh['s5b'] = sb_.reshape(L, 2, 2, 128, 512)
    sh['s5c'] = scc
    sh['s5d'] = np.ascontiguousarray(inp['s5_d'].reshape(L, 4, 128).transpose(0, 2, 1)).astype(f)
    return sh


def _core_inputs(inp, core):
    f = np.float32
    d = {}
    if core < 4:
        xs = inp['x_prompt'][core * 4:(core + 1) * 4].reshape(T, D)
        cond = inp['c_ctx']
    elif core < 6:
        xs = inp['x_sample'][core - 4]
        cond = inp['c'][core - 4]
    else:
        return None
    W_ = 256 if core < 4 else 64
    tt = np.arange(T)
    m0 = np.ones(T, f); m0[1:] = (tt[1:] % W_ != 0)
    m1 = (tt % W_ != 0).astype(f)
    d['cmask'] = np.ascontiguousarray(np.broadcast_to(np.stack([m0, m1])[None], (128, 2, T))).astype(f)
    h0s = np.zeros((L, 2, 128, 512), f)
    if core >= 4:
        st = inp['state_ssd'][core - 4]
        for hh in range(8):
            gr = hh // 4
            h0s[:, :, 64 * gr:64 * gr + 64, 64 * hh:64 * hh + 64] = st[:, :, hh].transpose(0, 1, 3, 2)
    d['ssdh0'] = h0s
    if core >= 4:
        d['dnh0'] = np.ascontiguousarray(inp['state_dn'][core - 4]).astype(f)
    else:
        d['dnh0'] = np.zeros((L, 2, 4, 128, 128), f)
    if core < 4:
        d['s5h0'] = np.zeros((L, 2, 128, 2, 16), f)
        d['cf'] = np.zeros((128, 1), f)
    else:
        b = core - 4
        hh = np.stack([inp['state_s5_re'][b], inp['state_s5_im'][b]], axis=2)
        hh = hh.reshape(L, 2, 2, 16, 2, 64).transpose(0, 1, 4, 5, 2, 3)
        d['s5h0'] = np.ascontiguousarray(hh).reshape(L, 2, 128, 2, 16).astype(f)
        d['cf'] = np.ones((128, 1), f)
    d['x_t'] = np.ascontiguousarray(xs.T.reshape(NFC, 128, T)).astype(f)
    c2 = cond.reshape(NFC, 128).T
    d['cond'] = np.ascontiguousarray(np.stack([c2, c2], axis=-1)).astype(f)
    return d


_NC_CACHE = {}


def kernel(**inputs):
    inp = {k: np.asarray(v) for k, v in inputs.items()}
    if 'nc' not in _NC_CACHE:
        _NC_CACHE['nc'] = build_program()
    nc = _NC_CACHE['nc']
    shared = _prep_shared(inp)
    in_maps = []
    for core in range(N_CORES):
        d = _core_inputs(inp, core)
        if d is None:
            d = in_maps[0]
            in_maps.append(d)
            continue
        d.update(shared)
        in_maps.append(d)
    res = run_bass_kernel_spmd(nc, in_maps, core_ids=list(range(N_CORES)))
    R = res.results
    B, S = 16, 256
    y_prompt = np.zeros((B, S, D), np.float32)
    for c in range(4):
        y_prompt[c * 4:(c + 1) * 4] = R[c]['y_t'].reshape(D, T).T.reshape(4, S, D)
    y_sample = np.stack([R[4 + b]['y_t'].reshape(D, T).T for b in range(2)]).astype(np.float32)
    s5re = np.zeros((B, L, 2, 32, 64), np.float32)
    s5im = np.zeros((B, L, 2, 32, 64), np.float32)
    ssd = np.zeros((B, L, 2, 8, 64, 64), np.float32)
    dn = np.zeros((B, L, 2, 4, 128, 128), np.float32)
    for c in range(4):
        s5 = R[c]['st_s5'].reshape(2, L, 2, 2, 64, 16, 4)
        s5 = s5.transpose(0, 6, 1, 2, 5, 3, 4).reshape(2, 4, L, 2, 32, 64)
        s5re[c * 4:(c + 1) * 4] = s5[0]
        s5im[c * 4:(c + 1) * 4] = s5[1]
        sd = R[c]['st_ssd'].reshape(L, 2, 4, 2, 64, 8, 64)
        for hh in range(8):
            ssd[c * 4:(c + 1) * 4, :, :, hh] = sd[:, :, :, hh // 4, :, hh, :].transpose(2, 0, 1, 4, 3)
        dd = R[c]['st_dn']
        dn[c * 4:(c + 1) * 4] = dd.transpose(2, 0, 1, 3, 4, 5)
    return (y_prompt, y_sample, s5re, s5im, ssd, dn)
```
